# Optimizing a Trainium2 kernel written in Bass

```python
import math
import jax, jax.numpy as jnp
from jax import lax
import numpy as np

D_MODEL = 1024
BATCH = 4
SEQ = 4096
DEPTH = 1

PLE_DIM = 256
ROPE_THETA = 10000.0
Q_BLOCK = 128
LN_EPS = 1e-5
RMS_EPS = 1e-6
NEG_INF = -1e30
MAX_POS_OFFSET = 1024

DIFF_HEADS = 8
DIFF_HEAD_DIM = 64
DIFF_QK_WIDTH = 2 * DIFF_HEADS * DIFF_HEAD_DIM
DIFF_WIDTH = DIFF_HEADS * 2 * DIFF_HEAD_DIM

MLA_HEADS = 8
MLA_Q_LORA = 384
MLA_KV_LORA = 256
MLA_NOPE = 128
MLA_ROPE = 64
MLA_V = 128
MLA_WIDTH = MLA_HEADS * MLA_V

IN_SPLIT_SIZES = (DIFF_QK_WIDTH, DIFF_QK_WIDTH, DIFF_WIDTH, DIFF_WIDTH,
                  MLA_Q_LORA, MLA_KV_LORA, MLA_ROPE, MLA_WIDTH, 2 * D_MODEL)
N_IN = sum(IN_SPLIT_SIZES)

DEEPNORM_ALPHA = (2 * DEPTH) ** 0.25
DEEPNORM_BETA = (8 * DEPTH) ** -0.25

kernel_name = "diffattn_mla_gated_hybrid_deepnorm"


def layer_norm(x, g, b):
    xf = x.astype(jnp.float32)
    mu = jnp.mean(xf, -1, keepdims=True)
    var = jnp.mean(jnp.square(xf - mu), -1, keepdims=True)
    return ((xf - mu) * lax.rsqrt(var + LN_EPS) * g + b).astype(x.dtype)


def rms_norm(x, g):
    xf = x.astype(jnp.float32)
    y = xf * lax.rsqrt(jnp.mean(xf * xf, -1, keepdims=True) + RMS_EPS)
    return (y * g).astype(x.dtype)


def rope(x, positions):
    dim = x.shape[-1]
    inv = ROPE_THETA ** (-jnp.arange(0, dim, 2, dtype=jnp.float32) / dim)
    ang = positions.astype(jnp.float32)[:, None, :, None] * inv
    cos, sin = jnp.cos(ang), jnp.sin(ang)
    x1, x2 = jnp.split(x.astype(jnp.float32), 2, axis=-1)
    return jnp.concatenate([x1 * cos - x2 * sin, x2 * cos + x1 * sin], -1).astype(x.dtype)


def _query_blocks(t, seq_axis):
    nb = t.shape[seq_axis] // Q_BLOCK
    t = t.reshape(t.shape[:seq_axis] + (nb, Q_BLOCK) + t.shape[seq_axis + 1:])
    return jnp.moveaxis(t, seq_axis, 0)


def _causal_mask(block_idx, seq):
    q_pos = block_idx * Q_BLOCK + jnp.arange(Q_BLOCK)
    return jnp.arange(seq)[None, :] <= q_pos[:, None]


def diff_attention(q, k, v, lam):
    S = q.shape[3]
    scale = DIFF_HEAD_DIM ** -0.5
    qb = _query_blocks(q, 3)

    def one_block(args):
        q_blk, i = args
        s = jnp.einsum('bhcqd,bhckd->bhcqk', q_blk, k).astype(jnp.float32) * scale
        s = jnp.where(_causal_mask(i, S), s, NEG_INF)
        pr = jax.nn.softmax(s, axis=-1)
        w = (pr[:, :, 0] - lam * pr[:, :, 1]).astype(v.dtype)
        return jnp.einsum('bhqk,bhkd->bhqd', w, v)

    o = lax.map(one_block, (qb, jnp.arange(qb.shape[0])))
    _, B, H, _, dv = o.shape
    return jnp.moveaxis(o, 0, 2).reshape(B, H, S, dv)


def mla_attention(q_nope, q_pe, k_nope, k_pe, v):
    S = q_nope.shape[2]
    scale = (MLA_NOPE + MLA_ROPE) ** -0.5
    qn_b = _query_blocks(q_nope, 2)
    qp_b = _query_blocks(q_pe, 2)

    def one_block(args):
        qn, qp, i = args
        s = (jnp.einsum('bhqd,bhkd->bhqk', qn, k_nope)
             + jnp.einsum('bhqr,bkr->bhqk', qp, k_pe)).astype(jnp.float32) * scale
        s = jnp.where(_causal_mask(i, S), s, NEG_INF)
        pr = jax.nn.softmax(s, axis=-1).astype(v.dtype)
        return jnp.einsum('bhqk,bhkd->bhqd', pr, v)

    o = lax.map(one_block, (qn_b, qp_b, jnp.arange(qn_b.shape[0])))
    _, B, H, _, dv = o.shape
    return jnp.moveaxis(o, 0, 2).reshape(B, H, S, dv)


def setup_inputs(seed: int = 0) -> dict:
    key = jax.random.key(seed)
    ks = jax.random.split(key, 24)
    f32 = jnp.float32
    D = D_MODEL

    def nrm(k, shape, scale):
        return jax.random.normal(k, shape, f32) * scale

    x = nrm(ks[0], (BATCH, SEQ, D), 1.0)
    p = nrm(ks[1], (DEPTH, BATCH, SEQ, PLE_DIM), 1.0)
    positions = (jax.random.randint(ks[2], (BATCH, 1), 0, MAX_POS_OFFSET)
                 + jnp.arange(SEQ)[None, :]).astype(jnp.int32)
    return {
        "x": x,
        "p": p,
        "positions": positions,
        "ln_emb_g": 1.0 + nrm(ks[3], (D,), 0.02),
        "ln_emb_b": nrm(ks[4], (D,), 0.02),
        "w_in": nrm(ks[5], (DEPTH, D, N_IN), D ** -0.5),
        "b_gate": nrm(ks[6], (DEPTH, 2 * D), 0.02),
        "diff_lambda": nrm(ks[7], (DEPTH, 4, DIFF_HEAD_DIM), 0.1),
        "diff_subln_g": 1.0 + nrm(ks[8], (DEPTH, 2 * DIFF_HEAD_DIM), 0.02),
        "w_o_a": nrm(ks[9], (DEPTH, DIFF_WIDTH, D), DIFF_WIDTH ** -0.5 * DEEPNORM_BETA),
        "mla_q_norm_g": 1.0 + nrm(ks[10], (DEPTH, MLA_Q_LORA), 0.02),
        "mla_w_uq": nrm(ks[11], (DEPTH, MLA_Q_LORA, MLA_HEADS * (MLA_NOPE + MLA_ROPE)), MLA_Q_LORA ** -0.5),
        "mla_kv_norm_g": 1.0 + nrm(ks[12], (DEPTH, MLA_KV_LORA), 0.02),
        "mla_w_ukv": nrm(ks[13], (DEPTH, MLA_KV_LORA, MLA_HEADS * (MLA_NOPE + MLA_V)), MLA_KV_LORA ** -0.5),
        "w_o_b": nrm(ks[14], (DEPTH, MLA_WIDTH, D), MLA_WIDTH ** -0.5 * DEEPNORM_BETA),
        "w_out": nrm(ks[15], (DEPTH, D, D), D ** -0.5 * DEEPNORM_BETA),
        "ple_w_gate": nrm(ks[16], (DEPTH, D, D), D ** -0.5),
        "ple_b_gate": nrm(ks[17], (DEPTH, D), 0.02),
        "ple_w_proj": nrm(ks[18], (DEPTH, PLE_DIM, D), PLE_DIM ** -0.5 * DEEPNORM_BETA),
        "ln_post_g": 1.0 + nrm(ks[19], (DEPTH, D), 0.02),
        "ln_post_b": nrm(ks[20], (DEPTH, D), 0.02),
    }


def reference(x, p, positions, ln_emb_g, ln_emb_b, w_in, b_gate, diff_lambda, diff_subln_g,
              w_o_a, mla_q_norm_g, mla_w_uq, mla_kv_norm_g, mla_w_ukv, w_o_b, w_out,
              ple_w_gate, ple_b_gate, ple_w_proj, ln_post_g, ln_post_b):
    B, S, D = x.shape
    split_idx = [int(v) for v in np.cumsum(IN_SPLIT_SIZES)[:-1]]
    x = layer_norm(x, ln_emb_g, ln_emb_b)

    for i in range(DEPTH):
        h = x @ w_in[i]
        q_a, k_a, v_a, z_a, c_q, c_kv, k_pe, z_b, g_logit = jnp.split(h, split_idx, axis=-1)

        def heads2(t):
            t = t.reshape(B, S, 2 * DIFF_HEADS, DIFF_HEAD_DIM).transpose(0, 2, 1, 3)
            return rope(t, positions).reshape(B, DIFF_HEADS, 2, S, DIFF_HEAD_DIM)
        qa, ka = heads2(q_a), heads2(k_a)
        va = v_a.reshape(B, S, DIFF_HEADS, 2 * DIFF_HEAD_DIM).transpose(0, 2, 1, 3)
        lam_init = 0.8 - 0.6 * math.exp(-0.3 * i)
        lq = diff_lambda[i].astype(jnp.float32)
        lam = jnp.exp(jnp.sum(lq[0] * lq[1])) - jnp.exp(jnp.sum(lq[2] * lq[3])) + lam_init
        o_a = diff_attention(qa, ka, va, lam)
        o_a = rms_norm(o_a, diff_subln_g[i]) * (1.0 - lam_init)
        o_a = o_a.transpose(0, 2, 1, 3).reshape(B, S, DIFF_WIDTH)
        y_a = (o_a * jax.nn.silu(z_a)) @ w_o_a[i]

        qb = (rms_norm(c_q, mla_q_norm_g[i]) @ mla_w_uq[i])
        qb = qb.reshape(B, S, MLA_HEADS, MLA_NOPE + MLA_ROPE).transpose(0, 2, 1, 3)
        q_nope, q_pe = qb[..., :MLA_NOPE], rope(qb[..., MLA_NOPE:], positions)
        kv = (rms_norm(c_kv, mla_kv_norm_g[i]) @ mla_w_ukv[i])
        kv = kv.reshape(B, S, MLA_HEADS, MLA_NOPE + MLA_V).transpose(0, 2, 1, 3)
        k_nope, v_b = kv[..., :MLA_NOPE], kv[..., MLA_NOPE:]
        k_rot = rope(k_pe[:, None], positions)[:, 0]
        o_b = mla_attention(q_nope, q_pe, k_nope, k_rot, v_b)
        o_b = o_b.transpose(0, 2, 1, 3).reshape(B, S, MLA_WIDTH)
        y_b = (o_b * jax.nn.silu(z_b)) @ w_o_b[i]

        g_a, g_b = jnp.split(jax.nn.sigmoid(g_logit + b_gate[i]), 2, axis=-1)
        mix = (g_a * y_a + g_b * y_b) @ w_out[i]

        y = DEEPNORM_ALPHA * x + mix
        y = y + jax.nn.sigmoid(y @ ple_w_gate[i] + ple_b_gate[i]) * (p[i] @ ple_w_proj[i])
        x = layer_norm(y, ln_post_g[i], ln_post_b[i])

    return x
```

```python
import numpy as np
import contextlib
import concourse.bass as bass
import concourse.mybir as mybir
from concourse.bass_utils import run_bass_kernel_spmd

F32 = mybir.dt.float32
BF16 = mybir.dt.bfloat16
F16 = mybir.dt.float16
I32 = mybir.dt.int32
AF = mybir.ActivationFunctionType
ALU = mybir.AluOpType
AX = mybir.AxisListType

ENGS = ("pe", "act", "dve", "pool", "sp")


class _Op:
    __slots__ = ("idx", "eng", "fn", "deps", "is_dma", "dsem", "dval", "flag", "rank", "predma")

    def __init__(self, idx, eng, fn, is_dma):
        self.idx = idx
        self.eng = eng
        self.fn = fn
        self.deps = set()
        self.is_dma = is_dma
        self.dsem = None
        self.dval = 0
        self.flag = False
        self.rank = 0
        self.predma = None


class Sched:
    def __init__(self, nc, n_dma_sems=40):
        self.nc = nc
        self.ops = []
        self.res = {}
        self.n_dma_sems = n_dma_sems
        self.n_dma = 0
        self.dma_hist = []
        self.bar = None
        self.bar_seen = set()
        self.last_on = {}
        self.dma_since_bar = []

    def _add(self, eng, fn, r, w, is_dma, x=()):
        op = _Op(len(self.ops), eng, fn, is_dma)
        self.ops.append(op)
        ek = ("dma", op.idx) if is_dma else eng
        deps = {}
        for key in x:
            st = self.res.get(key)
            if st is not None:
                for e, i in st[0].items():
                    deps.setdefault(i, False)
                for e, i in st[1].items():
                    deps.setdefault(i, False)
        for key in r:
            st = self.res.get(key)
            if st is not None:
                for e, i in st[0].items():
                    deps[i] = True
        for key in w:
            st = self.res.get(key)
            if st is not None:
                for e, i in st[0].items():
                    deps.setdefault(i, False)
                for e, i in st[1].items():
                    deps.setdefault(i, False)
        for i, raw in deps.items():
            d = self.ops[i]
            if (not d.is_dma) and (not is_dma) and d.eng == eng and not raw:
                continue
            if (not d.is_dma) and (not is_dma) and d.eng == eng and eng == "pe":
                continue
            op.deps.add(i)
        if self.bar is not None and eng not in self.bar_seen:
            self.bar_seen.add(eng)
            for i in self.bar:
                if i != op.idx:
                    d = self.ops[i]
                    if d.is_dma or d.eng != eng:
                        op.deps.add(i)
        for key in r:
            st = self.res.setdefault(key, ({}, {}))
            st[1][ek] = op.idx
        for key in w:
            self.res[key] = ({ek: op.idx}, {})
        for key in x:
            self.res[key] = ({ek: op.idx}, {})
        if is_dma:
            k = self.n_dma % self.n_dma_sems
            op.dsem = k
            op.dval = 16 * (self.n_dma // self.n_dma_sems + 1)
            if self.n_dma >= self.n_dma_sems:
                op.predma = self.dma_hist[self.n_dma - self.n_dma_sems]
            self.dma_hist.append(op.idx)
            self.n_dma += 1
            self.dma_since_bar.append(op.idx)
        else:
            self.last_on[eng] = op.idx
        return op

    def op(self, eng, fn, r=(), w=(), x=()):
        return self._add(eng, fn, r, w, False, x)

    def dma(self, fn, r=(), w=(), q="sp"):
        return self._add(q, fn, r, w, True)

    def barrier(self):
        self.bar = list(self.last_on.values()) + list(self.dma_since_bar)
        self.bar_seen = set()
        self.dma_since_bar = []

    def emit(self, final_waits=True):
        nc = self.nc
        ops = self.ops
        for op in ops:
            for i in op.deps:
                ops[i].flag = True
        cnt = {e: 0 for e in ENGS}
        for op in ops:
            if not op.is_dma and op.flag:
                cnt[op.eng] += 1
                op.rank = cnt[op.eng]
        per = {e: [] for e in ENGS}
        for op in ops:
            per[op.eng].append(op)
        import contextlib

        with contextlib.ExitStack() as st:
            esem = {e: st.enter_context(nc.semaphore("s_" + e)) for e in ENGS if e != "sp"}
            dsem = [st.enter_context(nc.semaphore("d_%d" % i)) for i in range(self.n_dma_sems)]
            block = st.enter_context(nc.Block())
            last_dma_vals = {}
            for op in ops:
                if op.is_dma:
                    last_dma_vals[op.dsem] = op.dval

            def run(eng_name, eng):
                waited = {}

                def wait(sem, key, val):
                    if waited.get(key, 0) >= val:
                        return
                    waited[key] = val
                    eng.wait_ge(sem, val)

                for op in per[eng_name]:
                    if op.predma is not None:
                        p = ops[op.predma]
                        wait(dsem[p.dsem], ("d", p.dsem), p.dval)
                    for i in sorted(op.deps):
                        d = ops[i]
                        if d.is_dma:
                            wait(dsem[d.dsem], ("d", d.dsem), d.dval)
                        else:
                            wait(esem[d.eng], ("e", d.eng), d.rank)
                    ins = op.fn(eng)
                    if op.is_dma:
                        ins.then_inc(dsem[op.dsem], 16)
                    elif op.flag:
                        ins.then_inc(esem[op.eng], 1)
                if eng_name == "sp" and final_waits:
                    for k, v in last_dma_vals.items():
                        wait(dsem[k], ("d", k), v)

            @block.tensor
            def _(e):
                run("pe", e)

            @block.scalar
            def _(e):
                run("act", e)

            @block.vector
            def _(e):
                run("dve", e)

            @block.gpsimd
            def _(e):
                run("pool", e)

            @block.sync
            def _(e):
                run("sp", e)


PI = float(np.pi)
C1 = 6.28125
C2 = float(2.0 * np.pi - 6.28125)
NT = 4096
NO = 2048
LAM_INIT = 0.2
ALPHA = float(2.0 ** 0.25)
LN_EPS_ = 1e-5


def build_program(stop_after=3, debug=False):
    nc = bass.Bass("TRN2", target_bir_lowering=False)
    S = Sched(nc, n_dma_sems=48)

    def din(name, shape, dt=F32):
        return nc.dram_tensor(name, list(shape), dt, kind="ExternalInput")

    x_in = din("x", [NT, 1024])
    pos_in = din("pos", [1, NT], I32)
    pT_in = din("pT", [128, 2, NO])
    wf_in = din("wf", [70, 128, 8, 128])
    wv_in = din("wv", [2, 128, 8, 512])
    wuqn_in = din("wuqn", [8, 128, 3, 128])
    wuqp_in = din("wuqp", [8, 128, 3, 128])
    wukk_in = din("wukk", [8, 128, 2, 128])
    wukv_in = din("wukv", [2, 128, 2, 512])
    wfin_in = din("wfin", [4, 128, 8, 1024])
    wpp_in = din("wpp", [128, 2, 1024])
    sm_in = din("sm", [128, 64])
    rows_in = din("rows", [5, 1024])
    dl_in = din("dl", [1, 256])
    ident_in = din("ident", [128, 128])
    mask_in = din("masks", [128, 256])
    out_t = nc.dram_tensor("out", [NO, 1024], F32, kind="ExternalOutput")

    def scr(name, shape):
        if debug:
            return nc.dram_tensor(name, list(shape), BF16, kind="ExternalOutput")
        return nc.dram_tensor(name, list(shape), BF16)

    Qa = scr("s_qa", [8, 128, NO]); Ka = scr("s_ka", [8, 128, NT]); Va = scr("s_va", [NT, 1024])
    Za = scr("s_za", [8, 128, NO]); Zb = scr("s_zb", [8, 128, NO]); Sg = scr("s_sg", [16, 128, NO])
    Qn = scr("s_qn", [8, 128, NO]); Qp = scr("s_qp", [8, 64, NO]); Kn = scr("s_kn", [8, 128, NT])
    Kp = scr("s_kp", [64, NT]); Vb = scr("s_vb", [NT, 1024])
    Ga = scr("s_ga", [8, 128, NO]); Gb = scr("s_gb", [8, 128, NO])

    ARENA = 206000
    arena = nc.alloc_sbuf_tensor("arena", [128, ARENA // 4], F32)
    views = {F32: arena, BF16: arena.bitcast(BF16), F16: arena.bitcast(F16), I32: arena.bitcast(I32)}
    esz = {F32: 4, BF16: 2, F16: 2, I32: 4}
    cur = [0]

    def alloc(shape, dt):
        n = int(np.prod(shape))
        nb = (n * esz[dt] + 63) // 64 * 64
        off = cur[0]
        cur[0] += nb
        assert cur[0] <= ARENA, ("SBUF arena overflow", cur[0])
        v = views[dt][:, off // esz[dt]: off // esz[dt] + n]
        if len(shape) == 2:
            v = v.rearrange("p (a b) -> p a b", a=shape[0])
        elif len(shape) == 3:
            v = v.rearrange("p (a b c) -> p a b c", a=shape[0], b=shape[1])
        return v

    pst = nc.alloc_psum_tensor("pst", [128, 8, 512], F32)
    pst16 = pst.bitcast(BF16)

    def B(i):
        return pst[:, i, :]

    def BK(i):
        return ("B", i)

    uid = [0]

    def U(p="u"):
        uid[0] += 1
        return "%s%d" % (p, uid[0])

    sm = alloc([64], F32)
    identf = alloc([128], F32)
    identb = alloc([128], BF16)
    onesb = alloc([128], BF16)
    masks = alloc([256], BF16)
    maskf = alloc([256], F32)
    stats = alloc([32, 4], F32)
    neglam = alloc([4], F32)
    gsub = alloc([1], F32)
    S.dma(lambda e: e.dma_start(out=sm, in_=sm_in[:, :]), w=["sm"])
    S.dma(lambda e: e.dma_start(out=identf, in_=ident_in[:, :]), w=["identf"])
    S.dma(lambda e: e.dma_start(out=maskf, in_=mask_in[:, :]), w=["maskf"])
    S.op("dve", lambda e: e.tensor_copy(out=identb, in_=identf), r=["identf"], w=["identb"])
    S.op("dve", lambda e: e.tensor_copy(out=masks, in_=maskf), r=["maskf"], w=["masks"])
    S.op("dve", lambda e: e.memset(onesb, 1.0), w=["onesb"])
    base0 = cur[0]

    xnT = alloc([8, NT], BF16)
    ctab = alloc([NT], F16)
    stab = alloc([NT], F16)
    base1 = cur[0]

    dlb = alloc([256], F32)
    dlp = alloc([128], F32)
    dls = alloc([4], F32)
    S.dma(lambda e: e.dma_start(out=dlb, in_=dl_in.ap().broadcast_to([128, 256])), w=["dlb"])
    S.op("dve", lambda e: e.tensor_tensor(out=dlp[:, 0:64], in0=dlb[:, 0:64], in1=dlb[:, 64:128], op=ALU.mult), r=["dlb"], w=["dlp0"])
    S.op("dve", lambda e: e.tensor_tensor(out=dlp[:, 64:128], in0=dlb[:, 128:192], in1=dlb[:, 192:256], op=ALU.mult), r=["dlb"], w=["dlp1"])
    S.op("dve", lambda e: e.reduce_sum(out=dls[:, 0:1], in_=dlp[:, 0:64], axis=AX.X), r=["dlp0"], w=["dls0"])
    S.op("dve", lambda e: e.reduce_sum(out=dls[:, 1:2], in_=dlp[:, 64:128], axis=AX.X), r=["dlp1"], w=["dls1"])
    S.op("act", lambda e: e.activation(out=dls[:, 2:4], in_=dls[:, 0:2], func=AF.Exp), r=["dls0", "dls1"], w=["dle"])
    S.op("dve", lambda e: e.tensor_tensor(out=neglam[:, 0:1], in0=dls[:, 3:4], in1=dls[:, 2:3], op=ALU.subtract), r=["dle"], w=["nl0"])
    S.op("dve", lambda e: e.tensor_scalar(out=neglam[:, 1:2], in0=neglam[:, 0:1], scalar1=-LAM_INIT, scalar2=None, op0=ALU.add), r=["nl0"], w=["neglam"])
    S.op("dve", lambda e: e.tensor_scalar(out=gsub, in0=sm[:, 37:38], scalar1=1.0 - LAM_INIT, scalar2=None, op0=ALU.mult), r=["sm"], w=["gsub"])

    posi = alloc([NT], I32)
    ang = alloc([NT], F32)
    kf = alloc([NT], F32)
    ki = alloc([NT], I32)
    rr = alloc([NT], F32)
    S.dma(lambda e: e.dma_start(out=posi, in_=pos_in.ap().broadcast_to([128, NT])), w=["posi"])
    S.op("dve", lambda e: e.tensor_copy(out=ang, in_=posi), r=["posi"], w=["angf"])
    S.op("dve", lambda e: e.tensor_scalar(out=ang, in0=ang, scalar1=sm[:, 38:39], scalar2=None, op0=ALU.mult), r=["angf", "sm"], w=["ang"])
    S.op("dve", lambda e: e.tensor_scalar(out=kf, in0=ang, scalar1=float(1.0 / (2 * np.pi)), scalar2=None, op0=ALU.mult), r=["ang"], w=["kf0"])
    S.op("dve", lambda e: e.tensor_copy(out=ki, in_=kf), r=["kf0"], w=["ki"])
    S.op("dve", lambda e: e.tensor_copy(out=kf, in_=ki), r=["ki"], w=["kf"])
    S.op("dve", lambda e: e.scalar_tensor_tensor(out=rr, in0=kf, scalar=-C1, in1=ang, op0=ALU.mult, op1=ALU.add), r=["kf", "ang"], w=["rr1"])
    S.op("dve", lambda e: e.scalar_tensor_tensor(out=rr, in0=kf, scalar=-C2, in1=rr, op0=ALU.mult, op1=ALU.add), r=["kf", "rr1"], w=["rr2"])
    S.op("dve", lambda e: e.tensor_scalar(out=kf, in0=rr, scalar1=PI, scalar2=None, op0=ALU.is_gt), r=["rr2"], w=["m1"])
    S.op("dve", lambda e: e.scalar_tensor_tensor(out=rr, in0=kf, scalar=-2 * PI, in1=rr, op0=ALU.mult, op1=ALU.add), r=["m1", "rr2"], w=["rr3"])
    S.op("dve", lambda e: e.tensor_scalar(out=kf, in0=rr, scalar1=-PI, scalar2=None, op0=ALU.is_lt), r=["rr3"], w=["m2"])
    S.op("dve", lambda e: e.scalar_tensor_tensor(out=rr, in0=kf, scalar=2 * PI, in1=rr, op0=ALU.mult, op1=ALU.add), r=["m2", "rr3"], w=["rs"])
    S.op("dve", lambda e: e.tensor_scalar(out=rr, in0=rr, scalar1=PI, scalar2=-PI, op0=ALU.min, op1=ALU.max), r=["rs"], w=["rs2"])
    S.op("act", lambda e: e.activation(out=stab, in_=rr, func=AF.Sin, scale=sm[:, 39:40]), r=["rs2", "sm"], w=["stab"])
    S.op("dve", lambda e: e.tensor_scalar(out=ang, in0=rr, scalar1=PI / 2, scalar2=None, op0=ALU.add), r=["rs2", "ang"], w=["rc"])
    S.op("dve", lambda e: e.tensor_scalar(out=kf, in0=ang, scalar1=PI, scalar2=None, op0=ALU.is_gt), r=["rc"], w=["m3"])
    S.op("dve", lambda e: e.scalar_tensor_tensor(out=ang, in0=kf, scalar=-2 * PI, in1=ang, op0=ALU.mult, op1=ALU.add), r=["m3", "rc"], w=["rc2"])
    S.op("dve", lambda e: e.tensor_scalar(out=ang, in0=ang, scalar1=PI, scalar2=-PI, op0=ALU.min, op1=ALU.max), r=["rc2"], w=["rc3"])
    S.op("act", lambda e: e.activation(out=ctab, in_=ang, func=AF.Sin), r=["rc3"], w=["ctab"])
    S.barrier()
    cur[0] = base1

    xb = [alloc([1024], F32) for _ in range(3)]
    xh = [alloc([1024], F32) for _ in range(2)]
    bst = [alloc([12], F32) for _ in range(2)]
    nmrb = alloc([32], F32)

    def p0_L(t):
        xt = xb[t % 3]
        xk = "xb%d" % (t % 3)
        S.dma(lambda e, xt=xt, t=t: e.dma_start(out=xt, in_=x_in[t * 128:(t + 1) * 128, :]), w=[xk])

    def p0_A(t):
        xt = xb[t % 3]
        xk = "xb%d" % (t % 3)
        st_ = bst[t % 2]
        sk = "bst%d" % (t % 2)
        S.op("dve", lambda e, st_=st_, xt=xt: e.bn_stats(out=st_[:, 0:6], in_=xt[:, 0:512]), r=[xk], w=[sk + "a"])
        S.op("dve", lambda e, st_=st_, xt=xt: e.bn_stats(out=st_[:, 6:12], in_=xt[:, 512:1024]), r=[xk], w=[sk + "b"])
        S.op("dve", lambda e, st_=st_, t=t: e.bn_aggr(out=stats[:, t, 0:2], in_=st_[:, 0:12]), r=[sk + "a", sk + "b"], w=["mv%d" % t])
        S.op("act", lambda e, t=t: e.activation(out=stats[:, t, 2:3], in_=stats[:, t, 1:2], func=AF.Ln, bias=sm[:, 40:41], scale=1.0), r=["mv%d" % t, "sm"], w=["lv%d" % t])
        S.op("act", lambda e, t=t: e.activation(out=stats[:, t, 3:4], in_=stats[:, t, 2:3], func=AF.Exp, scale=-0.5), r=["lv%d" % t], w=["rs%d" % t])
        xhh = xh[t % 2]
        hk = "xh%d" % (t % 2)
        S.op("dve", lambda e, t=t: e.scalar_tensor_tensor(out=nmrb[:, t:t + 1], in0=stats[:, t, 0:1], scalar=-1.0, in1=stats[:, t, 3:4], op0=ALU.mult, op1=ALU.mult),
             r=["mv%d" % t, "rs%d" % t], w=["nm%d" % t])
        S.op("act", lambda e, xhh=xhh, xt=xt, t=t: e.activation(out=xhh, in_=xt, func=AF.Identity, scale=stats[:, t, 3:4], bias=nmrb[:, t:t + 1]),
             r=[xk, "rs%d" % t, "nm%d" % t], w=[hk])
        return None

    def p0_B(t):
        xhh = xh[t % 2]
        hk = "xh%d" % (t % 2)
        b0 = 2 * (t % 4)
        for c in range(8):
            bk = b0 + c // 4
            S.op("pe", lambda e, bk=bk, c=c, xhh=xhh: e.transpose(out=B(bk)[:, (c % 4) * 128:(c % 4 + 1) * 128], in_=xhh[:, c * 128:(c + 1) * 128], identity=identf),
                 r=[hk, "identf"], x=[BK(bk)])
        for c in range(8):
            bk = b0 + c // 4
            if c % 2 == 0:
                S.op("act", lambda e, bk=bk, c=c, t=t: e.activation(out=xnT[:, c, t * 128:(t + 1) * 128], in_=B(bk)[:, (c % 4) * 128:(c % 4 + 1) * 128],
                                                                  func=AF.Identity, scale=sm[:, c:c + 1], bias=sm[:, 8 + c:9 + c]),
                     r=["sm"], w=[("xnTw", t, c)], x=[BK(bk)])
            else:
                S.op("dve", lambda e, bk=bk, c=c, t=t: e.tensor_scalar(out=xnT[:, c, t * 128:(t + 1) * 128], in0=B(bk)[:, (c % 4) * 128:(c % 4 + 1) * 128],
                                                                     scalar1=sm[:, c:c + 1], scalar2=sm[:, 8 + c:9 + c], op0=ALU.mult, op1=ALU.add),
                     r=["sm"], w=[("xnTw", t, c)], x=[BK(bk)])

    p0_L(0)
    p0_L(1)
    p0_A(0)
    for t in range(32):
        if t + 2 < 32:
            p0_L(t + 2)
        if t + 1 < 32:
            p0_A(t + 1)
        p0_B(t)
    cur[0] = base1
    if stop_after == 0:
        S.barrier()
        dbg = alloc([1024], F32)
        S.op("dve", lambda e: e.tensor_copy(out=dbg, in_=xnT[:, 0, 0:1024]), r=[("xnT", i) for i in range(8)], w=["dbg"])
        S.dma(lambda e: e.dma_start(out=out_t[0:128, :], in_=dbg), r=["dbg"])
        S.emit()
        return nc

    S.barrier()
    NW = 6
    wsl = [alloc([8, 128], BF16) for _ in range(NW)]
    wcnt = [0]

    def load_w(j):
        k = wcnt[0] % NW
        wcnt[0] += 1
        key = "wsl%d" % k
        S.dma(lambda e, k=k, j=j: e.dma_start(out=wsl[k], in_=wf_in[j], max_dma_last_dim=4096), w=[key], q="pool")
        return wsl[k], key

    t1b = [alloc([512], F32) for _ in range(2)]
    t2b = [alloc([512], F32) for _ in range(2)]
    ob = [alloc([512], BF16) for _ in range(4)]
    ocnt = [0]
    bcnt = [0]

    def nb():
        b = bcnt[0] % 8
        bcnt[0] += 1
        return b

    def proj_fm(wt, wkey, tb, bank, M=128, woff=0):
        for c in range(8):
            S.op("pe", lambda e, c=c: e.matmul(B(bank)[0:M, :], lhsT=wt[:, c, woff:woff + M], rhs=xnT[:, c, tb * 512:(tb + 1) * 512],
                                               start=(c == 0), stop=(c == 7)),
                 r=[wkey, ("xnT", tb)], x=[BK(bank)])

    def rope_out(bankA, bankB, tb, M, dst):
        i = ocnt[0]
        ocnt[0] += 1
        t1 = t1b[i % 2]; t2 = t2b[i % 2]; o = ob[i % 4]
        k1 = "t1_%d" % (i % 2); k2 = "t2_%d" % (i % 2); ko = "ob%d" % (i % 4)
        S.op("dve", lambda e: e.tensor_tensor(out=t1[0:M, :], in0=B(bankA)[0:M, :], in1=ctab[0:M, tb * 512:(tb + 1) * 512], op=ALU.mult),
             r=["ctab"], w=[k1], x=[BK(bankA)])
        S.op("dve", lambda e: e.tensor_tensor(out=t2[0:M, :], in0=B(bankB)[0:M, :], in1=stab[0:M, tb * 512:(tb + 1) * 512], op=ALU.mult),
             r=["stab"], w=[k2], x=[BK(bankB)])
        S.op("pool", lambda e: e.tensor_tensor(out=o[0:M, :], in0=t1[0:M, :], in1=t2[0:M, :], op=ALU.add), r=[k1, k2], w=[ko])
        S.dma(lambda e: e.dma_start(out=dst, in_=o[0:M, :]), r=[ko])

    def simple_out(bank, dst, kind, M=128, bias=None):
        i = ocnt[0]
        ocnt[0] += 1
        o = ob[i % 4]; ko = "ob%d" % (i % 4)
        if kind == "silu":
            S.op("act", lambda e: e.activation(out=o[0:M, :], in_=B(bank)[0:M, :], func=AF.Silu), w=[ko], x=[BK(bank)])
        elif kind == "sig":
            S.op("act", lambda e: e.activation(out=o[0:M, :], in_=B(bank)[0:M, :], func=AF.Sigmoid, bias=bias), r=["sm"], w=[ko], x=[BK(bank)])
        elif kind == "copy_act":
            S.op("act", lambda e: e.activation(out=o[0:M, :], in_=B(bank)[0:M, :], func=AF.Copy), w=[ko], x=[BK(bank)])
        else:
            S.op("dve", lambda e: e.tensor_copy(out=o[0:M, :], in_=B(bank)[0:M, :]), w=[ko], x=[BK(bank)])
        S.dma(lambda e: e.dma_start(out=dst, in_=o[0:M, :]), r=[ko])

    tasks = []

    def t_rope(ja, jr, ntb, dstf):
        def ld():
            return [load_w(ja), load_w(jr)]
        def cp_(ws):
            (wa, ka_), (wr, kr_) = ws
            for tb in range(ntb):
                bA = nb(); bB = nb()
                proj_fm(wa, ka_, tb, bA); proj_fm(wr, kr_, tb, bB)
                rope_out(bA, bB, tb, 128, dstf(tb))
        tasks.append((ld, cp_))

    def t_simple(j, dstf, kind, bias=None):
        def ld():
            return [load_w(j)]
        def cp_(ws):
            wa, ka_ = ws[0]
            for tb in range(4):
                bA = nb(); proj_fm(wa, ka_, tb, bA)
                simple_out(bA, dstf(tb), kind, bias=bias)
        tasks.append((ld, cp_))

    for h in range(8):
        t_rope(h, 8 + h, 4, lambda tb, h=h: Qa[h, :, tb * 512:(tb + 1) * 512])
        t_rope(16 + h, 24 + h, 8, lambda tb, h=h: Ka[h, :, tb * 512:(tb + 1) * 512])
    for h in range(8):
        t_simple(32 + h, lambda tb, h=h: Za[h, :, tb * 512:(tb + 1) * 512], "silu")
        t_simple(40 + h, lambda tb, h=h: Zb[h, :, tb * 512:(tb + 1) * 512], "silu")
    for j in range(16):
        t_simple(48 + j, lambda tb, j=j: Sg[j, :, tb * 512:(tb + 1) * 512], "sig", bias=sm[:, 16 + j:17 + j])
    hnd = {}
    for i in range(min(2, len(tasks))):
        hnd[i] = tasks[i][0]()
    for i in range(len(tasks)):
        if i + 2 < len(tasks):
            hnd[i + 2] = tasks[i + 2][0]()
        tasks[i][1](hnd.pop(i))
    wvs2 = [alloc([8, 512], BF16) for _ in range(2)]
    cp = [0]
    for g in range(2):
        S.dma(lambda e, g=g: e.dma_start(out=wvs2[g], in_=wv_in[g], max_dma_last_dim=4096), w=["wvs%d" % g], q="pool")
    for g in range(2):
        wvs = wvs2[g]
        for tt in range(32):
            bA = nb()
            for c in range(8):
                S.op("pe", lambda e, c=c, tt=tt, bA=bA, wvs=wvs: e.matmul(B(bA), lhsT=xnT[:, c, tt * 128:(tt + 1) * 128], rhs=wvs[:, c, :], start=(c == 0), stop=(c == 7)),
                     r=["wvs%d" % g, ("xnT", tt // 4)], x=[BK(bA)])
            cp[0] += 1
            simple_out(bA, Va[tt * 128:(tt + 1) * 128, g * 512:(g + 1) * 512], "copy_act" if cp[0] % 2 else "copy")
    cqn = alloc([3, NO], BF16)
    ckvn = alloc([2, NT], BF16)
    latf = alloc([3, 512], F32)
    sqb = alloc([3, 512], BF16)
    lnv = alloc([512], F32)
    rstd = alloc([512], F32)

    def latent(jlist, ntb, dstn, gcol, nfeat):
        ws = [load_w(j) for j in jlist]
        n = len(jlist)
        for tb in range(ntb):
            banks = []
            for q in range(n):
                bA = nb(); banks.append(bA)
                proj_fm(ws[q][0], ws[q][1], tb, bA)
            for q in range(n):
                S.op("dve", lambda e, q=q, bA=banks[q]: e.tensor_copy(out=latf[:, q, :], in_=B(bA)), w=[("latf", q)], x=[BK(banks[q])])
                S.op("pool", lambda e, q=q: e.tensor_tensor(out=sqb[:, q, :], in0=latf[:, q, :], in1=latf[:, q, :], op=ALU.mult), r=[("latf", q)], w=[("sqb", q)])
            bM = nb()
            for q in range(n):
                S.op("pe", lambda e, q=q, bM=bM: e.matmul(B(bM), lhsT=onesb, rhs=sqb[:, q, :], start=(q == 0), stop=(q == n - 1)),
                     r=["onesb", ("sqb", q)], x=[BK(bM)])
            S.op("act", lambda e, bM=bM: e.activation(out=lnv, in_=B(bM), func=AF.Ln, scale=1.0 / nfeat, bias=sm[:, 41:42]), r=["sm"], w=["lnv"], x=[BK(bM)])
            S.op("act", lambda e: e.activation(out=rstd, in_=lnv, func=AF.Exp, scale=-0.5), r=["lnv"], w=["rstd"])
            for q in range(n):
                S.op("dve", lambda e, q=q, tb=tb: e.scalar_tensor_tensor(out=dstn[:, q, tb * 512:(tb + 1) * 512], in0=latf[:, q, :], scalar=sm[:, gcol + q:gcol + q + 1],
                                                                       in1=rstd, op0=ALU.mult, op1=ALU.mult),
                     r=[("latf", q), "rstd", "sm"], w=[("lat", id(dstn), q, tb)])

    latent([64, 65, 66], 4, cqn, 32, 384.0)
    latent([67, 68], 8, ckvn, 35, 256.0)
    LATK = [("lat", id(cqn), q, tb) for q in range(3) for tb in range(4)]
    LATKV = [("lat", id(ckvn), q, tb) for q in range(2) for tb in range(8)]
    wa, ka_ = load_w(69)
    for tb in range(8):
        bA = nb(); bB = nb()
        proj_fm(wa, ka_, tb, bA, M=64, woff=0); proj_fm(wa, ka_, tb, bB, M=64, woff=64)
        rope_out(bA, bB, tb, 64, Kp[:, tb * 512:(tb + 1) * 512])
    wqn = [alloc([3, 128], BF16) for _ in range(2)]
    wqp = [alloc([3, 128], BF16) for _ in range(2)]
    wkk = [alloc([2, 128], BF16) for _ in range(2)]
    def ld_up(h):
        s2 = h % 2
        S.dma(lambda e, h=h, s2=s2: e.dma_start(out=wqn[s2], in_=wuqn_in[h], max_dma_last_dim=4096), w=["wqn%d" % s2], q="pool")
        S.dma(lambda e, h=h, s2=s2: e.dma_start(out=wqp[s2], in_=wuqp_in[h], max_dma_last_dim=4096), w=["wqp%d" % s2], q="pool")
        S.dma(lambda e, h=h, s2=s2: e.dma_start(out=wkk[s2], in_=wukk_in[h], max_dma_last_dim=4096), w=["wkk%d" % s2], q="pool")
    wvv2 = [alloc([2, 512], BF16) for _ in range(2)]
    for g in range(2):
        S.dma(lambda e, g=g: e.dma_start(out=wvv2[g], in_=wukv_in[g], max_dma_last_dim=4096), w=["wvv%d" % g], q="pool")
    ld_up(0)
    for h in range(8):
        s2 = h % 2
        if h + 1 < 8:
            ld_up(h + 1)
        for tb in range(4):
            bA = nb()
            for j in range(3):
                S.op("pe", lambda e, j=j, bA=bA, tb=tb, s2=s2: e.matmul(B(bA), lhsT=wqn[s2][:, j, :], rhs=cqn[:, j, tb * 512:(tb + 1) * 512], start=(j == 0), stop=(j == 2)),
                     r=["wqn%d" % s2] + LATK, x=[BK(bA)])
            simple_out(bA, Qn[h, :, tb * 512:(tb + 1) * 512], "copy")
            bA = nb(); bB = nb()
            for j in range(3):
                S.op("pe", lambda e, j=j, bA=bA, tb=tb, s2=s2: e.matmul(B(bA)[0:64, :], lhsT=wqp[s2][:, j, 0:64], rhs=cqn[:, j, tb * 512:(tb + 1) * 512], start=(j == 0), stop=(j == 2)),
                     r=["wqp%d" % s2] + LATK, x=[BK(bA)])
            for j in range(3):
                S.op("pe", lambda e, j=j, bB=bB, tb=tb, s2=s2: e.matmul(B(bB)[0:64, :], lhsT=wqp[s2][:, j, 64:128], rhs=cqn[:, j, tb * 512:(tb + 1) * 512], start=(j == 0), stop=(j == 2)),
                     r=["wqp%d" % s2] + LATK, x=[BK(bB)])
            rope_out(bA, bB, tb, 64, Qp[h, :, tb * 512:(tb + 1) * 512])
        for tb in range(8):
            bA = nb()
            for j in range(2):
                S.op("pe", lambda e, j=j, bA=bA, tb=tb, s2=s2: e.matmul(B(bA), lhsT=wkk[s2][:, j, :], rhs=ckvn[:, j, tb * 512:(tb + 1) * 512], start=(j == 0), stop=(j == 1)),
                     r=["wkk%d" % s2] + LATKV, x=[BK(bA)])
            simple_out(bA, Kn[h, :, tb * 512:(tb + 1) * 512], "copy_act")
    for g in range(2):
        wvv = wvv2[g]
        for tt in range(32):
            bA = nb()
            for j in range(2):
                S.op("pe", lambda e, j=j, tt=tt, bA=bA, wvv=wvv: e.matmul(B(bA), lhsT=ckvn[:, j, tt * 128:(tt + 1) * 128], rhs=wvv[:, j, :], start=(j == 0), stop=(j == 1)),
                     r=["wvv%d" % g] + LATKV, x=[BK(bA)])
            cp[0] += 1
            simple_out(bA, Vb[tt * 128:(tt + 1) * 128, g * 512:(g + 1) * 512], "copy_act" if cp[0] % 2 else "copy")

    S.barrier()
    cur[0] = base0
    NB2 = 2
    qT = [alloc([NO], BF16) for _ in range(NB2)]
    kT = [alloc([NT], BF16) for _ in range(NB2)]
    vS = [alloc([32, 128], BF16) for _ in range(NB2)]
    zT = [alloc([NO], BF16) for _ in range(NB2)]
    qpT = [alloc([NO], BF16) for _ in range(NB2)]
    kpT = alloc([NT], BF16)
    NP = 4
    P1 = [alloc([512], BF16) for _ in range(NP)]
    P2 = [alloc([512], BF16) for _ in range(NP)]
    fr1 = alloc([512], F32); fo1 = alloc([512], F32); fr2 = alloc([512], F32); ft2 = alloc([512], F32)
    fo = alloc([512], F32); fsq = alloc([512], BF16); fln = alloc([512], F32); frs = alloc([512], F32); fu = alloc([512], F32)
    gob = [alloc([512], BF16) for _ in range(2)]
    gcnt = [0]
    cur[0] = max(cur[0], 98048)
    base2 = cur[0]
    wfin = [alloc([8, 1024], BF16) for _ in range(4)]
    wpp = alloc([2, 1024], BF16)
    pTs = alloc([2, NO], BF16)
    rows = alloc([5, 1024], F32)

    def prefetch_p3():
        for i in range(4):
            for hh in range(2):
                S.dma(lambda e, i=i, hh=hh: e.dma_start(out=wfin[i][:, 4 * hh:4 * hh + 4, :], in_=wfin_in[i, :, 4 * hh:4 * hh + 4, :], max_dma_last_dim=4096), w=[("wfin", i, hh)], q="pool")
        S.dma(lambda e: e.dma_start(out=wpp, in_=wpp_in[:, :, :], max_dma_last_dim=4096), w=["wpp"], q="pool")
        S.dma(lambda e: e.dma_start(out=pTs, in_=pT_in[:, :, :], max_dma_last_dim=4096), w=["pTs"], q="pool")
        for i in range(5):
            S.dma(lambda e, i=i: e.dma_start(out=rows[:, i, :], in_=rows_in[i:i + 1, :].broadcast_to([128, 1024])), w=[("rows", i)], q="pool")

    def ktiles(g):
        lst = []
        for t in range(4 * g):
            lst.append((t, 0, None))
            lst.append((16 + t, 0, None))
        for m in range(4):
            lst.append((4 * g + m, 128 * m, 0))
            lst.append((16 + 4 * g + m, 128 * m, 1))
        return lst

    SC_A = float(64 ** -0.5)
    SC_B = float(192 ** -0.5)
    pcnt = [0]

    def load_head(h, mixer):
        s = h % NB2
        kq = "qT%d" % s; kk = "kT%d" % s; kv = "vS%d" % s; kz = "zT%d" % s; kqp = "qpT%d" % s
        if mixer == 0:
            S.dma(lambda e: e.dma_start(out=qT[s], in_=Qa[h]), w=[kq])
            S.dma(lambda e: e.dma_start(out=kT[s], in_=Ka[h]), w=[kk])
            for qq in range(4):
                S.dma(lambda e, qq=qq: e.dma_start(out=vS[s][:, 8 * qq:8 * qq + 8, :], in_=Va.ap().rearrange("(t p) n -> p t n", p=128)[:, 8 * qq:8 * qq + 8, h * 128:(h + 1) * 128]), w=[kv + "_%d" % qq])
            S.dma(lambda e: e.dma_start(out=zT[s], in_=Za[h]), w=[kz])
        else:
            S.dma(lambda e: e.dma_start(out=qT[s], in_=Qn[h]), w=[kq])
            S.dma(lambda e: e.dma_start(out=kT[s], in_=Kn[h]), w=[kk])
            for qq in range(4):
                S.dma(lambda e, qq=qq: e.dma_start(out=vS[s][:, 8 * qq:8 * qq + 8, :], in_=Vb.ap().rearrange("(t p) n -> p t n", p=128)[:, 8 * qq:8 * qq + 8, h * 128:(h + 1) * 128]), w=[kv + "_%d" % qq])
            S.dma(lambda e: e.dma_start(out=zT[s], in_=Zb[h]), w=[kz])
            S.dma(lambda e: e.dma_start(out=qpT[s][0:64, :], in_=Qp[h]), w=[kqp])

    def attention(h, mixer, nxt=None):
        s = h % NB2
        kq = "qT%d" % s; kk = "kT%d" % s; kv = "vS%d" % s; kz = "zT%d" % s; kqp = "qpT%d" % s
        nmap = 2 if mixer == 0 else 1
        for g in range(4):
            kts = ktiles(g)
            q0 = 512 * g

            def qk(i):
                slot, c0, mk = kts[i]
                for mp in range(nmap):
                    bank = (2 * mp + (i % 2)) if mixer == 0 else (i % 4)
                    nomask = (mk is None)
                    if mixer == 0:
                        rows = slice(64 * mp, 64 * mp + 64)
                        S.op("pe", lambda e, rows=rows, bank=bank, slot=slot, c0=c0, q0=q0, nomask=nomask: e.matmul(B(bank)[:, c0:512], lhsT=kT[s][rows, slot * 128:(slot + 1) * 128],
                                                                                            rhs=qT[s][rows, q0 + c0:q0 + 512], start=True, stop=nomask),
                             r=[kq, kk], x=[BK(bank)])
                    else:
                        S.op("pe", lambda e, bank=bank, slot=slot, c0=c0, q0=q0: e.matmul(B(bank)[:, c0:512], lhsT=kT[s][:, slot * 128:(slot + 1) * 128],
                                                                                 rhs=qT[s][:, q0 + c0:q0 + 512], start=True, stop=False),
                             r=[kq, kk], x=[BK(bank)])
                        S.op("pe", lambda e, bank=bank, slot=slot, c0=c0, q0=q0, nomask=nomask: e.matmul(B(bank)[:, c0:512], lhsT=kpT[:, slot * 128:(slot + 1) * 128],
                                                                                 rhs=qpT[s][:, q0 + c0:q0 + 512], start=False, stop=nomask),
                             r=[kqp, "kpT", "kpz", "qpz%d" % s], x=[BK(bank)])
                    if mk is not None:
                        S.op("pe", lambda e, bank=bank, c0=c0, mk=mk: e.matmul(B(bank)[:, c0:c0 + 128], lhsT=identb, rhs=masks[:, mk * 128:(mk + 1) * 128],
                                                                             start=False, stop=True),
                             r=["identb", "masks"], x=[BK(bank)])

            def expo(i):
                slot, c0, mk = kts[i]
                res = []
                for mp in range(nmap):
                    bank = (2 * mp + (i % 2)) if mixer == 0 else (i % 4)
                    pc = pcnt[0] % NP
                    Pb = (P1 if mp == 0 else P2)[pc]
                    pk = "P%d_%d" % (mp, pc)
                    S.op("act", lambda e, bank=bank, Pb=Pb, c0=c0: e.activation(out=Pb[:, c0:512], in_=B(bank)[:, c0:512], func=AF.Exp,
                                                                              scale=(SC_A if mixer == 0 else SC_B)),
                         w=[pk], x=[BK(bank)])
                    res.append((Pb, pk))
                pcnt[0] += 1
                return res

            def pv(i, pbs):
                slot, c0, mk = kts[i]
                first = (i == 0); last = (i == len(kts) - 1)
                for mp in range(nmap):
                    Pb, pk = pbs[mp]
                    bo = 4 + mp; bl = 6 + mp
                    S.op("pe", lambda e, Pb=Pb, bo=bo, slot=slot, c0=c0: e.matmul(B(bo)[:, c0:512], lhsT=vS[s][:, slot, :], rhs=Pb[:, c0:512], start=first, stop=last),
                         r=[pk] + [kv + "_%d" % qq for qq in range(4)], x=[BK(bo)])
                    S.op("pe", lambda e, Pb=Pb, bl=bl, c0=c0: e.matmul(B(bl)[:, c0:512], lhsT=onesb, rhs=Pb[:, c0:512], start=first, stop=last),
                         r=[pk, "onesb"], x=[BK(bl)])

            LA = 2
            for i0 in range(LA):
                qk(i0)
            for i in range(len(kts)):
                pbs = expo(i)
                if i == 5 and pend2[0] is not None:
                    pend2[0](i)
                    pend2[0] = None
                if i + LA < len(kts):
                    qk(i + LA)
                pv(i, pbs)
                if i == 1 and pend[0] is not None:
                    pend[0]()
                    pend[0] = None
                if i == 6 and pend3[0] is not None:
                    pend3[0]()
                    pend3[0] = None
                if i == 7 and g == 0 and nxt is not None:
                    load_head(*nxt)
            if pend[0] is not None:
                pend[0]()
                pend[0] = None
            if pend2[0] is not None:
                pend2[0](len(kts) - 1)
                pend2[0] = None
            if pend3[0] is not None:
                pend3[0]()
                pend3[0] = None
            gi = gcnt[0] % 2
            gcnt[0] += 1
            go = gob[gi]; gk = "gob%d" % gi
            if mixer == 0:
                S.op("dve", lambda e: e.tensor_copy(out=fr1, in_=B(6)), w=["fr1c"], x=[BK(6)])
                S.op("dve", lambda e: e.tensor_copy(out=fo1, in_=B(4)), w=["fo1c"], x=[BK(4)])
                S.op("dve", lambda e: e.tensor_copy(out=fr2, in_=B(7)), w=["fr2c"], x=[BK(7)])
                S.op("dve", lambda e: e.tensor_copy(out=ft2, in_=B(5)), w=["ft2c"], x=[BK(5)])

                def partB1():
                    S.op("dve", lambda e: e.tensor_tensor(out=fo1, in0=fo1, in1=fr2, op=ALU.mult), r=["fo1c", "fr2c"], w=["fo1"])
                    S.op("dve", lambda e: e.tensor_tensor(out=ft2, in0=ft2, in1=fr1, op=ALU.mult), r=["ft2c", "fr1c"], w=["ft2"])
                    S.op("dve", lambda e: e.scalar_tensor_tensor(out=fo, in0=ft2, scalar=neglam[:, 1:2], in1=fo1, op0=ALU.mult, op1=ALU.add),
                         r=["ft2", "fo1", "neglam"], w=["fo"])
                    S.op("dve", lambda e: e.tensor_tensor(out=fr1, in0=fr1, in1=fr2, op=ALU.mult), r=["fr1c", "fr2c", "ft2"], w=["fcc"])
                    S.op("dve", lambda e: e.scalar_tensor_tensor(out=fr2, in0=fr1, scalar=1e-6, in1=fr1, op0=ALU.mult, op1=ALU.mult),
                         r=["fcc", "fo1"], w=["fce"])
                    S.op("pool", lambda e: e.tensor_tensor(out=fsq, in0=fo, in1=fo, op=ALU.mult), r=["fo"], w=["fsq"])

                def partB2(i_):
                    bk_ = i_ % 2
                    S.op("pe", lambda e: e.matmul(B(bk_), lhsT=onesb, rhs=fsq, start=True, stop=True), r=["fsq", "onesb"], x=[BK(bk_)])
                    S.op("dve", lambda e: e.scalar_tensor_tensor(out=fln, in0=B(bk_), scalar=1.0 / 128.0, in1=fr2, op0=ALU.mult, op1=ALU.add),
                         r=["fce"], w=["flt"], x=[BK(bk_)])

                def partB3(go=go, gk=gk, q0=q0, h=h, s=s, kz=kz):
                    S.op("act", lambda e: e.activation(out=fln, in_=fln, func=AF.Ln), r=["flt"], w=["fln"])
                    S.op("act", lambda e: e.activation(out=frs, in_=fln, func=AF.Exp, scale=-0.5), r=["fln"], w=["frs"])
                    S.op("dve", lambda e: e.scalar_tensor_tensor(out=fu, in0=fo, scalar=gsub, in1=frs, op0=ALU.mult, op1=ALU.mult),
                         r=["fo", "frs", "gsub"], w=["fu"])
                    S.op("pool", lambda e: e.tensor_tensor(out=go, in0=fu, in1=zT[s][:, q0:q0 + 512], op=ALU.mult), r=["fu", kz], w=[gk])
                    S.dma(lambda e: e.dma_start(out=Ga[h, :, q0:q0 + 512], in_=go), r=[gk])
                pend3[0] = partB3
                pend[0] = partB1
                pend2[0] = partB2
            else:
                S.op("dve", lambda e: e.tensor_copy(out=fr1, in_=B(6)), w=["fr1c"], x=[BK(6)])
                S.op("dve", lambda e: e.tensor_copy(out=fo1, in_=B(4)), w=["fo1c"], x=[BK(4)])

                def partBm(i_, go=go, gk=gk, q0=q0, h=h, s=s, kz=kz):
                    S.op("dve", lambda e: e.reciprocal(out=fr1, in_=fr1), r=["fr1c"], w=["fr1"])
                    S.op("dve", lambda e: e.tensor_tensor(out=fo1, in0=fo1, in1=fr1, op=ALU.mult), r=["fr1", "fo1c"], w=["fo1"])
                    S.op("pool", lambda e: e.tensor_tensor(out=go, in0=fo1, in1=zT[s][:, q0:q0 + 512], op=ALU.mult), r=["fo1", kz], w=[gk])
                    S.dma(lambda e: e.dma_start(out=Gb[h, :, q0:q0 + 512], in_=go), r=[gk])
                pend2[0] = partBm

    for s_ in range(NB2):
        S.op("pool", lambda e, s_=s_: e.memset(qpT[s_][64:128, :], 0.0), w=["qpz%d" % s_])
    S.op("pool", lambda e: e.memset(kpT[64:128, :], 0.0), w=["kpz"])
    pend = [None]
    pend2 = [None]
    pend3 = [None]
    if stop_after >= 2:
        items = [(h, 0) for h in range(8)] + [(h, 1) for h in range(8)]
        S.dma(lambda e: e.dma_start(out=kpT[0:64, :], in_=Kp[:, :]), w=["kpT"])
        load_head(*items[0])
        prefetch_p3()
        for i, it in enumerate(items):
            attention(it[0], it[1], items[i + 1] if i + 1 < len(items) else None)
        if pend[0] is not None:
            pend[0]()
            pend[0] = None
        if pend2[0] is not None:
            pend2[0](1)
            pend2[0] = None
        if pend3[0] is not None:
            pend3[0]()
            pend3[0] = None

    S.barrier()
    cur[0] = base0
    if stop_after < 2:
        prefetch_p3()
    WF = lambda i: [("wfin", i, 0), ("wfin", i, 1)]
    gaS = alloc([8, 512], BF16); gbS = alloc([8, 512], BF16); sgS = alloc([16, 512], BF16)
    mT = alloc([8, 512], BF16)
    ft = alloc([512], F32); fu2 = alloc([512], F32)
    xr2 = [alloc([1024], F32) for _ in range(2)]; xnr2 = xr2
    yy2 = [alloc([1024], F32) for _ in range(2)]; ybf2 = [alloc([1024], BF16) for _ in range(2)]
    yT2 = [alloc([8, 128], BF16) for _ in range(2)]
    sgi2 = [alloc([1024], F32) for _ in range(2)]; y22 = [alloc([1024], F32) for _ in range(2)]
    st32 = [alloc([12], F32) for _ in range(2)]; mv32 = [alloc([4], F32) for _ in range(2)]
    oo = [alloc([1024], F32) for _ in range(2)]
    MT = [("mT", dc) for dc in range(8)]
    assert cur[0] <= base2, ("phase-3 work buffers overlap prefetched weights", cur[0], base2)

    def blk_load(blk):
        c0 = blk * 512
        S.dma(lambda e, c0=c0: e.dma_start(out=gaS, in_=Ga.ap().rearrange("h p t -> p h t")[:, :, c0:c0 + 512]), w=["gaS"])
        S.dma(lambda e, c0=c0: e.dma_start(out=gbS, in_=Gb.ap().rearrange("h p t -> p h t")[:, :, c0:c0 + 512]), w=["gbS"])
        for hh2 in range(2):
            S.dma(lambda e, c0=c0, hh2=hh2: e.dma_start(out=sgS[:, 8 * hh2:8 * hh2 + 8, :], in_=Sg.ap().rearrange("h p t -> p h t")[:, 8 * hh2:8 * hh2 + 8, c0:c0 + 512]), w=["sgS%d" % hh2])

    def blk_prep(blk):
        for dc in range(8):
            bA = 7; bB = 6
            for hh in range(8):
                S.op("pe", lambda e, hh=hh, dc=dc, bA=bA: e.matmul(B(bA), lhsT=wfin[0][:, hh, dc * 128:(dc + 1) * 128], rhs=gaS[:, hh, :], start=(hh == 0), stop=(hh == 7)),
                     r=WF(0) + ["gaS"], x=[BK(bA)])
            for hh in range(8):
                S.op("pe", lambda e, hh=hh, dc=dc, bB=bB: e.matmul(B(bB), lhsT=wfin[1][:, hh, dc * 128:(dc + 1) * 128], rhs=gbS[:, hh, :], start=(hh == 0), stop=(hh == 7)),
                     r=WF(1) + ["gbS"], x=[BK(bB)])
            S.op("dve", lambda e, dc=dc, bA=bA: e.tensor_tensor(out=ft, in0=B(bA), in1=sgS[:, dc, :], op=ALU.mult), r=["sgS0"], w=["ft"], x=[BK(bA)])
            S.op("dve", lambda e, dc=dc, bB=bB: e.tensor_tensor(out=fu2, in0=B(bB), in1=sgS[:, 8 + dc, :], op=ALU.mult), r=["sgS1"], w=["fu2"], x=[BK(bB)])
            S.op("pool", lambda e, dc=dc: e.tensor_tensor(out=mT[:, dc, :], in0=ft, in1=fu2, op=ALU.add), r=["ft", "fu2"], w=[("mT", dc)])

    def bufs(t):
        u = t % 2
        return dict(u=u, xr=xr2[u], xnr=xnr2[u], yy=yy2[u], ybf=ybf2[u], yT=yT2[u], sgi=sgi2[u], y2=y22[u], st3=st32[u], mv3=mv32[u], o_=oo[u])

    def st_X(t):
        d_ = bufs(t); u = d_["u"]
        xr = d_["xr"]
        K = lambda n: "%s_%d" % (n, u)
        S.dma(lambda e, t=t, xr=xr: e.dma_start(out=xr, in_=x_in[t * 128:(t + 1) * 128, :]), w=[K("xr"), K("xnr")])
        S.op("dve", lambda e, t=t, xr=xr: e.scalar_tensor_tensor(out=xr, in0=xr, scalar=stats[:, t, 0:1], in1=rows[:, 0, :], op0=ALU.subtract, op1=ALU.mult),
             r=[K("xr"), ("rows", 0)], w=[K("xnr0")])
        S.op("dve", lambda e, t=t, xr=xr: e.scalar_tensor_tensor(out=xr, in0=xr, scalar=stats[:, t, 3:4], in1=rows[:, 1, :], op0=ALU.mult, op1=ALU.add),
             r=[K("xnr0"), ("rows", 1)], w=[K("xnr")])

    def st_A(t):
        d_ = bufs(t); u = d_["u"]
        xr = d_["xr"]; xnr = d_["xnr"]; yy = d_["yy"]; ybf = d_["ybf"]; yT = d_["yT"]
        K = lambda n: "%s_%d" % (n, u)
        tc0 = (t % 4) * 128
        bt = 6
        for half in range(2):
            for dc in range(8):
                S.op("pe", lambda e, dc=dc, half=half, tc0=tc0: e.matmul(B(4 + half), lhsT=mT[:, dc, tc0:tc0 + 128], rhs=wfin[2][:, dc, half * 512:(half + 1) * 512],
                                                                       start=(dc == 0), stop=(dc == 7)),
                     r=WF(2) + MT, x=[BK(4 + half)])
        for half in range(2):
            S.op("dve", lambda e, half=half, yy=yy, xnr=xnr: e.scalar_tensor_tensor(out=yy[:, half * 512:(half + 1) * 512], in0=xnr[:, half * 512:(half + 1) * 512], scalar=ALPHA,
                                                                  in1=B(4 + half), op0=ALU.mult, op1=ALU.add),
                 r=[K("xnr")], w=[K("yy%d" % half)], x=[BK(4 + half)])
        S.op("act", lambda e, ybf=ybf, yy=yy: e.activation(out=ybf, in_=yy, func=AF.Copy), r=[K("yy0"), K("yy1")], w=[K("ybf")])

    def st_A2(t):
        d_ = bufs(t); u = d_["u"]
        ybf = d_["ybf"]; yT = d_["yT"]
        K = lambda n: "%s_%d" % (n, u)
        bt = 6
        for dc in range(8):
            S.op("pe", lambda e, dc=dc, ybf=ybf, bt=bt: e.transpose(out=pst16[:, bt, dc * 128:(dc + 1) * 128], in_=ybf[:, dc * 128:(dc + 1) * 128], identity=identb),
                 r=[K("ybf"), "identb"], x=[BK(bt)])
        S.op("dve", lambda e, yT=yT, bt=bt: e.tensor_copy(out=yT, in_=pst16[:, bt, :].rearrange("p (a b) -> p a b", a=8)), w=[K("yT")], x=[BK(bt)])

    def st_B(t):
        d_ = bufs(t); u = d_["u"]
        yy = d_["yy"]; yT = d_["yT"]; sgi = d_["sgi"]; y2 = d_["y2"]
        K = lambda n: "%s_%d" % (n, u)
        for half in range(2):
            for dc in range(8):
                S.op("pe", lambda e, dc=dc, half=half, yT=yT: e.matmul(B(half), lhsT=yT[:, dc, :], rhs=wfin[3][:, dc, half * 512:(half + 1) * 512], start=(dc == 0), stop=(dc == 7)),
                     r=WF(3) + [K("yT")], x=[BK(half)])
            for j in range(2):
                S.op("pe", lambda e, j=j, half=half, t=t: e.matmul(B(2 + half), lhsT=pTs[:, j, t * 128:(t + 1) * 128], rhs=wpp[:, j, half * 512:(half + 1) * 512], start=(j == 0), stop=(j == 1)),
                     r=["pTs", "wpp"], x=[BK(2 + half)])
        for half in range(2):
            hs = slice(half * 512, (half + 1) * 512)
            S.op("dve", lambda e, half=half, hs=hs, sgi=sgi: e.tensor_tensor(out=sgi[:, hs], in0=B(half), in1=rows[:, 2, hs], op=ALU.add), r=[("rows", 2)], w=[K("sgi%d" % half)], x=[BK(half)])
            S.op("act", lambda e, hs=hs, sgi=sgi: e.activation(out=sgi[:, hs], in_=sgi[:, hs], func=AF.Sigmoid), r=[K("sgi%d" % half)], w=[K("sgo%d" % half)])
            S.op("dve", lambda e, half=half, hs=hs, sgi=sgi, y2=y2: e.tensor_tensor(out=y2[:, hs], in0=B(2 + half), in1=sgi[:, hs], op=ALU.mult), r=[K("sgo%d" % half)], w=[K("y2a%d" % half)], x=[BK(2 + half)])
            S.op("pool", lambda e, hs=hs, y2=y2, yy=yy: e.tensor_tensor(out=y2[:, hs], in0=y2[:, hs], in1=yy[:, hs], op=ALU.add), r=[K("y2a%d" % half), K("yy%d" % half)], w=[K("y2%d" % half)])

    def st_C(t):
        d_ = bufs(t); u = d_["u"]
        y2 = d_["y2"]; st3 = d_["st3"]; mv3 = d_["mv3"]; o_ = d_["o_"]
        K = lambda n: "%s_%d" % (n, u)
        ok_ = "oo%d" % u
        S.op("dve", lambda e, st3=st3, y2=y2: e.bn_stats(out=st3[:, 0:6], in_=y2[:, 0:512]), r=[K("y20")], w=[K("st3a")])
        S.op("dve", lambda e, st3=st3, y2=y2: e.bn_stats(out=st3[:, 6:12], in_=y2[:, 512:1024]), r=[K("y21")], w=[K("st3b")])
        S.op("dve", lambda e, st3=st3, mv3=mv3: e.bn_aggr(out=mv3[:, 0:2], in_=st3), r=[K("st3a"), K("st3b")], w=[K("mv3")])
        S.op("act", lambda e, mv3=mv3: e.activation(out=mv3[:, 2:3], in_=mv3[:, 1:2], func=AF.Ln, bias=sm[:, 40:41], scale=1.0), r=[K("mv3"), "sm"], w=[K("lv3")])
        S.op("act", lambda e, mv3=mv3: e.activation(out=mv3[:, 3:4], in_=mv3[:, 2:3], func=AF.Exp, scale=-0.5), r=[K("lv3")], w=[K("rs3")])
        S.op("dve", lambda e, o_=o_, y2=y2, mv3=mv3: e.scalar_tensor_tensor(out=o_, in0=y2, scalar=mv3[:, 0:1], in1=rows[:, 3, :], op0=ALU.subtract, op1=ALU.mult),
             r=[K("y20"), K("y21"), K("mv3"), ("rows", 3)], w=[ok_ + "a"])
        S.op("dve", lambda e, o_=o_, mv3=mv3: e.scalar_tensor_tensor(out=o_, in0=o_, scalar=mv3[:, 3:4], in1=rows[:, 4, :], op0=ALU.mult, op1=ALU.add),
             r=[ok_ + "a", K("rs3"), ("rows", 4)], w=[ok_])
        S.dma(lambda e, o_=o_, t=t: e.dma_start(out=out_t[t * 128:(t + 1) * 128, :], in_=o_), r=[ok_])

    blk_load(0)
    st_X(0)
    for it in range(16 + 2):
        if 0 <= it - 2 < 16:
            st_C(it - 2)
        if it < 16:
            if it % 4 == 0:
                blk_prep(it // 4)
                if it // 4 + 1 < 4:
                    blk_load(it // 4 + 1)
            st_A(it)
            if it + 1 < 16:
                st_X(it + 1)
        if 0 <= it - 1 < 16:
            st_B(it - 1)
        if it < 16:
            st_A2(it)
    S.emit()
    return nc


def _prep_shared(inp):
    f = lambda k: np.asarray(inp[k], dtype=np.float32)
    W = f("w_in")[0]
    def chunk(cols):
        return W[:, cols].reshape(8, 128, len(cols)).transpose(1, 0, 2)
    def rot128(base):
        n = np.arange(128); m = n // 64; i = n % 64
        return base + 64 * m + np.where(i < 32, i + 32, i - 32)
    wf = []
    for h in range(8): wf.append(chunk(np.arange(128 * h, 128 * h + 128)))
    for h in range(8): wf.append(chunk(rot128(128 * h)))
    for h in range(8): wf.append(chunk(np.arange(1024 + 128 * h, 1024 + 128 * h + 128)))
    for h in range(8): wf.append(chunk(rot128(1024 + 128 * h)))
    for h in range(8): wf.append(chunk(np.arange(3072 + 128 * h, 3072 + 128 * h + 128)))
    for h in range(8): wf.append(chunk(np.arange(4800 + 128 * h, 4800 + 128 * h + 128)))
    for j in range(16): wf.append(chunk(np.arange(5824 + 128 * j, 5824 + 128 * j + 128)))
    for j in range(3): wf.append(chunk(np.arange(4096 + 128 * j, 4096 + 128 * j + 128)))
    for j in range(2): wf.append(chunk(np.arange(4480 + 128 * j, 4480 + 128 * j + 128)))
    i64 = np.arange(64)
    r64 = np.where(i64 < 32, i64 + 32, i64 - 32)
    wf.append(chunk(np.concatenate([4736 + i64, 4736 + r64])))
    wf = np.ascontiguousarray(np.stack(wf), dtype=np.float32)
    wv = np.ascontiguousarray(np.stack([W[:, 2048 + g * 512: 2048 + (g + 1) * 512].reshape(8, 128, 512).transpose(1, 0, 2) for g in range(2)]), dtype=np.float32)
    uq = f("mla_w_uq")[0]; ukv = f("mla_w_ukv")[0]
    wuqn = np.ascontiguousarray(np.stack([uq[:, h * 192:h * 192 + 128].reshape(3, 128, 128).transpose(1, 0, 2) for h in range(8)]), dtype=np.float32)
    wuqp = np.ascontiguousarray(np.stack([uq[:, np.concatenate([h * 192 + 128 + i64, h * 192 + 128 + r64])].reshape(3, 128, 128).transpose(1, 0, 2) for h in range(8)]), dtype=np.float32)
    wukk = np.ascontiguousarray(np.stack([ukv[:, h * 256:h * 256 + 128].reshape(2, 128, 128).transpose(1, 0, 2) for h in range(8)]), dtype=np.float32)
    wukv = np.ascontiguousarray(np.stack([ukv[:, np.concatenate([h * 256 + 128 + np.arange(128) for h in range(4 * g, 4 * g + 4)])].reshape(2, 128, 512).transpose(1, 0, 2) for g in range(2)]), dtype=np.float32)
    wfin = np.ascontiguousarray(np.stack([f(k)[0].reshape(8, 128, 1024).transpose(1, 0, 2) for k in ("w_o_a", "w_o_b", "w_out", "ple_w_gate")]), dtype=np.float32)
    wpp = np.ascontiguousarray(f("ple_w_proj")[0].reshape(2, 128, 1024).transpose(1, 0, 2), dtype=np.float32)
    sm = np.zeros((128, 64), np.float32)
    sm[:, 0:8] = f("ln_emb_g").reshape(8, 128).T
    sm[:, 8:16] = f("ln_emb_b").reshape(8, 128).T
    sm[:, 16:32] = f("b_gate")[0].reshape(16, 128).T
    sm[:, 32:35] = f("mla_q_norm_g")[0].reshape(3, 128).T
    sm[:, 35:37] = f("mla_kv_norm_g")[0].reshape(2, 128).T
    sm[:, 37] = f("diff_subln_g")[0]
    inv = (np.float32(10000.0) ** (-(np.arange(0, 64, 2, dtype=np.float32)) / np.float32(64))).astype(np.float32)
    pp = np.arange(128)
    sm[:, 38] = inv[pp % 32]
    sm[:, 39] = np.where((pp % 64) < 32, -1.0, 1.0)
    sm[:, 40] = 1e-5
    sm[:, 41] = 1e-6
    rows = np.ascontiguousarray(np.stack([f("ln_emb_g"), f("ln_emb_b"), f("ple_b_gate")[0], f("ln_post_g")[0], f("ln_post_b")[0]]), dtype=np.float32)
    dl = np.ascontiguousarray(f("diff_lambda")[0].reshape(1, 256))
    return dict(wf=wf, wv=wv, wuqn=wuqn, wuqp=wuqp, wukk=wukk, wukv=wukv, wfin=wfin, wpp=wpp, sm=sm, rows=rows, dl=dl,
                ident=np.eye(128, dtype=np.float32))


def _core_maps(inp, shared):
    x = np.asarray(inp["x"], dtype=np.float32)
    p = np.asarray(inp["p"], dtype=np.float32)[0]
    pos = np.asarray(inp["positions"]).astype(np.int32)
    maps = []
    orders = []
    kk = np.arange(128)
    tri = np.where(kk[:, None] <= kk[None, :], 0.0, -30000.0).astype(np.float32)
    for c in range(8):
        b, hf = c // 2, c % 2
        order = [2 * s + hf for s in range(16)] + [2 * u + 1 - hf for u in range(16)]
        idx = np.concatenate([np.arange(t * 128, (t + 1) * 128) for t in order])
        m = dict(shared)
        m["x"] = np.ascontiguousarray(x[b][idx])
        m["pos"] = np.ascontiguousarray(pos[b][idx][None, :])
        m["pT"] = np.ascontiguousarray(p[b][idx[:NO]].T.reshape(2, 128, NO).transpose(1, 0, 2))
        m["masks"] = np.ascontiguousarray(np.concatenate([tri, np.full((128, 128), 0.0 if hf == 1 else -30000.0, np.float32)], axis=1))
        maps.append(m)
        orders.append(order)
    return maps, orders


_NC_CACHE = {}


def kernel(**inp):
    if "nc" not in _NC_CACHE:
        _NC_CACHE["nc"] = build_program()
    nc = _NC_CACHE["nc"]
    shared = _prep_shared(inp)
    maps, orders = _core_maps(inp, shared)
    res = run_bass_kernel_spmd(nc, maps, core_ids=list(range(8)))
    out = np.zeros((4, 4096, 1024), np.float32)
    for c in range(8):
        b = c // 2
        o = np.asarray(res.results[c]["out"], dtype=np.float32)
        for s in range(16):
            t = orders[c][s]
            out[b, t * 128:(t + 1) * 128] = o[s * 128:(s + 1) * 128]
    return out
```

```python
import numpy as np
import contextlib
import concourse.bass as bass
import concourse.mybir as mybir
from concourse.bass_utils import run_bass_kernel_spmd

F32 = mybir.dt.float32
BF16 = mybir.dt.bfloat16
F16 = mybir.dt.float16
I32 = mybir.dt.int32
AF = mybir.ActivationFunctionType
ALU = mybir.AluOpType
AX = mybir.AxisListType

ENGS = ("pe", "act", "dve", "pool", "sp")


class _Op:
    __slots__ = ("idx", "eng", "fn", "deps", "is_dma", "dsem", "dval", "flag", "rank", "predma")

    def __init__(self, idx, eng, fn, is_dma):
        self.idx = idx
        self.eng = eng
        self.fn = fn
        self.deps = set()
        self.is_dma = is_dma
        self.dsem = None
        self.dval = 0
        self.flag = False
        self.rank = 0
        self.predma = None


class Sched:
    def __init__(self, nc, n_dma_sems=40):
        self.nc = nc
        self.ops = []
        self.res = {}
        self.n_dma_sems = n_dma_sems
        self.n_dma = 0
        self.dma_hist = []
        self.bar = None
        self.bar_seen = set()
        self.last_on = {}
        self.dma_since_bar = []

    def _add(self, eng, fn, r, w, is_dma, x=()):
        op = _Op(len(self.ops), eng, fn, is_dma)
        self.ops.append(op)
        ek = ("dma", op.idx) if is_dma else eng
        deps = {}
        for key in x:
            st = self.res.get(key)
            if st is not None:
                for e, i in st[0].items():
                    deps.setdefault(i, False)
                for e, i in st[1].items():
                    deps.setdefault(i, False)
        for key in r:
            st = self.res.get(key)
            if st is not None:
                for e, i in st[0].items():
                    deps[i] = True
        for key in w:
            st = self.res.get(key)
            if st is not None:
                for e, i in st[0].items():
                    deps.setdefault(i, False)
                for e, i in st[1].items():
                    deps.setdefault(i, False)
        for i, raw in deps.items():
            d = self.ops[i]
            if (not d.is_dma) and (not is_dma) and d.eng == eng and not raw:
                continue
            if (not d.is_dma) and (not is_dma) and d.eng == eng and eng == "pe":
                continue
            op.deps.add(i)
        if self.bar is not None and eng not in self.bar_seen:
            self.bar_seen.add(eng)
            for i in self.bar:
                if i != op.idx:
                    d = self.ops[i]
                    if d.is_dma or d.eng != eng:
                        op.deps.add(i)
        for key in r:
            st = self.res.setdefault(key, ({}, {}))
            st[1][ek] = op.idx
        for key in w:
            self.res[key] = ({ek: op.idx}, {})
        for key in x:
            self.res[key] = ({ek: op.idx}, {})
        if is_dma:
            k = self.n_dma % self.n_dma_sems
            op.dsem = k
            op.dval = 16 * (self.n_dma // self.n_dma_sems + 1)
            if self.n_dma >= self.n_dma_sems:
                op.predma = self.dma_hist[self.n_dma - self.n_dma_sems]
            self.dma_hist.append(op.idx)
            self.n_dma += 1
            self.dma_since_bar.append(op.idx)
        else:
            self.last_on[eng] = op.idx
        return op

    def op(self, eng, fn, r=(), w=(), x=()):
        return self._add(eng, fn, r, w, False, x)

    def dma(self, fn, r=(), w=(), q="sp"):
        return self._add(q, fn, r, w, True)

    def barrier(self):
        self.bar = list(self.last_on.values()) + list(self.dma_since_bar)
        self.bar_seen = set()
        self.dma_since_bar = []

    def emit(self, final_waits=True):
        nc = self.nc
        ops = self.ops
        for op in ops:
            for i in op.deps:
                ops[i].flag = True
        cnt = {e: 0 for e in ENGS}
        for op in ops:
            if not op.is_dma and op.flag:
                cnt[op.eng] += 1
                op.rank = cnt[op.eng]
        per = {e: [] for e in ENGS}
        for op in ops:
            per[op.eng].append(op)
        import contextlib

        with contextlib.ExitStack() as st:
            esem = {e: st.enter_context(nc.semaphore("s_" + e)) for e in ENGS if e != "sp"}
            dsem = [st.enter_context(nc.semaphore("d_%d" % i)) for i in range(self.n_dma_sems)]
            block = st.enter_context(nc.Block())
            last_dma_vals = {}
            for op in ops:
                if op.is_dma:
                    last_dma_vals[op.dsem] = op.dval

            def run(eng_name, eng):
                waited = {}

                def wait(sem, key, val):
                    if waited.get(key, 0) >= val:
                        return
                    waited[key] = val
                    eng.wait_ge(sem, val)

                for op in per[eng_name]:
                    if op.predma is not None:
                        p = ops[op.predma]
                        wait(dsem[p.dsem], ("d", p.dsem), p.dval)
                    for i in sorted(op.deps):
                        d = ops[i]
                        if d.is_dma:
                            wait(dsem[d.dsem], ("d", d.dsem), d.dval)
                        else:
                            wait(esem[d.eng], ("e", d.eng), d.rank)
                    ins = op.fn(eng)
                    if op.is_dma:
                        ins.then_inc(dsem[op.dsem], 16)
                    elif op.flag:
                        ins.then_inc(esem[op.eng], 1)
                if eng_name == "sp" and final_waits:
                    for k, v in last_dma_vals.items():
                        wait(dsem[k], ("d", k), v)

            @block.tensor
            def _(e):
                run("pe", e)

            @block.scalar
            def _(e):
                run("act", e)

            @block.vector
            def _(e):
                run("dve", e)

            @block.gpsimd
            def _(e):
                run("pool", e)

            @block.sync
            def _(e):
                run("sp", e)


PI = float(np.pi)
C1 = 6.28125
C2 = float(2.0 * np.pi - 6.28125)
NT = 4096
NO = 2048
LAM_INIT = 0.2
ALPHA = float(2.0 ** 0.25)
LN_EPS_ = 1e-5


def build_program(stop_after=3, debug=False):
    nc = bass.Bass("TRN2", target_bir_lowering=False)
    S = Sched(nc, n_dma_sems=48)

    def din(name, shape, dt=F32):
        return nc.dram_tensor(name, list(shape), dt, kind="ExternalInput")

    x_in = din("x", [NT, 1024])
    pos_in = din("pos", [1, NT], I32)
    pT_in = din("pT", [128, 2, NO])
    wf_in = din("wf", [70, 128, 8, 128])
    wv_in = din("wv", [2, 128, 8, 512])
    wuqn_in = din("wuqn", [8, 128, 3, 128])
    wuqp_in = din("wuqp", [8, 128, 3, 128])
    wukk_in = din("wukk", [8, 128, 2, 128])
    wukv_in = din("wukv", [2, 128, 2, 512])
    wfin_in = din("wfin", [4, 128, 8, 1024])
    wpp_in = din("wpp", [128, 2, 1024])
    sm_in = din("sm", [128, 64])
    rows_in = din("rows", [5, 1024])
    dl_in = din("dl", [1, 256])
    ident_in = din("ident", [128, 128])
    mask_in = din("masks", [128, 256])
    out_t = nc.dram_tensor("out", [NO, 1024], F32, kind="ExternalOutput")

    def scr(name, shape):
        if debug:
            return nc.dram_tensor(name, list(shape), BF16, kind="ExternalOutput")
        return nc.dram_tensor(name, list(shape), BF16)

    Qa = scr("s_qa", [8, 128, NO]); Ka = scr("s_ka", [8, 128, NT]); Va = scr("s_va", [NT, 1024])
    Za = scr("s_za", [8, 128, NO]); Zb = scr("s_zb", [8, 128, NO]); Sg = scr("s_sg", [16, 128, NO])
    Qn = scr("s_qn", [8, 128, NO]); Qp = scr("s_qp", [8, 64, NO]); Kn = scr("s_kn", [8, 128, NT])
    Kp = scr("s_kp", [64, NT]); Vb = scr("s_vb", [NT, 1024])
    Ga = scr("s_ga", [8, 128, NO]); Gb = scr("s_gb", [8, 128, NO])

    ARENA = 206000
    arena = nc.alloc_sbuf_tensor("arena", [128, ARENA // 4], F32)
    views = {F32: arena, BF16: arena.bitcast(BF16), F16: arena.bitcast(F16), I32: arena.bitcast(I32)}
    esz = {F32: 4, BF16: 2, F16: 2, I32: 4}
    cur = [0]

    def alloc(shape, dt):
        n = int(np.prod(shape))
        nb = (n * esz[dt] + 63) // 64 * 64
        off = cur[0]
        cur[0] += nb
        assert cur[0] <= ARENA, ("SBUF arena overflow", cur[0])
        v = views[dt][:, off // esz[dt]: off // esz[dt] + n]
        if len(shape) == 2:
            v = v.rearrange("p (a b) -> p a b", a=shape[0])
        elif len(shape) == 3:
            v = v.rearrange("p (a b c) -> p a b c", a=shape[0], b=shape[1])
        return v

    pst = nc.alloc_psum_tensor("pst", [128, 8, 512], F32)
    pst16 = pst.bitcast(BF16)

    def B(i):
        return pst[:, i, :]

    def BK(i):
        return ("B", i)

    uid = [0]

    def U(p="u"):
        uid[0] += 1
        return "%s%d" % (p, uid[0])

    sm = alloc([64], F32)
    identf = alloc([128], F32)
    identb = alloc([128], BF16)
    onesb = alloc([128], BF16)
    masks = alloc([256], BF16)
    maskf = alloc([256], F32)
    stats = alloc([32, 4], F32)
    neglam = alloc([4], F32)
    gsub = alloc([1], F32)
    S.dma(lambda e: e.dma_start(out=sm, in_=sm_in[:, :]), w=["sm"])
    S.dma(lambda e: e.dma_start(out=identf, in_=ident_in[:, :]), w=["identf"])
    S.dma(lambda e: e.dma_start(out=maskf, in_=mask_in[:, :]), w=["maskf"])
    S.op("dve", lambda e: e.tensor_copy(out=identb, in_=identf), r=["identf"], w=["identb"])
    S.op("dve", lambda e: e.tensor_copy(out=masks, in_=maskf), r=["maskf"], w=["masks"])
    S.op("dve", lambda e: e.memset(onesb, 1.0), w=["onesb"])
    base0 = cur[0]

    xnT = alloc([8, NT], BF16)
    ctab = alloc([NT], F16)
    stab = alloc([NT], F16)
    base1 = cur[0]

    dlb = alloc([256], F32)
    dlp = alloc([128], F32)
    dls = alloc([4], F32)
    S.dma(lambda e: e.dma_start(out=dlb, in_=dl_in.ap().broadcast_to([128, 256])), w=["dlb"])
    S.op("dve", lambda e: e.tensor_tensor(out=dlp[:, 0:64], in0=dlb[:, 0:64], in1=dlb[:, 64:128], op=ALU.mult), r=["dlb"], w=["dlp0"])
    S.op("dve", lambda e: e.tensor_tensor(out=dlp[:, 64:128], in0=dlb[:, 128:192], in1=dlb[:, 192:256], op=ALU.mult), r=["dlb"], w=["dlp1"])
    S.op("dve", lambda e: e.reduce_sum(out=dls[:, 0:1], in_=dlp[:, 0:64], axis=AX.X), r=["dlp0"], w=["dls0"])
    S.op("dve", lambda e: e.reduce_sum(out=dls[:, 1:2], in_=dlp[:, 64:128], axis=AX.X), r=["dlp1"], w=["dls1"])
    S.op("act", lambda e: e.activation(out=dls[:, 2:4], in_=dls[:, 0:2], func=AF.Exp), r=["dls0", "dls1"], w=["dle"])
    S.op("dve", lambda e: e.tensor_tensor(out=neglam[:, 0:1], in0=dls[:, 3:4], in1=dls[:, 2:3], op=ALU.subtract), r=["dle"], w=["nl0"])
    S.op("dve", lambda e: e.tensor_scalar(out=neglam[:, 1:2], in0=neglam[:, 0:1], scalar1=-LAM_INIT, scalar2=None, op0=ALU.add), r=["nl0"], w=["neglam"])
    S.op("dve", lambda e: e.tensor_scalar(out=gsub, in0=sm[:, 37:38], scalar1=1.0 - LAM_INIT, scalar2=None, op0=ALU.mult), r=["sm"], w=["gsub"])

    posi = alloc([NT], I32)
    ang = alloc([NT], F32)
    kf = alloc([NT], F32)
    ki = alloc([NT], I32)
    rr = alloc([NT], F32)
    S.dma(lambda e: e.dma_start(out=posi, in_=pos_in.ap().broadcast_to([128, NT])), w=["posi"])
    S.op("dve", lambda e: e.tensor_copy(out=ang, in_=posi), r=["posi"], w=["angf"])
    S.op("dve", lambda e: e.tensor_scalar(out=ang, in0=ang, scalar1=sm[:, 38:39], scalar2=None, op0=ALU.mult), r=["angf", "sm"], w=["ang"])
    S.op("dve", lambda e: e.tensor_scalar(out=kf, in0=ang, scalar1=float(1.0 / (2 * np.pi)), scalar2=None, op0=ALU.mult), r=["ang"], w=["kf0"])
    S.op("dve", lambda e: e.tensor_copy(out=ki, in_=kf), r=["kf0"], w=["ki"])
    S.op("dve", lambda e: e.tensor_copy(out=kf, in_=ki), r=["ki"], w=["kf"])
    S.op("dve", lambda e: e.scalar_tensor_tensor(out=rr, in0=kf, scalar=-C1, in1=ang, op0=ALU.mult, op1=ALU.add), r=["kf", "ang"], w=["rr1"])
    S.op("dve", lambda e: e.scalar_tensor_tensor(out=rr, in0=kf, scalar=-C2, in1=rr, op0=ALU.mult, op1=ALU.add), r=["kf", "rr1"], w=["rr2"])
    S.op("dve", lambda e: e.tensor_scalar(out=kf, in0=rr, scalar1=PI, scalar2=None, op0=ALU.is_gt), r=["rr2"], w=["m1"])
    S.op("dve", lambda e: e.scalar_tensor_tensor(out=rr, in0=kf, scalar=-2 * PI, in1=rr, op0=ALU.mult, op1=ALU.add), r=["m1", "rr2"], w=["rr3"])
    S.op("dve", lambda e: e.tensor_scalar(out=kf, in0=rr, scalar1=-PI, scalar2=None, op0=ALU.is_lt), r=["rr3"], w=["m2"])
    S.op("dve", lambda e: e.scalar_tensor_tensor(out=rr, in0=kf, scalar=2 * PI, in1=rr, op0=ALU.mult, op1=ALU.add), r=["m2", "rr3"], w=["rs"])
    S.op("dve", lambda e: e.tensor_scalar(out=rr, in0=rr, scalar1=PI, scalar2=-PI, op0=ALU.min, op1=ALU.max), r=["rs"], w=["rs2"])
    S.op("act", lambda e: e.activation(out=stab, in_=rr, func=AF.Sin, scale=sm[:, 39:40]), r=["rs2", "sm"], w=["stab"])
    S.op("dve", lambda e: e.tensor_scalar(out=ang, in0=rr, scalar1=PI / 2, scalar2=None, op0=ALU.add), r=["rs2", "ang"], w=["rc"])
    S.op("dve", lambda e: e.tensor_scalar(out=kf, in0=ang, scalar1=PI, scalar2=None, op0=ALU.is_gt), r=["rc"], w=["m3"])
    S.op("dve", lambda e: e.scalar_tensor_tensor(out=ang, in0=kf, scalar=-2 * PI, in1=ang, op0=ALU.mult, op1=ALU.add), r=["m3", "rc"], w=["rc2"])
    S.op("dve", lambda e: e.tensor_scalar(out=ang, in0=ang, scalar1=PI, scalar2=-PI, op0=ALU.min, op1=ALU.max), r=["rc2"], w=["rc3"])
    S.op("act", lambda e: e.activation(out=ctab, in_=ang, func=AF.Sin), r=["rc3"], w=["ctab"])
    S.barrier()
    cur[0] = base1

    xb = [alloc([1024], F32) for _ in range(3)]
    xh = [alloc([1024], F32) for _ in range(2)]
    bst = [alloc([12], F32) for _ in range(2)]
    def p0_L(t):
        xt = xb[t % 3]
        xk = "xb%d" % (t % 3)
        S.dma(lambda e, xt=xt, t=t: e.dma_start(out=xt, in_=x_in[t * 128:(t + 1) * 128, :]), w=[xk])

    def p0_A(t):
        xt = xb[t % 3]
        xk = "xb%d" % (t % 3)
        st_ = bst[t % 2]
        sk = "bst%d" % (t % 2)
        S.op("dve", lambda e, st_=st_, xt=xt: e.bn_stats(out=st_[:, 0:6], in_=xt[:, 0:512]), r=[xk], w=[sk + "a"])
        S.op("dve", lambda e, st_=st_, xt=xt: e.bn_stats(out=st_[:, 6:12], in_=xt[:, 512:1024]), r=[xk], w=[sk + "b"])
        S.op("dve", lambda e, st_=st_, t=t: e.bn_aggr(out=stats[:, t, 0:2], in_=st_[:, 0:12]), r=[sk + "a", sk + "b"], w=["mv%d" % t])
        S.op("act", lambda e, t=t: e.activation(out=stats[:, t, 2:3], in_=stats[:, t, 1:2], func=AF.Ln, bias=sm[:, 40:41], scale=1.0), r=["mv%d" % t, "sm"], w=["lv%d" % t])
        S.op("act", lambda e, t=t: e.activation(out=stats[:, t, 3:4], in_=stats[:, t, 2:3], func=AF.Exp, scale=-0.5), r=["lv%d" % t], w=["rs%d" % t])
        xhh = xh[t % 2]
        hk = "xh%d" % (t % 2)
        S.op("dve", lambda e, xhh=xhh, xt=xt, t=t: e.tensor_scalar(out=xhh, in0=xt, scalar1=stats[:, t, 0:1], scalar2=stats[:, t, 3:4],
                                                               op0=ALU.subtract, op1=ALU.mult), r=[xk, "mv%d" % t, "rs%d" % t], w=[hk])
        return None

    def p0_B(t):
        xhh = xh[t % 2]
        hk = "xh%d" % (t % 2)
        b0 = 2 * (t % 4)
        for c in range(8):
            bk = b0 + c // 4
            S.op("pe", lambda e, bk=bk, c=c, xhh=xhh: e.transpose(out=B(bk)[:, (c % 4) * 128:(c % 4 + 1) * 128], in_=xhh[:, c * 128:(c + 1) * 128], identity=identf),
                 r=[hk, "identf"], x=[BK(bk)])
        for c in range(8):
            bk = b0 + c // 4
            if c % 2 == 0:
                S.op("act", lambda e, bk=bk, c=c, t=t: e.activation(out=xnT[:, c, t * 128:(t + 1) * 128], in_=B(bk)[:, (c % 4) * 128:(c % 4 + 1) * 128],
                                                                  func=AF.Identity, scale=sm[:, c:c + 1], bias=sm[:, 8 + c:9 + c]),
                     r=["sm"], w=[("xnTw", t, c)], x=[BK(bk)])
            else:
                S.op("dve", lambda e, bk=bk, c=c, t=t: e.tensor_scalar(out=xnT[:, c, t * 128:(t + 1) * 128], in0=B(bk)[:, (c % 4) * 128:(c % 4 + 1) * 128],
                                                                     scalar1=sm[:, c:c + 1], scalar2=sm[:, 8 + c:9 + c], op0=ALU.mult, op1=ALU.add),
                     r=["sm"], w=[("xnTw", t, c)], x=[BK(bk)])

    p0_L(0)
    p0_L(1)
    p0_A(0)
    for t in range(32):
        if t + 2 < 32:
            p0_L(t + 2)
        if t + 1 < 32:
            p0_A(t + 1)
        p0_B(t)
    cur[0] = base1
    if stop_after == 0:
        S.barrier()
        dbg = alloc([1024], F32)
        S.op("dve", lambda e: e.tensor_copy(out=dbg, in_=xnT[:, 0, 0:1024]), r=[("xnT", i) for i in range(8)], w=["dbg"])
        S.dma(lambda e: e.dma_start(out=out_t[0:128, :], in_=dbg), r=["dbg"])
        S.emit()
        return nc

    S.barrier()
    NW = 6
    wsl = [alloc([8, 128], BF16) for _ in range(NW)]
    wcnt = [0]

    def load_w(j):
        k = wcnt[0] % NW
        wcnt[0] += 1
        key = "wsl%d" % k
        S.dma(lambda e, k=k, j=j: e.dma_start(out=wsl[k], in_=wf_in[j], max_dma_last_dim=4096), w=[key], q="pool")
        return wsl[k], key

    t1b = [alloc([512], F32) for _ in range(2)]
    t2b = [alloc([512], F32) for _ in range(2)]
    ob = [alloc([512], BF16) for _ in range(4)]
    ocnt = [0]
    bcnt = [0]

    def nb():
        b = bcnt[0] % 8
        bcnt[0] += 1
        return b

    def proj_fm(wt, wkey, tb, bank, M=128, woff=0):
        for c in range(8):
            S.op("pe", lambda e, c=c: e.matmul(B(bank)[0:M, :], lhsT=wt[:, c, woff:woff + M], rhs=xnT[:, c, tb * 512:(tb + 1) * 512],
                                               start=(c == 0), stop=(c == 7)),
                 r=[wkey, ("xnT", tb)], x=[BK(bank)])

    def rope_out(bankA, bankB, tb, M, dst):
        i = ocnt[0]
        ocnt[0] += 1
        t1 = t1b[i % 2]; t2 = t2b[i % 2]; o = ob[i % 4]
        k1 = "t1_%d" % (i % 2); k2 = "t2_%d" % (i % 2); ko = "ob%d" % (i % 4)
        S.op("dve", lambda e: e.tensor_tensor(out=t1[0:M, :], in0=B(bankA)[0:M, :], in1=ctab[0:M, tb * 512:(tb + 1) * 512], op=ALU.mult),
             r=["ctab"], w=[k1], x=[BK(bankA)])
        S.op("dve", lambda e: e.tensor_tensor(out=t2[0:M, :], in0=B(bankB)[0:M, :], in1=stab[0:M, tb * 512:(tb + 1) * 512], op=ALU.mult),
             r=["stab"], w=[k2], x=[BK(bankB)])
        S.op("pool", lambda e: e.tensor_tensor(out=o[0:M, :], in0=t1[0:M, :], in1=t2[0:M, :], op=ALU.add), r=[k1, k2], w=[ko])
        S.dma(lambda e: e.dma_start(out=dst, in_=o[0:M, :]), r=[ko])

    def simple_out(bank, dst, kind, M=128, bias=None):
        i = ocnt[0]
        ocnt[0] += 1
        o = ob[i % 4]; ko = "ob%d" % (i % 4)
        if kind == "silu":
            S.op("act", lambda e: e.activation(out=o[0:M, :], in_=B(bank)[0:M, :], func=AF.Silu), w=[ko], x=[BK(bank)])
        elif kind == "sig":
            S.op("act", lambda e: e.activation(out=o[0:M, :], in_=B(bank)[0:M, :], func=AF.Sigmoid, bias=bias), r=["sm"], w=[ko], x=[BK(bank)])
        elif kind == "copy_act":
            S.op("act", lambda e: e.activation(out=o[0:M, :], in_=B(bank)[0:M, :], func=AF.Copy), w=[ko], x=[BK(bank)])
        else:
            S.op("dve", lambda e: e.tensor_copy(out=o[0:M, :], in_=B(bank)[0:M, :]), w=[ko], x=[BK(bank)])
        S.dma(lambda e: e.dma_start(out=dst, in_=o[0:M, :]), r=[ko])

    tasks = []

    def t_rope(ja, jr, ntb, dstf):
        def ld():
            return [load_w(ja), load_w(jr)]
        def cp_(ws):
            (wa, ka_), (wr, kr_) = ws
            for tb in range(ntb):
                bA = nb(); bB = nb()
                proj_fm(wa, ka_, tb, bA); proj_fm(wr, kr_, tb, bB)
                rope_out(bA, bB, tb, 128, dstf(tb))
        tasks.append((ld, cp_))

    def t_simple(j, dstf, kind, bias=None):
        def ld():
            return [load_w(j)]
        def cp_(ws):
            wa, ka_ = ws[0]
            for tb in range(4):
                bA = nb(); proj_fm(wa, ka_, tb, bA)
                simple_out(bA, dstf(tb), kind, bias=bias)
        tasks.append((ld, cp_))

    for h in range(8):
        t_rope(h, 8 + h, 4, lambda tb, h=h: Qa[h, :, tb * 512:(tb + 1) * 512])
        t_rope(16 + h, 24 + h, 8, lambda tb, h=h: Ka[h, :, tb * 512:(tb + 1) * 512])
    for h in range(8):
        t_simple(32 + h, lambda tb, h=h: Za[h, :, tb * 512:(tb + 1) * 512], "silu")
        t_simple(40 + h, lambda tb, h=h: Zb[h, :, tb * 512:(tb + 1) * 512], "silu")
    for j in range(16):
        t_simple(48 + j, lambda tb, j=j: Sg[j, :, tb * 512:(tb + 1) * 512], "sig", bias=sm[:, 16 + j:17 + j])
    hnd = {}
    for i in range(min(2, len(tasks))):
        hnd[i] = tasks[i][0]()
    for i in range(len(tasks)):
        if i + 2 < len(tasks):
            hnd[i + 2] = tasks[i + 2][0]()
        tasks[i][1](hnd.pop(i))
    wvs2 = [alloc([8, 512], BF16) for _ in range(2)]
    cp = [0]
    for g in range(2):
        S.dma(lambda e, g=g: e.dma_start(out=wvs2[g], in_=wv_in[g], max_dma_last_dim=4096), w=["wvs%d" % g], q="pool")
    for g in range(2):
        wvs = wvs2[g]
        for tt in range(32):
            bA = nb()
            for c in range(8):
                S.op("pe", lambda e, c=c, tt=tt, bA=bA, wvs=wvs: e.matmul(B(bA), lhsT=xnT[:, c, tt * 128:(tt + 1) * 128], rhs=wvs[:, c, :], start=(c == 0), stop=(c == 7)),
                     r=["wvs%d" % g, ("xnT", tt // 4)], x=[BK(bA)])
            cp[0] += 1
            simple_out(bA, Va[tt * 128:(tt + 1) * 128, g * 512:(g + 1) * 512], "copy_act" if cp[0] % 2 else "copy")
    cqn = alloc([3, NO], BF16)
    ckvn = alloc([2, NT], BF16)
    latf = alloc([3, 512], F32)
    sqb = alloc([3, 512], BF16)
    lnv = alloc([512], F32)
    rstd = alloc([512], F32)

    def latent(jlist, ntb, dstn, gcol, nfeat):
        ws = [load_w(j) for j in jlist]
        n = len(jlist)
        for tb in range(ntb):
            banks = []
            for q in range(n):
                bA = nb(); banks.append(bA)
                proj_fm(ws[q][0], ws[q][1], tb, bA)
            for q in range(n):
                S.op("dve", lambda e, q=q, bA=banks[q]: e.tensor_copy(out=latf[:, q, :], in_=B(bA)), w=[("latf", q)], x=[BK(banks[q])])
                S.op("pool", lambda e, q=q: e.tensor_tensor(out=sqb[:, q, :], in0=latf[:, q, :], in1=latf[:, q, :], op=ALU.mult), r=[("latf", q)], w=[("sqb", q)])
            bM = nb()
            for q in range(n):
                S.op("pe", lambda e, q=q, bM=bM: e.matmul(B(bM), lhsT=onesb, rhs=sqb[:, q, :], start=(q == 0), stop=(q == n - 1)),
                     r=["onesb", ("sqb", q)], x=[BK(bM)])
            S.op("act", lambda e, bM=bM: e.activation(out=lnv, in_=B(bM), func=AF.Ln, scale=1.0 / nfeat, bias=sm[:, 41:42]), r=["sm"], w=["lnv"], x=[BK(bM)])
            S.op("act", lambda e: e.activation(out=rstd, in_=lnv, func=AF.Exp, scale=-0.5), r=["lnv"], w=["rstd"])
            for q in range(n):
                S.op("dve", lambda e, q=q, tb=tb: e.scalar_tensor_tensor(out=dstn[:, q, tb * 512:(tb + 1) * 512], in0=latf[:, q, :], scalar=sm[:, gcol + q:gcol + q + 1],
                                                                       in1=rstd, op0=ALU.mult, op1=ALU.mult),
                     r=[("latf", q), "rstd", "sm"], w=[("lat", id(dstn), q, tb)])

    latent([64, 65, 66], 4, cqn, 32, 384.0)
    latent([67, 68], 8, ckvn, 35, 256.0)
    LATK = [("lat", id(cqn), q, tb) for q in range(3) for tb in range(4)]
    LATKV = [("lat", id(ckvn), q, tb) for q in range(2) for tb in range(8)]
    wa, ka_ = load_w(69)
    for tb in range(8):
        bA = nb(); bB = nb()
        proj_fm(wa, ka_, tb, bA, M=64, woff=0); proj_fm(wa, ka_, tb, bB, M=64, woff=64)
        rope_out(bA, bB, tb, 64, Kp[:, tb * 512:(tb + 1) * 512])
    wqn = [alloc([3, 128], BF16) for _ in range(2)]
    wqp = [alloc([3, 128], BF16) for _ in range(2)]
    wkk = [alloc([2, 128], BF16) for _ in range(2)]
    def ld_up(h):
        s2 = h % 2
        S.dma(lambda e, h=h, s2=s2: e.dma_start(out=wqn[s2], in_=wuqn_in[h], max_dma_last_dim=4096), w=["wqn%d" % s2], q="pool")
        S.dma(lambda e, h=h, s2=s2: e.dma_start(out=wqp[s2], in_=wuqp_in[h], max_dma_last_dim=4096), w=["wqp%d" % s2], q="pool")
        S.dma(lambda e, h=h, s2=s2: e.dma_start(out=wkk[s2], in_=wukk_in[h], max_dma_last_dim=4096), w=["wkk%d" % s2], q="pool")
    wvv2 = [alloc([2, 512], BF16) for _ in range(2)]
    for g in range(2):
        S.dma(lambda e, g=g: e.dma_start(out=wvv2[g], in_=wukv_in[g], max_dma_last_dim=4096), w=["wvv%d" % g], q="pool")
    ld_up(0)
    for h in range(8):
        s2 = h % 2
        if h + 1 < 8:
            ld_up(h + 1)
        for tb in range(4):
            bA = nb()
            for j in range(3):
                S.op("pe", lambda e, j=j, bA=bA, tb=tb, s2=s2: e.matmul(B(bA), lhsT=wqn[s2][:, j, :], rhs=cqn[:, j, tb * 512:(tb + 1) * 512], start=(j == 0), stop=(j == 2)),
                     r=["wqn%d" % s2] + LATK, x=[BK(bA)])
            simple_out(bA, Qn[h, :, tb * 512:(tb + 1) * 512], "copy")
            bA = nb(); bB = nb()
            for j in range(3):
                S.op("pe", lambda e, j=j, bA=bA, tb=tb, s2=s2: e.matmul(B(bA)[0:64, :], lhsT=wqp[s2][:, j, 0:64], rhs=cqn[:, j, tb * 512:(tb + 1) * 512], start=(j == 0), stop=(j == 2)),
                     r=["wqp%d" % s2] + LATK, x=[BK(bA)])
            for j in range(3):
                S.op("pe", lambda e, j=j, bB=bB, tb=tb, s2=s2: e.matmul(B(bB)[0:64, :], lhsT=wqp[s2][:, j, 64:128], rhs=cqn[:, j, tb * 512:(tb + 1) * 512], start=(j == 0), stop=(j == 2)),
                     r=["wqp%d" % s2] + LATK, x=[BK(bB)])
            rope_out(bA, bB, tb, 64, Qp[h, :, tb * 512:(tb + 1) * 512])
        for tb in range(8):
            bA = nb()
            for j in range(2):
                S.op("pe", lambda e, j=j, bA=bA, tb=tb, s2=s2: e.matmul(B(bA), lhsT=wkk[s2][:, j, :], rhs=ckvn[:, j, tb * 512:(tb + 1) * 512], start=(j == 0), stop=(j == 1)),
                     r=["wkk%d" % s2] + LATKV, x=[BK(bA)])
            simple_out(bA, Kn[h, :, tb * 512:(tb + 1) * 512], "copy_act")
    for g in range(2):
        wvv = wvv2[g]
        for tt in range(32):
            bA = nb()
            for j in range(2):
                S.op("pe", lambda e, j=j, tt=tt, bA=bA, wvv=wvv: e.matmul(B(bA), lhsT=ckvn[:, j, tt * 128:(tt + 1) * 128], rhs=wvv[:, j, :], start=(j == 0), stop=(j == 1)),
                     r=["wvv%d" % g] + LATKV, x=[BK(bA)])
            cp[0] += 1
            simple_out(bA, Vb[tt * 128:(tt + 1) * 128, g * 512:(g + 1) * 512], "copy_act" if cp[0] % 2 else "copy")

    S.barrier()
    cur[0] = base0
    NB2 = 2
    qT = [alloc([NO], BF16) for _ in range(NB2)]
    kT = [alloc([NT], BF16) for _ in range(NB2)]
    vS = [alloc([32, 128], BF16) for _ in range(NB2)]
    zT = [alloc([NO], BF16) for _ in range(NB2)]
    qpT = [alloc([NO], BF16) for _ in range(NB2)]
    kpT = alloc([NT], BF16)
    NP = 4
    P1 = [alloc([512], BF16) for _ in range(NP)]
    P2 = [alloc([512], BF16) for _ in range(NP)]
    fr1 = alloc([512], F32); fo1 = alloc([512], F32); fr2 = alloc([512], F32); ft2 = alloc([512], F32)
    fo = alloc([512], F32); fsq = alloc([512], BF16); fln = alloc([512], F32); frs = alloc([512], F32); fu = alloc([512], F32)
    gob = [alloc([512], BF16) for _ in range(2)]
    gcnt = [0]
    cur[0] = max(cur[0], 98048)
    base2 = cur[0]
    wfin = [alloc([8, 1024], BF16) for _ in range(4)]
    wpp = alloc([2, 1024], BF16)
    pTs = alloc([2, NO], BF16)
    rows = alloc([5, 1024], F32)

    def prefetch_p3():
        for i in range(4):
            for hh in range(2):
                S.dma(lambda e, i=i, hh=hh: e.dma_start(out=wfin[i][:, 4 * hh:4 * hh + 4, :], in_=wfin_in[i, :, 4 * hh:4 * hh + 4, :], max_dma_last_dim=4096), w=[("wfin", i, hh)], q="pool")
        S.dma(lambda e: e.dma_start(out=wpp, in_=wpp_in[:, :, :], max_dma_last_dim=4096), w=["wpp"], q="pool")
        S.dma(lambda e: e.dma_start(out=pTs, in_=pT_in[:, :, :], max_dma_last_dim=4096), w=["pTs"], q="pool")
        for i in range(5):
            S.dma(lambda e, i=i: e.dma_start(out=rows[:, i, :], in_=rows_in[i:i + 1, :].broadcast_to([128, 1024])), w=[("rows", i)], q="pool")

    def ktiles(g):
        lst = []
        for t in range(4 * g):
            lst.append((t, 0, None))
            lst.append((16 + t, 0, None))
        for m in range(4):
            lst.append((4 * g + m, 128 * m, 0))
            lst.append((16 + 4 * g + m, 128 * m, 1))
        return lst

    SC_A = float(64 ** -0.5)
    SC_B = float(192 ** -0.5)
    pcnt = [0]

    def load_head(h, mixer):
        s = h % NB2
        kq = "qT%d" % s; kk = "kT%d" % s; kv = "vS%d" % s; kz = "zT%d" % s; kqp = "qpT%d" % s
        if mixer == 0:
            S.dma(lambda e: e.dma_start(out=qT[s], in_=Qa[h]), w=[kq])
            S.dma(lambda e: e.dma_start(out=kT[s], in_=Ka[h]), w=[kk])
            for qq in range(4):
                S.dma(lambda e, qq=qq: e.dma_start(out=vS[s][:, 8 * qq:8 * qq + 8, :], in_=Va.ap().rearrange("(t p) n -> p t n", p=128)[:, 8 * qq:8 * qq + 8, h * 128:(h + 1) * 128]), w=[kv + "_%d" % qq])
            S.dma(lambda e: e.dma_start(out=zT[s], in_=Za[h]), w=[kz])
        else:
            S.dma(lambda e: e.dma_start(out=qT[s], in_=Qn[h]), w=[kq])
            S.dma(lambda e: e.dma_start(out=kT[s], in_=Kn[h]), w=[kk])
            for qq in range(4):
                S.dma(lambda e, qq=qq: e.dma_start(out=vS[s][:, 8 * qq:8 * qq + 8, :], in_=Vb.ap().rearrange("(t p) n -> p t n", p=128)[:, 8 * qq:8 * qq + 8, h * 128:(h + 1) * 128]), w=[kv + "_%d" % qq])
            S.dma(lambda e: e.dma_start(out=zT[s], in_=Zb[h]), w=[kz])
            S.dma(lambda e: e.dma_start(out=qpT[s][0:64, :], in_=Qp[h]), w=[kqp])

    def attention(h, mixer, nxt=None):
        s = h % NB2
        kq = "qT%d" % s; kk = "kT%d" % s; kv = "vS%d" % s; kz = "zT%d" % s; kqp = "qpT%d" % s
        nmap = 2 if mixer == 0 else 1
        for g in range(4):
            kts = ktiles(g)
            q0 = 512 * g

            def qk(i):
                slot, c0, mk = kts[i]
                for mp in range(nmap):
                    bank = (2 * mp + (i % 2)) if mixer == 0 else (i % 4)
                    nomask = (mk is None)
                    if mixer == 0:
                        rows = slice(64 * mp, 64 * mp + 64)
                        S.op("pe", lambda e, rows=rows, bank=bank, slot=slot, c0=c0, q0=q0, nomask=nomask: e.matmul(B(bank)[:, c0:512], lhsT=kT[s][rows, slot * 128:(slot + 1) * 128],
                                                                                            rhs=qT[s][rows, q0 + c0:q0 + 512], start=True, stop=nomask),
                             r=[kq, kk], x=[BK(bank)])
                    else:
                        S.op("pe", lambda e, bank=bank, slot=slot, c0=c0, q0=q0: e.matmul(B(bank)[:, c0:512], lhsT=kT[s][:, slot * 128:(slot + 1) * 128],
                                                                                 rhs=qT[s][:, q0 + c0:q0 + 512], start=True, stop=False),
                             r=[kq, kk], x=[BK(bank)])
                        S.op("pe", lambda e, bank=bank, slot=slot, c0=c0, q0=q0, nomask=nomask: e.matmul(B(bank)[:, c0:512], lhsT=kpT[:, slot * 128:(slot + 1) * 128],
                                                                                 rhs=qpT[s][:, q0 + c0:q0 + 512], start=False, stop=nomask),
                             r=[kqp, "kpT", "kpz", "qpz%d" % s], x=[BK(bank)])
                    if mk is not None:
                        S.op("pe", lambda e, bank=bank, c0=c0, mk=mk: e.matmul(B(bank)[:, c0:c0 + 128], lhsT=identb, rhs=masks[:, mk * 128:(mk + 1) * 128],
                                                                             start=False, stop=True),
                             r=["identb", "masks"], x=[BK(bank)])

            def expo(i):
                slot, c0, mk = kts[i]
                res = []
                for mp in range(nmap):
                    bank = (2 * mp + (i % 2)) if mixer == 0 else (i % 4)
                    pc = pcnt[0] % NP
                    Pb = (P1 if mp == 0 else P2)[pc]
                    pk = "P%d_%d" % (mp, pc)
                    S.op("act", lambda e, bank=bank, Pb=Pb, c0=c0: e.activation(out=Pb[:, c0:512], in_=B(bank)[:, c0:512], func=AF.Exp,
                                                                              scale=(SC_A if mixer == 0 else SC_B)),
                         w=[pk], x=[BK(bank)])
                    res.append((Pb, pk))
                pcnt[0] += 1
                return res

            def pv(i, pbs):
                slot, c0, mk = kts[i]
                first = (i == 0); last = (i == len(kts) - 1)
                for mp in range(nmap):
                    Pb, pk = pbs[mp]
                    bo = 4 + mp; bl = 6 + mp
                    S.op("pe", lambda e, Pb=Pb, bo=bo, slot=slot, c0=c0: e.matmul(B(bo)[:, c0:512], lhsT=vS[s][:, slot, :], rhs=Pb[:, c0:512], start=first, stop=last),
                         r=[pk] + [kv + "_%d" % qq for qq in range(4)], x=[BK(bo)])
                    S.op("pe", lambda e, Pb=Pb, bl=bl, c0=c0: e.matmul(B(bl)[:, c0:512], lhsT=onesb, rhs=Pb[:, c0:512], start=first, stop=last),
                         r=[pk, "onesb"], x=[BK(bl)])

            LA = 2
            for i0 in range(LA):
                qk(i0)
            for i in range(len(kts)):
                pbs = expo(i)
                if i == 5 and pend2[0] is not None:
                    pend2[0](i)
                    pend2[0] = None
                if i + LA < len(kts):
                    qk(i + LA)
                pv(i, pbs)
                if i == 1 and pend[0] is not None:
                    pend[0]()
                    pend[0] = None
                if i == 6 and pend3[0] is not None:
                    pend3[0]()
                    pend3[0] = None
                if i == 7 and g == 0 and nxt is not None:
                    load_head(*nxt)
            if pend[0] is not None:
                pend[0]()
                pend[0] = None
            if pend2[0] is not None:
                pend2[0](len(kts) - 1)
                pend2[0] = None
            if pend3[0] is not None:
                pend3[0]()
                pend3[0] = None
            gi = gcnt[0] % 2
            gcnt[0] += 1
            go = gob[gi]; gk = "gob%d" % gi
            if mixer == 0:
                S.op("dve", lambda e: e.tensor_copy(out=fr1, in_=B(6)), w=["fr1c"], x=[BK(6)])
                S.op("dve", lambda e: e.tensor_copy(out=fo1, in_=B(4)), w=["fo1c"], x=[BK(4)])
                S.op("dve", lambda e: e.tensor_copy(out=fr2, in_=B(7)), w=["fr2c"], x=[BK(7)])
                S.op("dve", lambda e: e.tensor_copy(out=ft2, in_=B(5)), w=["ft2c"], x=[BK(5)])

                def partB1():
                    S.op("dve", lambda e: e.tensor_tensor(out=fo1, in0=fo1, in1=fr2, op=ALU.mult), r=["fo1c", "fr2c"], w=["fo1"])
                    S.op("dve", lambda e: e.tensor_tensor(out=ft2, in0=ft2, in1=fr1, op=ALU.mult), r=["ft2c", "fr1c"], w=["ft2"])
                    S.op("dve", lambda e: e.scalar_tensor_tensor(out=fo, in0=ft2, scalar=neglam[:, 1:2], in1=fo1, op0=ALU.mult, op1=ALU.add),
                         r=["ft2", "fo1", "neglam"], w=["fo"])
                    S.op("dve", lambda e: e.tensor_tensor(out=fr1, in0=fr1, in1=fr2, op=ALU.mult), r=["fr1c", "fr2c", "ft2"], w=["fcc"])
                    S.op("dve", lambda e: e.scalar_tensor_tensor(out=fr2, in0=fr1, scalar=1e-6, in1=fr1, op0=ALU.mult, op1=ALU.mult),
                         r=["fcc", "fo1"], w=["fce"])
                    S.op("pool", lambda e: e.tensor_tensor(out=fsq, in0=fo, in1=fo, op=ALU.mult), r=["fo"], w=["fsq"])

                def partB2(i_):
                    bk_ = i_ % 2
                    S.op("pe", lambda e: e.matmul(B(bk_), lhsT=onesb, rhs=fsq, start=True, stop=True), r=["fsq", "onesb"], x=[BK(bk_)])
                    S.op("dve", lambda e: e.scalar_tensor_tensor(out=fln, in0=B(bk_), scalar=1.0 / 128.0, in1=fr2, op0=ALU.mult, op1=ALU.add),
                         r=["fce"], w=["flt"], x=[BK(bk_)])

                def partB3(go=go, gk=gk, q0=q0, h=h, s=s, kz=kz):
                    S.op("act", lambda e: e.activation(out=fln, in_=fln, func=AF.Ln), r=["flt"], w=["fln"])
                    S.op("act", lambda e: e.activation(out=frs, in_=fln, func=AF.Exp, scale=-0.5), r=["fln"], w=["frs"])
                    S.op("dve", lambda e: e.scalar_tensor_tensor(out=fu, in0=fo, scalar=gsub, in1=frs, op0=ALU.mult, op1=ALU.mult),
                         r=["fo", "frs", "gsub"], w=["fu"])
                    S.op("pool", lambda e: e.tensor_tensor(out=go, in0=fu, in1=zT[s][:, q0:q0 + 512], op=ALU.mult), r=["fu", kz], w=[gk])
                    S.dma(lambda e: e.dma_start(out=Ga[h, :, q0:q0 + 512], in_=go), r=[gk])
                pend3[0] = partB3
                pend[0] = partB1
                pend2[0] = partB2
            else:
                S.op("dve", lambda e: e.tensor_copy(out=fr1, in_=B(6)), w=["fr1c"], x=[BK(6)])
                S.op("dve", lambda e: e.tensor_copy(out=fo1, in_=B(4)), w=["fo1c"], x=[BK(4)])

                def partBm(i_, go=go, gk=gk, q0=q0, h=h, s=s, kz=kz):
                    S.op("dve", lambda e: e.reciprocal(out=fr1, in_=fr1), r=["fr1c"], w=["fr1"])
                    S.op("dve", lambda e: e.tensor_tensor(out=fo1, in0=fo1, in1=fr1, op=ALU.mult), r=["fr1", "fo1c"], w=["fo1"])
                    S.op("pool", lambda e: e.tensor_tensor(out=go, in0=fo1, in1=zT[s][:, q0:q0 + 512], op=ALU.mult), r=["fo1", kz], w=[gk])
                    S.dma(lambda e: e.dma_start(out=Gb[h, :, q0:q0 + 512], in_=go), r=[gk])
                pend2[0] = partBm

    for s_ in range(NB2):
        S.op("pool", lambda e, s_=s_: e.memset(qpT[s_][64:128, :], 0.0), w=["qpz%d" % s_])
    S.op("pool", lambda e: e.memset(kpT[64:128, :], 0.0), w=["kpz"])
    pend = [None]
    pend2 = [None]
    pend3 = [None]
    if stop_after >= 2:
        items = [(h, 0) for h in range(8)] + [(h, 1) for h in range(8)]
        S.dma(lambda e: e.dma_start(out=kpT[0:64, :], in_=Kp[:, :]), w=["kpT"])
        load_head(*items[0])
        prefetch_p3()
        for i, it in enumerate(items):
            attention(it[0], it[1], items[i + 1] if i + 1 < len(items) else None)
        if pend[0] is not None:
            pend[0]()
            pend[0] = None
        if pend2[0] is not None:
            pend2[0](1)
            pend2[0] = None
        if pend3[0] is not None:
            pend3[0]()
            pend3[0] = None

    S.barrier()
    cur[0] = base0
    if stop_after < 2:
        prefetch_p3()
    WF = lambda i: [("wfin", i, 0), ("wfin", i, 1)]
    gaS = alloc([8, 512], BF16); gbS = alloc([8, 512], BF16); sgS = alloc([16, 512], BF16)
    mT = alloc([8, 512], BF16)
    ft = alloc([512], F32); fu2 = alloc([512], F32)
    xr2 = [alloc([1024], F32) for _ in range(2)]; xnr2 = xr2
    yy2 = [alloc([1024], F32) for _ in range(2)]; ybf2 = [alloc([1024], BF16) for _ in range(2)]
    yT2 = [alloc([8, 128], BF16) for _ in range(2)]
    sgi2 = [alloc([1024], F32) for _ in range(2)]; y22 = [alloc([1024], F32) for _ in range(2)]
    st32 = [alloc([12], F32) for _ in range(2)]; mv32 = [alloc([4], F32) for _ in range(2)]
    oo = [alloc([1024], F32) for _ in range(2)]
    MT = [("mT", dc) for dc in range(8)]
    assert cur[0] <= base2, ("phase-3 work buffers overlap prefetched weights", cur[0], base2)

    def blk_load(blk):
        c0 = blk * 512
        S.dma(lambda e, c0=c0: e.dma_start(out=gaS, in_=Ga.ap().rearrange("h p t -> p h t")[:, :, c0:c0 + 512]), w=["gaS"])
        S.dma(lambda e, c0=c0: e.dma_start(out=gbS, in_=Gb.ap().rearrange("h p t -> p h t")[:, :, c0:c0 + 512]), w=["gbS"])
        for hh2 in range(2):
            S.dma(lambda e, c0=c0, hh2=hh2: e.dma_start(out=sgS[:, 8 * hh2:8 * hh2 + 8, :], in_=Sg.ap().rearrange("h p t -> p h t")[:, 8 * hh2:8 * hh2 + 8, c0:c0 + 512]), w=["sgS%d" % hh2])

    def blk_prep(blk):
        for dc in range(8):
            bA = 7; bB = 6
            for hh in range(8):
                S.op("pe", lambda e, hh=hh, dc=dc, bA=bA: e.matmul(B(bA), lhsT=wfin[0][:, hh, dc * 128:(dc + 1) * 128], rhs=gaS[:, hh, :], start=(hh == 0), stop=(hh == 7)),
                     r=WF(0) + ["gaS"], x=[BK(bA)])
            for hh in range(8):
                S.op("pe", lambda e, hh=hh, dc=dc, bB=bB: e.matmul(B(bB), lhsT=wfin[1][:, hh, dc * 128:(dc + 1) * 128], rhs=gbS[:, hh, :], start=(hh == 0), stop=(hh == 7)),
                     r=WF(1) + ["gbS"], x=[BK(bB)])
            S.op("dve", lambda e, dc=dc, bA=bA: e.tensor_tensor(out=ft, in0=B(bA), in1=sgS[:, dc, :], op=ALU.mult), r=["sgS0"], w=["ft"], x=[BK(bA)])
            S.op("dve", lambda e, dc=dc, bB=bB: e.tensor_tensor(out=fu2, in0=B(bB), in1=sgS[:, 8 + dc, :], op=ALU.mult), r=["sgS1"], w=["fu2"], x=[BK(bB)])
            S.op("pool", lambda e, dc=dc: e.tensor_tensor(out=mT[:, dc, :], in0=ft, in1=fu2, op=ALU.add), r=["ft", "fu2"], w=[("mT", dc)])

    def bufs(t):
        u = t % 2
        return dict(u=u, xr=xr2[u], xnr=xnr2[u], yy=yy2[u], ybf=ybf2[u], yT=yT2[u], sgi=sgi2[u], y2=y22[u], st3=st32[u], mv3=mv32[u], o_=oo[u])

    def st_X(t):
        d_ = bufs(t); u = d_["u"]
        xr = d_["xr"]
        K = lambda n: "%s_%d" % (n, u)
        S.dma(lambda e, t=t, xr=xr: e.dma_start(out=xr, in_=x_in[t * 128:(t + 1) * 128, :]), w=[K("xr"), K("xnr")])
        S.op("dve", lambda e, t=t, xr=xr: e.scalar_tensor_tensor(out=xr, in0=xr, scalar=stats[:, t, 0:1], in1=rows[:, 0, :], op0=ALU.subtract, op1=ALU.mult),
             r=[K("xr"), ("rows", 0)], w=[K("xnr0")])
        S.op("dve", lambda e, t=t, xr=xr: e.scalar_tensor_tensor(out=xr, in0=xr, scalar=stats[:, t, 3:4], in1=rows[:, 1, :], op0=ALU.mult, op1=ALU.add),
             r=[K("xnr0"), ("rows", 1)], w=[K("xnr")])

    def st_A(t):
        d_ = bufs(t); u = d_["u"]
        xr = d_["xr"]; xnr = d_["xnr"]; yy = d_["yy"]; ybf = d_["ybf"]; yT = d_["yT"]
        K = lambda n: "%s_%d" % (n, u)
        tc0 = (t % 4) * 128
        bt = 6
        for half in range(2):
            for dc in range(8):
                S.op("pe", lambda e, dc=dc, half=half, tc0=tc0: e.matmul(B(4 + half), lhsT=mT[:, dc, tc0:tc0 + 128], rhs=wfin[2][:, dc, half * 512:(half + 1) * 512],
                                                                       start=(dc == 0), stop=(dc == 7)),
                     r=WF(2) + MT, x=[BK(4 + half)])
        for half in range(2):
            S.op("dve", lambda e, half=half, yy=yy, xnr=xnr: e.scalar_tensor_tensor(out=yy[:, half * 512:(half + 1) * 512], in0=xnr[:, half * 512:(half + 1) * 512], scalar=ALPHA,
                                                                  in1=B(4 + half), op0=ALU.mult, op1=ALU.add),
                 r=[K("xnr")], w=[K("yy%d" % half)], x=[BK(4 + half)])
        S.op("act", lambda e, ybf=ybf, yy=yy: e.activation(out=ybf, in_=yy, func=AF.Copy), r=[K("yy0"), K("yy1")], w=[K("ybf")])

    def st_A2(t):
        d_ = bufs(t); u = d_["u"]
        ybf = d_["ybf"]; yT = d_["yT"]
        K = lambda n: "%s_%d" % (n, u)
        bt = 6
        for dc in range(8):
            S.op("pe", lambda e, dc=dc, ybf=ybf, bt=bt: e.transpose(out=pst16[:, bt, dc * 128:(dc + 1) * 128], in_=ybf[:, dc * 128:(dc + 1) * 128], identity=identb),
                 r=[K("ybf"), "identb"], x=[BK(bt)])
        S.op("dve", lambda e, yT=yT, bt=bt: e.tensor_copy(out=yT, in_=pst16[:, bt, :].rearrange("p (a b) -> p a b", a=8)), w=[K("yT")], x=[BK(bt)])

    def st_B(t):
        d_ = bufs(t); u = d_["u"]
        yy = d_["yy"]; yT = d_["yT"]; sgi = d_["sgi"]; y2 = d_["y2"]
        K = lambda n: "%s_%d" % (n, u)
        for half in range(2):
            for dc in range(8):
                S.op("pe", lambda e, dc=dc, half=half, yT=yT: e.matmul(B(half), lhsT=yT[:, dc, :], rhs=wfin[3][:, dc, half * 512:(half + 1) * 512], start=(dc == 0), stop=(dc == 7)),
                     r=WF(3) + [K("yT")], x=[BK(half)])
            for j in range(2):
                S.op("pe", lambda e, j=j, half=half, t=t: e.matmul(B(2 + half), lhsT=pTs[:, j, t * 128:(t + 1) * 128], rhs=wpp[:, j, half * 512:(half + 1) * 512], start=(j == 0), stop=(j == 1)),
                     r=["pTs", "wpp"], x=[BK(2 + half)])
        for half in range(2):
            hs = slice(half * 512, (half + 1) * 512)
            S.op("dve", lambda e, half=half, hs=hs, sgi=sgi: e.tensor_tensor(out=sgi[:, hs], in0=B(half), in1=rows[:, 2, hs], op=ALU.add), r=[("rows", 2)], w=[K("sgi%d" % half)], x=[BK(half)])
            S.op("act", lambda e, hs=hs, sgi=sgi: e.activation(out=sgi[:, hs], in_=sgi[:, hs], func=AF.Sigmoid), r=[K("sgi%d" % half)], w=[K("sgo%d" % half)])
            S.op("dve", lambda e, half=half, hs=hs, sgi=sgi, y2=y2: e.tensor_tensor(out=y2[:, hs], in0=B(2 + half), in1=sgi[:, hs], op=ALU.mult), r=[K("sgo%d" % half)], w=[K("y2a%d" % half)], x=[BK(2 + half)])
            S.op("pool", lambda e, hs=hs, y2=y2, yy=yy: e.tensor_tensor(out=y2[:, hs], in0=y2[:, hs], in1=yy[:, hs], op=ALU.add), r=[K("y2a%d" % half), K("yy%d" % half)], w=[K("y2%d" % half)])

    def st_C(t):
        d_ = bufs(t); u = d_["u"]
        y2 = d_["y2"]; st3 = d_["st3"]; mv3 = d_["mv3"]; o_ = d_["o_"]
        K = lambda n: "%s_%d" % (n, u)
        ok_ = "oo%d" % u
        S.op("dve", lambda e, st3=st3, y2=y2: e.bn_stats(out=st3[:, 0:6], in_=y2[:, 0:512]), r=[K("y20")], w=[K("st3a")])
        S.op("dve", lambda e, st3=st3, y2=y2: e.bn_stats(out=st3[:, 6:12], in_=y2[:, 512:1024]), r=[K("y21")], w=[K("st3b")])
        S.op("dve", lambda e, st3=st3, mv3=mv3: e.bn_aggr(out=mv3[:, 0:2], in_=st3), r=[K("st3a"), K("st3b")], w=[K("mv3")])
        S.op("act", lambda e, mv3=mv3: e.activation(out=mv3[:, 2:3], in_=mv3[:, 1:2], func=AF.Ln, bias=sm[:, 40:41], scale=1.0), r=[K("mv3"), "sm"], w=[K("lv3")])
        S.op("act", lambda e, mv3=mv3: e.activation(out=mv3[:, 3:4], in_=mv3[:, 2:3], func=AF.Exp, scale=-0.5), r=[K("lv3")], w=[K("rs3")])
        S.op("dve", lambda e, o_=o_, y2=y2, mv3=mv3: e.scalar_tensor_tensor(out=o_, in0=y2, scalar=mv3[:, 0:1], in1=rows[:, 3, :], op0=ALU.subtract, op1=ALU.mult),
             r=[K("y20"), K("y21"), K("mv3"), ("rows", 3)], w=[ok_ + "a"])
        S.op("dve", lambda e, o_=o_, mv3=mv3: e.scalar_tensor_tensor(out=o_, in0=o_, scalar=mv3[:, 3:4], in1=rows[:, 4, :], op0=ALU.mult, op1=ALU.add),
             r=[ok_ + "a", K("rs3"), ("rows", 4)], w=[ok_])
        S.dma(lambda e, o_=o_, t=t: e.dma_start(out=out_t[t * 128:(t + 1) * 128, :], in_=o_), r=[ok_])

    blk_load(0)
    st_X(0)
    for it in range(16 + 2):
        if 0 <= it - 2 < 16:
            st_C(it - 2)
        if it < 16:
            if it % 4 == 0:
                blk_prep(it // 4)
                if it // 4 + 1 < 4:
                    blk_load(it // 4 + 1)
            st_A(it)
            if it + 1 < 16:
                st_X(it + 1)
        if 0 <= it - 1 < 16:
            st_B(it - 1)
        if it < 16:
            st_A2(it)
    S.emit()
    return nc


def _prep_shared(inp):
    f = lambda k: np.asarray(inp[k], dtype=np.float32)
    W = f("w_in")[0]
    def chunk(cols):
        return W[:, cols].reshape(8, 128, len(cols)).transpose(1, 0, 2)
    def rot128(base):
        n = np.arange(128); m = n // 64; i = n % 64
        return base + 64 * m + np.where(i < 32, i + 32, i - 32)
    wf = []
    for h in range(8): wf.append(chunk(np.arange(128 * h, 128 * h + 128)))
    for h in range(8): wf.append(chunk(rot128(128 * h)))
    for h in range(8): wf.append(chunk(np.arange(1024 + 128 * h, 1024 + 128 * h + 128)))
    for h in range(8): wf.append(chunk(rot128(1024 + 128 * h)))
    for h in range(8): wf.append(chunk(np.arange(3072 + 128 * h, 3072 + 128 * h + 128)))
    for h in range(8): wf.append(chunk(np.arange(4800 + 128 * h, 4800 + 128 * h + 128)))
    for j in range(16): wf.append(chunk(np.arange(5824 + 128 * j, 5824 + 128 * j + 128)))
    for j in range(3): wf.append(chunk(np.arange(4096 + 128 * j, 4096 + 128 * j + 128)))
    for j in range(2): wf.append(chunk(np.arange(4480 + 128 * j, 4480 + 128 * j + 128)))
    i64 = np.arange(64)
    r64 = np.where(i64 < 32, i64 + 32, i64 - 32)
    wf.append(chunk(np.concatenate([4736 + i64, 4736 + r64])))
    wf = np.ascontiguousarray(np.stack(wf), dtype=np.float32)
    wv = np.ascontiguousarray(np.stack([W[:, 2048 + g * 512: 2048 + (g + 1) * 512].reshape(8, 128, 512).transpose(1, 0, 2) for g in range(2)]), dtype=np.float32)
    uq = f("mla_w_uq")[0]; ukv = f("mla_w_ukv")[0]
    wuqn = np.ascontiguousarray(np.stack([uq[:, h * 192:h * 192 + 128].reshape(3, 128, 128).transpose(1, 0, 2) for h in range(8)]), dtype=np.float32)
    wuqp = np.ascontiguousarray(np.stack([uq[:, np.concatenate([h * 192 + 128 + i64, h * 192 + 128 + r64])].reshape(3, 128, 128).transpose(1, 0, 2) for h in range(8)]), dtype=np.float32)
    wukk = np.ascontiguousarray(np.stack([ukv[:, h * 256:h * 256 + 128].reshape(2, 128, 128).transpose(1, 0, 2) for h in range(8)]), dtype=np.float32)
    wukv = np.ascontiguousarray(np.stack([ukv[:, np.concatenate([h * 256 + 128 + np.arange(128) for h in range(4 * g, 4 * g + 4)])].reshape(2, 128, 512).transpose(1, 0, 2) for g in range(2)]), dtype=np.float32)
    wfin = np.ascontiguousarray(np.stack([f(k)[0].reshape(8, 128, 1024).transpose(1, 0, 2) for k in ("w_o_a", "w_o_b", "w_out", "ple_w_gate")]), dtype=np.float32)
    wpp = np.ascontiguousarray(f("ple_w_proj")[0].reshape(2, 128, 1024).transpose(1, 0, 2), dtype=np.float32)
    sm = np.zeros((128, 64), np.float32)
    sm[:, 0:8] = f("ln_emb_g").reshape(8, 128).T
    sm[:, 8:16] = f("ln_emb_b").reshape(8, 128).T
    sm[:, 16:32] = f("b_gate")[0].reshape(16, 128).T
    sm[:, 32:35] = f("mla_q_norm_g")[0].reshape(3, 128).T
    sm[:, 35:37] = f("mla_kv_norm_g")[0].reshape(2, 128).T
    sm[:, 37] = f("diff_subln_g")[0]
    inv = (np.float32(10000.0) ** (-(np.arange(0, 64, 2, dtype=np.float32)) / np.float32(64))).astype(np.float32)
    pp = np.arange(128)
    sm[:, 38] = inv[pp % 32]
    sm[:, 39] = np.where((pp % 64) < 32, -1.0, 1.0)
    sm[:, 40] = 1e-5
    sm[:, 41] = 1e-6
    rows = np.ascontiguousarray(np.stack([f("ln_emb_g"), f("ln_emb_b"), f("ple_b_gate")[0], f("ln_post_g")[0], f("ln_post_b")[0]]), dtype=np.float32)
    dl = np.ascontiguousarray(f("diff_lambda")[0].reshape(1, 256))
    return dict(wf=wf, wv=wv, wuqn=wuqn, wuqp=wuqp, wukk=wukk, wukv=wukv, wfin=wfin, wpp=wpp, sm=sm, rows=rows, dl=dl,
                ident=np.eye(128, dtype=np.float32))


def _core_maps(inp, shared):
    x = np.asarray(inp["x"], dtype=np.float32)
    p = np.asarray(inp["p"], dtype=np.float32)[0]
    pos = np.asarray(inp["positions"]).astype(np.int32)
    maps = []
    orders = []
    kk = np.arange(128)
    tri = np.where(kk[:, None] <= kk[None, :], 0.0, -30000.0).astype(np.float32)
    for c in range(8):
        b, hf = c // 2, c % 2
        order = [2 * s + hf for s in range(16)] + [2 * u + 1 - hf for u in range(16)]
        idx = np.concatenate([np.arange(t * 128, (t + 1) * 128) for t in order])
        m = dict(shared)
        m["x"] = np.ascontiguousarray(x[b][idx])
        m["pos"] = np.ascontiguousarray(pos[b][idx][None, :])
        m["pT"] = np.ascontiguousarray(p[b][idx[:NO]].T.reshape(2, 128, NO).transpose(1, 0, 2))
        m["masks"] = np.ascontiguousarray(np.concatenate([tri, np.full((128, 128), 0.0 if hf == 1 else -30000.0, np.float32)], axis=1))
        maps.append(m)
        orders.append(order)
    return maps, orders


_NC_CACHE = {}


def kernel(**inp):
    if "nc" not in _NC_CACHE:
        _NC_CACHE["nc"] = build_program()
    nc = _NC_CACHE["nc"]
    shared = _prep_shared(inp)
    maps, orders = _core_maps(inp, shared)
    res = run_bass_kernel_spmd(nc, maps, core_ids=list(range(8)))
    out = np.zeros((4, 4096, 1024), np.float32)
    for c in range(8):
        b = c // 2
        o = np.asarray(res.results[c]["out"], dtype=np.float32)
        for s in range(16):
            t = orders[c][s]
            out[b, t * 128:(t + 1) * 128] = o[s * 128:(s + 1) * 128]
    return out
```

```python
import numpy as np
import contextlib
import concourse.bass as bass
import concourse.mybir as mybir
from concourse.bass_utils import run_bass_kernel_spmd

F32 = mybir.dt.float32
BF16 = mybir.dt.bfloat16
F16 = mybir.dt.float16
I32 = mybir.dt.int32
AF = mybir.ActivationFunctionType
ALU = mybir.AluOpType
AX = mybir.AxisListType

ENGS = ("pe", "act", "dve", "pool", "sp")


class _Op:
    __slots__ = ("idx", "eng", "fn", "deps", "is_dma", "dsem", "dval", "flag", "rank", "predma")

    def __init__(self, idx, eng, fn, is_dma):
        self.idx = idx
        self.eng = eng
        self.fn = fn
        self.deps = set()
        self.is_dma = is_dma
        self.dsem = None
        self.dval = 0
        self.flag = False
        self.rank = 0
        self.predma = None


class Sched:
    def __init__(self, nc, n_dma_sems=40):
        self.nc = nc
        self.ops = []
        self.res = {}
        self.n_dma_sems = n_dma_sems
        self.n_dma = 0
        self.dma_hist = []
        self.bar = None
        self.bar_seen = set()
        self.last_on = {}
        self.dma_since_bar = []

    def _add(self, eng, fn, r, w, is_dma, x=()):
        op = _Op(len(self.ops), eng, fn, is_dma)
        self.ops.append(op)
        ek = ("dma", op.idx) if is_dma else eng
        deps = {}
        for key in x:
            st = self.res.get(key)
            if st is not None:
                for e, i in st[0].items():
                    deps.setdefault(i, False)
                for e, i in st[1].items():
                    deps.setdefault(i, False)
        for key in r:
            st = self.res.get(key)
            if st is not None:
                for e, i in st[0].items():
                    deps[i] = True
        for key in w:
            st = self.res.get(key)
            if st is not None:
                for e, i in st[0].items():
                    deps.setdefault(i, False)
                for e, i in st[1].items():
                    deps.setdefault(i, False)
        for i, raw in deps.items():
            d = self.ops[i]
            if (not d.is_dma) and (not is_dma) and d.eng == eng and not raw:
                continue
            if (not d.is_dma) and (not is_dma) and d.eng == eng and eng == "pe":
                continue
            op.deps.add(i)
        if self.bar is not None and eng not in self.bar_seen:
            self.bar_seen.add(eng)
            for i in self.bar:
                if i != op.idx:
                    d = self.ops[i]
                    if d.is_dma or d.eng != eng:
                        op.deps.add(i)
        for key in r:
            st = self.res.setdefault(key, ({}, {}))
            st[1][ek] = op.idx
        for key in w:
            self.res[key] = ({ek: op.idx}, {})
        for key in x:
            self.res[key] = ({ek: op.idx}, {})
        if is_dma:
            k = self.n_dma % self.n_dma_sems
            op.dsem = k
            op.dval = 16 * (self.n_dma // self.n_dma_sems + 1)
            if self.n_dma >= self.n_dma_sems:
                op.predma = self.dma_hist[self.n_dma - self.n_dma_sems]
            self.dma_hist.append(op.idx)
            self.n_dma += 1
            self.dma_since_bar.append(op.idx)
        else:
            self.last_on[eng] = op.idx
        return op

    def op(self, eng, fn, r=(), w=(), x=()):
        return self._add(eng, fn, r, w, False, x)

    def dma(self, fn, r=(), w=(), q="sp"):
        return self._add(q, fn, r, w, True)

    def barrier(self):
        self.bar = list(self.last_on.values()) + list(self.dma_since_bar)
        self.bar_seen = set()
        self.dma_since_bar = []

    def emit(self, final_waits=True):
        nc = self.nc
        ops = self.ops
        for op in ops:
            for i in op.deps:
                ops[i].flag = True
        cnt = {e: 0 for e in ENGS}
        for op in ops:
            if not op.is_dma and op.flag:
                cnt[op.eng] += 1
                op.rank = cnt[op.eng]
        per = {e: [] for e in ENGS}
        for op in ops:
            per[op.eng].append(op)
        import contextlib

        with contextlib.ExitStack() as st:
            esem = {e: st.enter_context(nc.semaphore("s_" + e)) for e in ENGS if e != "sp"}
            dsem = [st.enter_context(nc.semaphore("d_%d" % i)) for i in range(self.n_dma_sems)]
            block = st.enter_context(nc.Block())
            last_dma_vals = {}
            for op in ops:
                if op.is_dma:
                    last_dma_vals[op.dsem] = op.dval

            def run(eng_name, eng):
                waited = {}

                def wait(sem, key, val):
                    if waited.get(key, 0) >= val:
                        return
                    waited[key] = val
                    eng.wait_ge(sem, val)

                for op in per[eng_name]:
                    if op.predma is not None:
                        p = ops[op.predma]
                        wait(dsem[p.dsem], ("d", p.dsem), p.dval)
                    for i in sorted(op.deps):
                        d = ops[i]
                        if d.is_dma:
                            wait(dsem[d.dsem], ("d", d.dsem), d.dval)
                        else:
                            wait(esem[d.eng], ("e", d.eng), d.rank)
                    ins = op.fn(eng)
                    if op.is_dma:
                        ins.then_inc(dsem[op.dsem], 16)
                    elif op.flag:
                        ins.then_inc(esem[op.eng], 1)
                if eng_name == "sp" and final_waits:
                    for k, v in last_dma_vals.items():
                        wait(dsem[k], ("d", k), v)

            @block.tensor
            def _(e):
                run("pe", e)

            @block.scalar
            def _(e):
                run("act", e)

            @block.vector
            def _(e):
                run("dve", e)

            @block.gpsimd
            def _(e):
                run("pool", e)

            @block.sync
            def _(e):
                run("sp", e)


PI = float(np.pi)
C1 = 6.28125
C2 = float(2.0 * np.pi - 6.28125)
NT = 4096
NO = 2048
LAM_INIT = 0.2
ALPHA = float(2.0 ** 0.25)
LN_EPS_ = 1e-5


def build_program(stop_after=3, debug=False):
    nc = bass.Bass("TRN2", target_bir_lowering=False)
    S = Sched(nc, n_dma_sems=48)

    def din(name, shape, dt=F32):
        return nc.dram_tensor(name, list(shape), dt, kind="ExternalInput")

    x_in = din("x", [NT, 1024])
    pos_in = din("pos", [1, NT], I32)
    pT_in = din("pT", [128, 2, NO])
    wf_in = din("wf", [70, 128, 8, 128])
    wv_in = din("wv", [2, 128, 8, 512])
    wuqn_in = din("wuqn", [8, 128, 3, 128])
    wuqp_in = din("wuqp", [8, 128, 3, 128])
    wukk_in = din("wukk", [8, 128, 2, 128])
    wukv_in = din("wukv", [2, 128, 2, 512])
    wfin_in = din("wfin", [4, 128, 8, 1024])
    wpp_in = din("wpp", [128, 2, 1024])
    sm_in = din("sm", [128, 64])
    rows_in = din("rows", [5, 1024])
    dl_in = din("dl", [1, 256])
    ident_in = din("ident", [128, 128])
    mask_in = din("masks", [128, 256])
    perm_in = din("perm", [128, 128])
    out_t = nc.dram_tensor("out", [NO, 1024], F32, kind="ExternalOutput")

    def scr(name, shape):
        if debug:
            return nc.dram_tensor(name, list(shape), BF16, kind="ExternalOutput")
        return nc.dram_tensor(name, list(shape), BF16)

    Qa = scr("s_qa", [8, 128, NO]); Ka = scr("s_ka", [8, 128, NT]); Va = scr("s_va", [NT, 1024])
    Za = scr("s_za", [8, 128, NO]); Zb = scr("s_zb", [8, 128, NO]); Sg = scr("s_sg", [16, 128, NO])
    Qn = scr("s_qn", [8, 128, NO]); Qp = scr("s_qp", [8, 64, NO]); Kn = scr("s_kn", [8, 128, NT])
    Kp = scr("s_kp", [64, NT]); Vb = scr("s_vb", [NT, 1024])
    Ga = scr("s_ga", [8, 128, NO]); Gb = scr("s_gb", [8, 128, NO])

    ARENA = 206000
    arena = nc.alloc_sbuf_tensor("arena", [128, ARENA // 4], F32)
    views = {F32: arena, BF16: arena.bitcast(BF16), F16: arena.bitcast(F16), I32: arena.bitcast(I32)}
    esz = {F32: 4, BF16: 2, F16: 2, I32: 4}
    cur = [0]

    def alloc(shape, dt):
        n = int(np.prod(shape))
        nb = (n * esz[dt] + 63) // 64 * 64
        off = cur[0]
        cur[0] += nb
        assert cur[0] <= ARENA, ("SBUF arena overflow", cur[0])
        v = views[dt][:, off // esz[dt]: off // esz[dt] + n]
        if len(shape) == 2:
            v = v.rearrange("p (a b) -> p a b", a=shape[0])
        elif len(shape) == 3:
            v = v.rearrange("p (a b c) -> p a b c", a=shape[0], b=shape[1])
        return v

    pst = nc.alloc_psum_tensor("pst", [128, 8, 512], F32)
    pst16 = pst.bitcast(BF16)

    def B(i):
        return pst[:, i, :]

    def BK(i):
        return ("B", i)

    uid = [0]

    def U(p="u"):
        uid[0] += 1
        return "%s%d" % (p, uid[0])

    sm = alloc([64], F32)
    identf = alloc([128], F32)
    identb = alloc([128], BF16)
    onesb = alloc([128], BF16)
    masks = alloc([256], BF16)
    maskf = alloc([256], F32)
    stats = alloc([32, 4], F32)
    neglam = alloc([4], F32)
    gsub = alloc([1], F32)
    S.dma(lambda e: e.dma_start(out=sm, in_=sm_in[:, :]), w=["sm"])
    S.dma(lambda e: e.dma_start(out=identf, in_=ident_in[:, :]), w=["identf"])
    S.dma(lambda e: e.dma_start(out=maskf, in_=mask_in[:, :]), w=["maskf"])
    S.op("dve", lambda e: e.tensor_copy(out=identb, in_=identf), r=["identf"], w=["identb"])
    S.op("dve", lambda e: e.tensor_copy(out=masks, in_=maskf), r=["maskf"], w=["masks"])
    S.op("dve", lambda e: e.memset(onesb, 1.0), w=["onesb"])
    permb = alloc([128], BF16)
    S.dma(lambda e: e.dma_start(out=permb, in_=perm_in[:, :]), w=["permb"], q="pool")
    base0 = cur[0]

    xnT = alloc([8, NT], BF16)
    ctab = alloc([NT], F16)
    stab = alloc([NT], F16)
    base1 = cur[0]

    dlb = alloc([256], F32)
    dlp = alloc([128], F32)
    dls = alloc([4], F32)
    S.dma(lambda e: e.dma_start(out=dlb, in_=dl_in.ap().broadcast_to([128, 256])), w=["dlb"])
    S.op("dve", lambda e: e.tensor_tensor(out=dlp[:, 0:64], in0=dlb[:, 0:64], in1=dlb[:, 64:128], op=ALU.mult), r=["dlb"], w=["dlp0"])
    S.op("dve", lambda e: e.tensor_tensor(out=dlp[:, 64:128], in0=dlb[:, 128:192], in1=dlb[:, 192:256], op=ALU.mult), r=["dlb"], w=["dlp1"])
    S.op("dve", lambda e: e.reduce_sum(out=dls[:, 0:1], in_=dlp[:, 0:64], axis=AX.X), r=["dlp0"], w=["dls0"])
    S.op("dve", lambda e: e.reduce_sum(out=dls[:, 1:2], in_=dlp[:, 64:128], axis=AX.X), r=["dlp1"], w=["dls1"])
    S.op("act", lambda e: e.activation(out=dls[:, 2:4], in_=dls[:, 0:2], func=AF.Exp), r=["dls0", "dls1"], w=["dle"])
    S.op("dve", lambda e: e.tensor_tensor(out=neglam[:, 0:1], in0=dls[:, 3:4], in1=dls[:, 2:3], op=ALU.subtract), r=["dle"], w=["nl0"])
    S.op("dve", lambda e: e.tensor_scalar(out=neglam[:, 1:2], in0=neglam[:, 0:1], scalar1=-LAM_INIT, scalar2=None, op0=ALU.add), r=["nl0"], w=["neglam"])
    S.op("dve", lambda e: e.tensor_scalar(out=gsub, in0=sm[:, 37:38], scalar1=1.0 - LAM_INIT, scalar2=None, op0=ALU.mult), r=["sm"], w=["gsub"])

    posi = alloc([NT], I32)
    ang = alloc([NT], F32)
    kf = alloc([NT], F32)
    ki = alloc([NT], I32)
    rr = alloc([NT], F32)
    S.dma(lambda e: e.dma_start(out=posi, in_=pos_in.ap().broadcast_to([128, NT])), w=["posi"])
    S.op("dve", lambda e: e.tensor_copy(out=ang, in_=posi), r=["posi"], w=["angf"])
    S.op("dve", lambda e: e.tensor_scalar(out=ang, in0=ang, scalar1=sm[:, 38:39], scalar2=None, op0=ALU.mult), r=["angf", "sm"], w=["ang"])
    S.op("dve", lambda e: e.tensor_scalar(out=kf, in0=ang, scalar1=float(1.0 / (2 * np.pi)), scalar2=None, op0=ALU.mult), r=["ang"], w=["kf0"])
    S.op("dve", lambda e: e.tensor_copy(out=ki, in_=kf), r=["kf0"], w=["ki"])
    S.op("dve", lambda e: e.tensor_copy(out=kf, in_=ki), r=["ki"], w=["kf"])
    S.op("dve", lambda e: e.scalar_tensor_tensor(out=rr, in0=kf, scalar=-C1, in1=ang, op0=ALU.mult, op1=ALU.add), r=["kf", "ang"], w=["rr1"])
    S.op("dve", lambda e: e.scalar_tensor_tensor(out=rr, in0=kf, scalar=-C2, in1=rr, op0=ALU.mult, op1=ALU.add), r=["kf", "rr1"], w=["rr2"])
    S.op("dve", lambda e: e.tensor_scalar(out=kf, in0=rr, scalar1=PI, scalar2=None, op0=ALU.is_gt), r=["rr2"], w=["m1"])
    S.op("dve", lambda e: e.scalar_tensor_tensor(out=rr, in0=kf, scalar=-2 * PI, in1=rr, op0=ALU.mult, op1=ALU.add), r=["m1", "rr2"], w=["rr3"])
    S.op("dve", lambda e: e.tensor_scalar(out=kf, in0=rr, scalar1=-PI, scalar2=None, op0=ALU.is_lt), r=["rr3"], w=["m2"])
    S.op("dve", lambda e: e.scalar_tensor_tensor(out=rr, in0=kf, scalar=2 * PI, in1=rr, op0=ALU.mult, op1=ALU.add), r=["m2", "rr3"], w=["rs"])
    S.op("dve", lambda e: e.tensor_scalar(out=rr, in0=rr, scalar1=PI, scalar2=-PI, op0=ALU.min, op1=ALU.max), r=["rs"], w=["rs2"])
    S.op("act", lambda e: e.activation(out=stab, in_=rr, func=AF.Sin, scale=sm[:, 39:40]), r=["rs2", "sm"], w=["stab"])
    S.op("dve", lambda e: e.tensor_scalar(out=ang, in0=rr, scalar1=PI / 2, scalar2=None, op0=ALU.add), r=["rs2", "ang"], w=["rc"])
    S.op("dve", lambda e: e.tensor_scalar(out=kf, in0=ang, scalar1=PI, scalar2=None, op0=ALU.is_gt), r=["rc"], w=["m3"])
    S.op("dve", lambda e: e.scalar_tensor_tensor(out=ang, in0=kf, scalar=-2 * PI, in1=ang, op0=ALU.mult, op1=ALU.add), r=["m3", "rc"], w=["rc2"])
    S.op("dve", lambda e: e.tensor_scalar(out=ang, in0=ang, scalar1=PI, scalar2=-PI, op0=ALU.min, op1=ALU.max), r=["rc2"], w=["rc3"])
    S.op("act", lambda e: e.activation(out=ctab, in_=ang, func=AF.Sin), r=["rc3"], w=["ctab"])
    S.barrier()
    cur[0] = base1

    xb = [alloc([1024], F32) for _ in range(3)]
    xh = [alloc([1024], F32) for _ in range(2)]
    bst = [alloc([12], F32) for _ in range(2)]
    def p0_L(t):
        xt = xb[t % 3]
        xk = "xb%d" % (t % 3)
        S.dma(lambda e, xt=xt, t=t: e.dma_start(out=xt, in_=x_in[t * 128:(t + 1) * 128, :]), w=[xk])

    def p0_A(t):
        xt = xb[t % 3]
        xk = "xb%d" % (t % 3)
        st_ = bst[t % 2]
        sk = "bst%d" % (t % 2)
        S.op("dve", lambda e, st_=st_, xt=xt: e.bn_stats(out=st_[:, 0:6], in_=xt[:, 0:512]), r=[xk], w=[sk + "a"])
        S.op("dve", lambda e, st_=st_, xt=xt: e.bn_stats(out=st_[:, 6:12], in_=xt[:, 512:1024]), r=[xk], w=[sk + "b"])
        S.op("dve", lambda e, st_=st_, t=t: e.bn_aggr(out=stats[:, t, 0:2], in_=st_[:, 0:12]), r=[sk + "a", sk + "b"], w=["mv%d" % t])
        S.op("act", lambda e, t=t: e.activation(out=stats[:, t, 2:3], in_=stats[:, t, 1:2], func=AF.Ln, bias=sm[:, 40:41], scale=1.0), r=["mv%d" % t, "sm"], w=["lv%d" % t])
        S.op("act", lambda e, t=t: e.activation(out=stats[:, t, 3:4], in_=stats[:, t, 2:3], func=AF.Exp, scale=-0.5), r=["lv%d" % t], w=["rs%d" % t])
        xhh = xh[t % 2]
        hk = "xh%d" % (t % 2)
        S.op("dve", lambda e, xhh=xhh, xt=xt, t=t: e.tensor_scalar(out=xhh, in0=xt, scalar1=stats[:, t, 0:1], scalar2=stats[:, t, 3:4],
                                                               op0=ALU.subtract, op1=ALU.mult), r=[xk, "mv%d" % t, "rs%d" % t], w=[hk])
        return None

    def p0_B(t):
        xhh = xh[t % 2]
        hk = "xh%d" % (t % 2)
        b0 = 2 * (t % 4)
        for c in range(8):
            bk = b0 + c // 4
            S.op("pe", lambda e, bk=bk, c=c, xhh=xhh: e.transpose(out=B(bk)[:, (c % 4) * 128:(c % 4 + 1) * 128], in_=xhh[:, c * 128:(c + 1) * 128], identity=identf),
                 r=[hk, "identf"], x=[BK(bk)])
        for c in range(8):
            bk = b0 + c // 4
            if c % 2 == 0:
                S.op("act", lambda e, bk=bk, c=c, t=t: e.activation(out=xnT[:, c, t * 128:(t + 1) * 128], in_=B(bk)[:, (c % 4) * 128:(c % 4 + 1) * 128],
                                                                  func=AF.Identity, scale=sm[:, c:c + 1], bias=sm[:, 8 + c:9 + c]),
                     r=["sm"], w=[("xnTw", t, c)], x=[BK(bk)])
            else:
                S.op("dve", lambda e, bk=bk, c=c, t=t: e.tensor_scalar(out=xnT[:, c, t * 128:(t + 1) * 128], in0=B(bk)[:, (c % 4) * 128:(c % 4 + 1) * 128],
                                                                     scalar1=sm[:, c:c + 1], scalar2=sm[:, 8 + c:9 + c], op0=ALU.mult, op1=ALU.add),
                     r=["sm"], w=[("xnTw", t, c)], x=[BK(bk)])

    p0_L(0)
    p0_L(1)
    p0_A(0)
    for t in range(32):
        if t + 2 < 32:
            p0_L(t + 2)
        if t + 1 < 32:
            p0_A(t + 1)
        p0_B(t)
    cur[0] = base1
    if stop_after == 0:
        S.barrier()
        dbg = alloc([1024], F32)
        S.op("dve", lambda e: e.tensor_copy(out=dbg, in_=xnT[:, 0, 0:1024]), r=[("xnT", i) for i in range(8)], w=["dbg"])
        S.dma(lambda e: e.dma_start(out=out_t[0:128, :], in_=dbg), r=["dbg"])
        S.emit()
        return nc

    S.barrier()
    NW = 6
    wsl = [alloc([8, 128], BF16) for _ in range(NW)]
    wcnt = [0]

    def load_w(j):
        k = wcnt[0] % NW
        wcnt[0] += 1
        key = "wsl%d" % k
        S.dma(lambda e, k=k, j=j: e.dma_start(out=wsl[k], in_=wf_in[j], max_dma_last_dim=4096), w=[key], q="pool")
        return wsl[k], key

    t1b = [alloc([512], F32) for _ in range(2)]
    t2b = [alloc([512], F32) for _ in range(2)]
    ob = [alloc([512], BF16) for _ in range(4)]
    ocnt = [0]
    bcnt = [0]

    def nb():
        b = bcnt[0] % 8
        bcnt[0] += 1
        return b

    def proj_fm(wt, wkey, tb, bank, M=128, woff=0):
        for c in range(8):
            S.op("pe", lambda e, c=c: e.matmul(B(bank)[0:M, :], lhsT=wt[:, c, woff:woff + M], rhs=xnT[:, c, tb * 512:(tb + 1) * 512],
                                               start=(c == 0), stop=(c == 7)),
                 r=[wkey, ("xnT", tb)], x=[BK(bank)])

    def rope_out(bankA, bankB, tb, M, dst):
        i = ocnt[0]
        ocnt[0] += 1
        t1 = t1b[i % 2]; t2 = t2b[i % 2]; o = ob[i % 4]
        k1 = "t1_%d" % (i % 2); k2 = "t2_%d" % (i % 2); ko = "ob%d" % (i % 4)
        S.op("dve", lambda e: e.tensor_tensor(out=t1[0:M, :], in0=B(bankA)[0:M, :], in1=ctab[0:M, tb * 512:(tb + 1) * 512], op=ALU.mult),
             r=["ctab"], w=[k1], x=[BK(bankA)])
        S.op("dve", lambda e: e.tensor_tensor(out=t2[0:M, :], in0=B(bankB)[0:M, :], in1=stab[0:M, tb * 512:(tb + 1) * 512], op=ALU.mult),
             r=["stab"], w=[k2], x=[BK(bankB)])
        S.op("pool", lambda e: e.tensor_tensor(out=o[0:M, :], in0=t1[0:M, :], in1=t2[0:M, :], op=ALU.add), r=[k1, k2], w=[ko])
        S.dma(lambda e: e.dma_start(out=dst, in_=o[0:M, :]), r=[ko])

    def simple_out(bank, dst, kind, M=128, bias=None):
        i = ocnt[0]
        ocnt[0] += 1
        o = ob[i % 4]; ko = "ob%d" % (i % 4)
        if kind == "silu":
            S.op("act", lambda e: e.activation(out=o[0:M, :], in_=B(bank)[0:M, :], func=AF.Silu), w=[ko], x=[BK(bank)])
        elif kind == "sig":
            S.op("act", lambda e: e.activation(out=o[0:M, :], in_=B(bank)[0:M, :], func=AF.Sigmoid, bias=bias), r=["sm"], w=[ko], x=[BK(bank)])
        elif kind == "copy_act":
            S.op("act", lambda e: e.activation(out=o[0:M, :], in_=B(bank)[0:M, :], func=AF.Copy), w=[ko], x=[BK(bank)])
        else:
            S.op("dve", lambda e: e.tensor_copy(out=o[0:M, :], in_=B(bank)[0:M, :]), w=[ko], x=[BK(bank)])
        S.dma(lambda e: e.dma_start(out=dst, in_=o[0:M, :]), r=[ko])

    tasks = []

    hbuf = [alloc([512], BF16) for _ in range(3)]
    hcnt = [0]
    rpend = [None]

    def rflush():
        if rpend[0] is None:
            return
        bA, tb, hb, hk, dst = rpend[0]
        rpend[0] = None
        bB = nb()
        S.op("pe", lambda e: e.matmul(B(bB), lhsT=permb, rhs=hb, start=True, stop=True), r=[hk, "permb"], x=[BK(bB)])
        rope_out(bA, bB, tb, 128, dst)

    def t_rope(ja, jr, ntb, dstf):
        def ld():
            return [load_w(ja)]
        def cp_(ws):
            (wa, ka_) = ws[0]
            for tb in range(ntb):
                bA = nb()
                proj_fm(wa, ka_, tb, bA)
                hi = hcnt[0] % 3; hcnt[0] += 1
                hb = hbuf[hi]; hk = "hb%d" % hi
                S.op("act", lambda e, hb=hb, bA=bA: e.activation(out=hb, in_=B(bA), func=AF.Copy), w=[hk], x=[BK(bA)])
                rflush()
                rpend[0] = (bA, tb, hb, hk, dstf(tb))
        tasks.append((ld, cp_))

    def t_simple(j, dstf, kind, bias=None):
        def ld():
            return [load_w(j)]
        def cp_(ws):
            wa, ka_ = ws[0]
            for tb in range(4):
                bA = nb(); proj_fm(wa, ka_, tb, bA)
                rflush()
                simple_out(bA, dstf(tb), kind, bias=bias)
        tasks.append((ld, cp_))

    for h in range(8):
        t_rope(h, 8 + h, 4, lambda tb, h=h: Qa[h, :, tb * 512:(tb + 1) * 512])
        t_rope(16 + h, 24 + h, 8, lambda tb, h=h: Ka[h, :, tb * 512:(tb + 1) * 512])
    for h in range(8):
        t_simple(32 + h, lambda tb, h=h: Za[h, :, tb * 512:(tb + 1) * 512], "silu")
        t_simple(40 + h, lambda tb, h=h: Zb[h, :, tb * 512:(tb + 1) * 512], "silu")
    for j in range(16):
        t_simple(48 + j, lambda tb, j=j: Sg[j, :, tb * 512:(tb + 1) * 512], "sig", bias=sm[:, 16 + j:17 + j])
    hnd = {}
    for i in range(min(2, len(tasks))):
        hnd[i] = tasks[i][0]()
    for i in range(len(tasks)):
        if i + 2 < len(tasks):
            hnd[i + 2] = tasks[i + 2][0]()
        tasks[i][1](hnd.pop(i))
    rflush()
    wvs2 = [alloc([8, 512], BF16) for _ in range(2)]
    cp = [0]
    for g in range(2):
        S.dma(lambda e, g=g: e.dma_start(out=wvs2[g], in_=wv_in[g], max_dma_last_dim=4096), w=["wvs%d" % g], q="pool")
    for g in range(2):
        wvs = wvs2[g]
        for tt in range(32):
            bA = nb()
            for c in range(8):
                S.op("pe", lambda e, c=c, tt=tt, bA=bA, wvs=wvs: e.matmul(B(bA), lhsT=xnT[:, c, tt * 128:(tt + 1) * 128], rhs=wvs[:, c, :], start=(c == 0), stop=(c == 7)),
                     r=["wvs%d" % g, ("xnT", tt // 4)], x=[BK(bA)])
            cp[0] += 1
            simple_out(bA, Va[tt * 128:(tt + 1) * 128, g * 512:(g + 1) * 512], "copy_act" if cp[0] % 2 else "copy")
    cqn = alloc([3, NO], BF16)
    ckvn = alloc([2, NT], BF16)
    latf = alloc([3, 512], F32)
    sqb = alloc([3, 512], BF16)
    lnv = alloc([512], F32)
    rstd = alloc([512], F32)

    def latent(jlist, ntb, dstn, gcol, nfeat):
        ws = [load_w(j) for j in jlist]
        n = len(jlist)
        for tb in range(ntb):
            banks = []
            for q in range(n):
                bA = nb(); banks.append(bA)
                proj_fm(ws[q][0], ws[q][1], tb, bA)
            for q in range(n):
                S.op("dve", lambda e, q=q, bA=banks[q]: e.tensor_copy(out=latf[:, q, :], in_=B(bA)), w=[("latf", q)], x=[BK(banks[q])])
                S.op("pool", lambda e, q=q: e.tensor_tensor(out=sqb[:, q, :], in0=latf[:, q, :], in1=latf[:, q, :], op=ALU.mult), r=[("latf", q)], w=[("sqb", q)])
            bM = nb()
            for q in range(n):
                S.op("pe", lambda e, q=q, bM=bM: e.matmul(B(bM), lhsT=onesb, rhs=sqb[:, q, :], start=(q == 0), stop=(q == n - 1)),
                     r=["onesb", ("sqb", q)], x=[BK(bM)])
            S.op("act", lambda e, bM=bM: e.activation(out=lnv, in_=B(bM), func=AF.Ln, scale=1.0 / nfeat, bias=sm[:, 41:42]), r=["sm"], w=["lnv"], x=[BK(bM)])
            S.op("act", lambda e: e.activation(out=rstd, in_=lnv, func=AF.Exp, scale=-0.5), r=["lnv"], w=["rstd"])
            for q in range(n):
                S.op("dve", lambda e, q=q, tb=tb: e.scalar_tensor_tensor(out=dstn[:, q, tb * 512:(tb + 1) * 512], in0=latf[:, q, :], scalar=sm[:, gcol + q:gcol + q + 1],
                                                                       in1=rstd, op0=ALU.mult, op1=ALU.mult),
                     r=[("latf", q), "rstd", "sm"], w=[("lat", id(dstn), q, tb)])

    latent([64, 65, 66], 4, cqn, 32, 384.0)
    latent([67, 68], 8, ckvn, 35, 256.0)
    LATK = [("lat", id(cqn), q, tb) for q in range(3) for tb in range(4)]
    LATKV = [("lat", id(ckvn), q, tb) for q in range(2) for tb in range(8)]
    wa, ka_ = load_w(69)
    for tb in range(8):
        bA = nb(); bB = nb()
        proj_fm(wa, ka_, tb, bA, M=64, woff=0); proj_fm(wa, ka_, tb, bB, M=64, woff=64)
        rope_out(bA, bB, tb, 64, Kp[:, tb * 512:(tb + 1) * 512])
    wqn = [alloc([3, 128], BF16) for _ in range(2)]
    wqp = [alloc([3, 128], BF16) for _ in range(2)]
    wkk = [alloc([2, 128], BF16) for _ in range(2)]
    def ld_up(h):
        s2 = h % 2
        S.dma(lambda e, h=h, s2=s2: e.dma_start(out=wqn[s2], in_=wuqn_in[h], max_dma_last_dim=4096), w=["wqn%d" % s2], q="pool")
        S.dma(lambda e, h=h, s2=s2: e.dma_start(out=wqp[s2], in_=wuqp_in[h], max_dma_last_dim=4096), w=["wqp%d" % s2], q="pool")
        S.dma(lambda e, h=h, s2=s2: e.dma_start(out=wkk[s2], in_=wukk_in[h], max_dma_last_dim=4096), w=["wkk%d" % s2], q="pool")
    wvv2 = [alloc([2, 512], BF16) for _ in range(2)]
    for g in range(2):
        S.dma(lambda e, g=g: e.dma_start(out=wvv2[g], in_=wukv_in[g], max_dma_last_dim=4096), w=["wvv%d" % g], q="pool")
    ld_up(0)
    for h in range(8):
        s2 = h % 2
        if h + 1 < 8:
            ld_up(h + 1)
        for tb in range(4):
            bA = nb()
            for j in range(3):
                S.op("pe", lambda e, j=j, bA=bA, tb=tb, s2=s2: e.matmul(B(bA), lhsT=wqn[s2][:, j, :], rhs=cqn[:, j, tb * 512:(tb + 1) * 512], start=(j == 0), stop=(j == 2)),
                     r=["wqn%d" % s2] + LATK, x=[BK(bA)])
            simple_out(bA, Qn[h, :, tb * 512:(tb + 1) * 512], "copy")
            bA = nb(); bB = nb()
            for j in range(3):
                S.op("pe", lambda e, j=j, bA=bA, tb=tb, s2=s2: e.matmul(B(bA)[0:64, :], lhsT=wqp[s2][:, j, 0:64], rhs=cqn[:, j, tb * 512:(tb + 1) * 512], start=(j == 0), stop=(j == 2)),
                     r=["wqp%d" % s2] + LATK, x=[BK(bA)])
            for j in range(3):
                S.op("pe", lambda e, j=j, bB=bB, tb=tb, s2=s2: e.matmul(B(bB)[0:64, :], lhsT=wqp[s2][:, j, 64:128], rhs=cqn[:, j, tb * 512:(tb + 1) * 512], start=(j == 0), stop=(j == 2)),
                     r=["wqp%d" % s2] + LATK, x=[BK(bB)])
            rope_out(bA, bB, tb, 64, Qp[h, :, tb * 512:(tb + 1) * 512])
        for tb in range(8):
            bA = nb()
            for j in range(2):
                S.op("pe", lambda e, j=j, bA=bA, tb=tb, s2=s2: e.matmul(B(bA), lhsT=wkk[s2][:, j, :], rhs=ckvn[:, j, tb * 512:(tb + 1) * 512], start=(j == 0), stop=(j == 1)),
                     r=["wkk%d" % s2] + LATKV, x=[BK(bA)])
            simple_out(bA, Kn[h, :, tb * 512:(tb + 1) * 512], "copy_act")
    for g in range(2):
        wvv = wvv2[g]
        for tt in range(32):
            bA = nb()
            for j in range(2):
                S.op("pe", lambda e, j=j, tt=tt, bA=bA, wvv=wvv: e.matmul(B(bA), lhsT=ckvn[:, j, tt * 128:(tt + 1) * 128], rhs=wvv[:, j, :], start=(j == 0), stop=(j == 1)),
                     r=["wvv%d" % g] + LATKV, x=[BK(bA)])
            cp[0] += 1
            simple_out(bA, Vb[tt * 128:(tt + 1) * 128, g * 512:(g + 1) * 512], "copy_act" if cp[0] % 2 else "copy")

    S.barrier()
    cur[0] = base0
    NB2 = 2
    qT = [alloc([NO], BF16) for _ in range(NB2)]
    kT = [alloc([NT], BF16) for _ in range(NB2)]
    vS = [alloc([32, 128], BF16) for _ in range(NB2)]
    zT = [alloc([NO], BF16) for _ in range(NB2)]
    qpT = [alloc([NO], BF16) for _ in range(NB2)]
    kpT = alloc([NT], BF16)
    NP = 4
    P1 = [alloc([512], BF16) for _ in range(NP)]
    P2 = [alloc([512], BF16) for _ in range(NP)]
    fr1 = alloc([512], F32); fo1 = alloc([512], F32); fr2 = alloc([512], F32); ft2 = alloc([512], F32)
    fo = alloc([512], F32); fsq = alloc([512], BF16); fln = alloc([512], F32); frs = alloc([512], F32); fu = alloc([512], F32)
    gob = [alloc([512], BF16) for _ in range(2)]
    gcnt = [0]
    cur[0] = max(cur[0], 98560)
    base2 = cur[0]
    wfin = [alloc([8, 1024], BF16) for _ in range(4)]
    wpp = alloc([2, 1024], BF16)
    pTs = alloc([2, NO], BF16)
    rows = alloc([5, 1024], F32)

    def prefetch_p3():
        for i in range(4):
            for hh in range(2):
                S.dma(lambda e, i=i, hh=hh: e.dma_start(out=wfin[i][:, 4 * hh:4 * hh + 4, :], in_=wfin_in[i, :, 4 * hh:4 * hh + 4, :], max_dma_last_dim=4096), w=[("wfin", i, hh)], q="pool")
        S.dma(lambda e: e.dma_start(out=wpp, in_=wpp_in[:, :, :], max_dma_last_dim=4096), w=["wpp"], q="pool")
        S.dma(lambda e: e.dma_start(out=pTs, in_=pT_in[:, :, :], max_dma_last_dim=4096), w=["pTs"], q="pool")
        for i in range(5):
            S.dma(lambda e, i=i: e.dma_start(out=rows[:, i, :], in_=rows_in[i:i + 1, :].broadcast_to([128, 1024])), w=[("rows", i)], q="pool")

    def ktiles(g):
        lst = []
        for t in range(4 * g):
            lst.append((t, 0, None))
            lst.append((16 + t, 0, None))
        for m in range(4):
            lst.append((4 * g + m, 128 * m, 0))
            lst.append((16 + 4 * g + m, 128 * m, 1))
        return lst

    SC_A = float(64 ** -0.5)
    SC_B = float(192 ** -0.5)
    pcnt = [0]

    def load_head(h, mixer):
        s = h % NB2
        kq = "qT%d" % s; kk = "kT%d" % s; kv = "vS%d" % s; kz = "zT%d" % s; kqp = "qpT%d" % s
        if mixer == 0:
            S.dma(lambda e: e.dma_start(out=qT[s], in_=Qa[h]), w=[kq])
            S.dma(lambda e: e.dma_start(out=kT[s], in_=Ka[h]), w=[kk])
            for qq in range(4):
                S.dma(lambda e, qq=qq: e.dma_start(out=vS[s][:, 8 * qq:8 * qq + 8, :], in_=Va.ap().rearrange("(t p) n -> p t n", p=128)[:, 8 * qq:8 * qq + 8, h * 128:(h + 1) * 128]), w=[kv + "_%d" % qq])
            S.dma(lambda e: e.dma_start(out=zT[s], in_=Za[h]), w=[kz])
        else:
            S.dma(lambda e: e.dma_start(out=qT[s], in_=Qn[h]), w=[kq])
            S.dma(lambda e: e.dma_start(out=kT[s], in_=Kn[h]), w=[kk])
            for qq in range(4):
                S.dma(lambda e, qq=qq: e.dma_start(out=vS[s][:, 8 * qq:8 * qq + 8, :], in_=Vb.ap().rearrange("(t p) n -> p t n", p=128)[:, 8 * qq:8 * qq + 8, h * 128:(h + 1) * 128]), w=[kv + "_%d" % qq])
            S.dma(lambda e: e.dma_start(out=zT[s], in_=Zb[h]), w=[kz])
            S.dma(lambda e: e.dma_start(out=qpT[s][0:64, :], in_=Qp[h]), w=[kqp])

    def attention(h, mixer, nxt=None):
        s = h % NB2
        kq = "qT%d" % s; kk = "kT%d" % s; kv = "vS%d" % s; kz = "zT%d" % s; kqp = "qpT%d" % s
        nmap = 2 if mixer == 0 else 1
        for g in range(4):
            kts = ktiles(g)
            q0 = 512 * g

            def qk(i):
                slot, c0, mk = kts[i]
                for mp in range(nmap):
                    bank = (2 * mp + (i % 2)) if mixer == 0 else (i % 4)
                    nomask = (mk is None)
                    if mixer == 0:
                        rows = slice(64 * mp, 64 * mp + 64)
                        S.op("pe", lambda e, rows=rows, bank=bank, slot=slot, c0=c0, q0=q0, nomask=nomask: e.matmul(B(bank)[:, c0:512], lhsT=kT[s][rows, slot * 128:(slot + 1) * 128],
                                                                                            rhs=qT[s][rows, q0 + c0:q0 + 512], start=True, stop=nomask),
                             r=[kq, kk], x=[BK(bank)])
                    else:
                        S.op("pe", lambda e, bank=bank, slot=slot, c0=c0, q0=q0: e.matmul(B(bank)[:, c0:512], lhsT=kT[s][:, slot * 128:(slot + 1) * 128],
                                                                                 rhs=qT[s][:, q0 + c0:q0 + 512], start=True, stop=False),
                             r=[kq, kk], x=[BK(bank)])
                        S.op("pe", lambda e, bank=bank, slot=slot, c0=c0, q0=q0, nomask=nomask: e.matmul(B(bank)[:, c0:512], lhsT=kpT[:, slot * 128:(slot + 1) * 128],
                                                                                 rhs=qpT[s][:, q0 + c0:q0 + 512], start=False, stop=nomask),
                             r=[kqp, "kpT", "kpz", "qpz%d" % s], x=[BK(bank)])
                    if mk is not None:
                        S.op("pe", lambda e, bank=bank, c0=c0, mk=mk: e.matmul(B(bank)[:, c0:c0 + 128], lhsT=identb, rhs=masks[:, mk * 128:(mk + 1) * 128],
                                                                             start=False, stop=True),
                             r=["identb", "masks"], x=[BK(bank)])

            def expo(i):
                slot, c0, mk = kts[i]
                res = []
                for mp in range(nmap):
                    bank = (2 * mp + (i % 2)) if mixer == 0 else (i % 4)
                    pc = pcnt[0] % NP
                    Pb = (P1 if mp == 0 else P2)[pc]
                    pk = "P%d_%d" % (mp, pc)
                    S.op("act", lambda e, bank=bank, Pb=Pb, c0=c0: e.activation(out=Pb[:, c0:512], in_=B(bank)[:, c0:512], func=AF.Exp,
                                                                              scale=(SC_A if mixer == 0 else SC_B)),
                         w=[pk], x=[BK(bank)])
                    res.append((Pb, pk))
                pcnt[0] += 1
                return res

            def pv(i, pbs):
                slot, c0, mk = kts[i]
                first = (i == 0); last = (i == len(kts) - 1)
                for mp in range(nmap):
                    Pb, pk = pbs[mp]
                    bo = 4 + mp; bl = 6 + mp
                    S.op("pe", lambda e, Pb=Pb, bo=bo, slot=slot, c0=c0: e.matmul(B(bo)[:, c0:512], lhsT=vS[s][:, slot, :], rhs=Pb[:, c0:512], start=first, stop=last),
                         r=[pk] + [kv + "_%d" % qq for qq in range(4)], x=[BK(bo)])
                    S.op("pe", lambda e, Pb=Pb, bl=bl, c0=c0: e.matmul(B(bl)[:, c0:512], lhsT=onesb, rhs=Pb[:, c0:512], start=first, stop=last),
                         r=[pk, "onesb"], x=[BK(bl)])

            LA = 2
            for i0 in range(LA):
                qk(i0)
            for i in range(len(kts)):
                pbs = expo(i)
                if i == 5 and pend2[0] is not None:
                    pend2[0](i)
                    pend2[0] = None
                if i + LA < len(kts):
                    qk(i + LA)
                pv(i, pbs)
                if i == 1 and pend[0] is not None:
                    pend[0]()
                    pend[0] = None
                if i == 6 and pend3[0] is not None:
                    pend3[0]()
                    pend3[0] = None
                if i == 7 and g == 0 and nxt is not None:
                    load_head(*nxt)
            if pend[0] is not None:
                pend[0]()
                pend[0] = None
            if pend2[0] is not None:
                pend2[0](len(kts) - 1)
                pend2[0] = None
            if pend3[0] is not None:
                pend3[0]()
                pend3[0] = None
            gi = gcnt[0] % 2
            gcnt[0] += 1
            go = gob[gi]; gk = "gob%d" % gi
            if mixer == 0:
                S.op("dve", lambda e: e.tensor_copy(out=fr1, in_=B(6)), w=["fr1c"], x=[BK(6)])
                S.op("dve", lambda e: e.tensor_copy(out=fo1, in_=B(4)), w=["fo1c"], x=[BK(4)])
                S.op("dve", lambda e: e.tensor_copy(out=fr2, in_=B(7)), w=["fr2c"], x=[BK(7)])
                S.op("dve", lambda e: e.tensor_copy(out=ft2, in_=B(5)), w=["ft2c"], x=[BK(5)])

                def partB1():
                    S.op("dve", lambda e: e.tensor_tensor(out=fo1, in0=fo1, in1=fr2, op=ALU.mult), r=["fo1c", "fr2c"], w=["fo1"])
                    S.op("dve", lambda e: e.tensor_tensor(out=ft2, in0=ft2, in1=fr1, op=ALU.mult), r=["ft2c", "fr1c"], w=["ft2"])
                    S.op("dve", lambda e: e.scalar_tensor_tensor(out=fo, in0=ft2, scalar=neglam[:, 1:2], in1=fo1, op0=ALU.mult, op1=ALU.add),
                         r=["ft2", "fo1", "neglam"], w=["fo"])
                    S.op("dve", lambda e: e.tensor_tensor(out=fr1, in0=fr1, in1=fr2, op=ALU.mult), r=["fr1c", "fr2c", "ft2"], w=["fcc"])
                    S.op("dve", lambda e: e.scalar_tensor_tensor(out=fr2, in0=fr1, scalar=1e-6, in1=fr1, op0=ALU.mult, op1=ALU.mult),
                         r=["fcc", "fo1"], w=["fce"])
                    S.op("pool", lambda e: e.tensor_tensor(out=fsq, in0=fo, in1=fo, op=ALU.mult), r=["fo"], w=["fsq"])

                def partB2(i_):
                    bk_ = i_ % 2
                    S.op("pe", lambda e: e.matmul(B(bk_), lhsT=onesb, rhs=fsq, start=True, stop=True), r=["fsq", "onesb"], x=[BK(bk_)])
                    S.op("dve", lambda e: e.scalar_tensor_tensor(out=fln, in0=B(bk_), scalar=1.0 / 128.0, in1=fr2, op0=ALU.mult, op1=ALU.add),
                         r=["fce"], w=["flt"], x=[BK(bk_)])

                def partB3(go=go, gk=gk, q0=q0, h=h, s=s, kz=kz):
                    S.op("act", lambda e: e.activation(out=fln, in_=fln, func=AF.Ln), r=["flt"], w=["fln"])
                    S.op("act", lambda e: e.activation(out=frs, in_=fln, func=AF.Exp, scale=-0.5), r=["fln"], w=["frs"])
                    S.op("dve", lambda e: e.scalar_tensor_tensor(out=fu, in0=fo, scalar=gsub, in1=frs, op0=ALU.mult, op1=ALU.mult),
                         r=["fo", "frs", "gsub"], w=["fu"])
                    S.op("pool", lambda e: e.tensor_tensor(out=go, in0=fu, in1=zT[s][:, q0:q0 + 512], op=ALU.mult), r=["fu", kz], w=[gk])
                    S.dma(lambda e: e.dma_start(out=Ga[h, :, q0:q0 + 512], in_=go), r=[gk])
                pend3[0] = partB3
                pend[0] = partB1
                pend2[0] = partB2
            else:
                S.op("dve", lambda e: e.tensor_copy(out=fr1, in_=B(6)), w=["fr1c"], x=[BK(6)])
                S.op("dve", lambda e: e.tensor_copy(out=fo1, in_=B(4)), w=["fo1c"], x=[BK(4)])

                def partBm(i_, go=go, gk=gk, q0=q0, h=h, s=s, kz=kz):
                    S.op("dve", lambda e: e.reciprocal(out=fr1, in_=fr1), r=["fr1c"], w=["fr1"])
                    S.op("dve", lambda e: e.tensor_tensor(out=fo1, in0=fo1, in1=fr1, op=ALU.mult), r=["fr1", "fo1c"], w=["fo1"])
                    S.op("pool", lambda e: e.tensor_tensor(out=go, in0=fo1, in1=zT[s][:, q0:q0 + 512], op=ALU.mult), r=["fo1", kz], w=[gk])
                    S.dma(lambda e: e.dma_start(out=Gb[h, :, q0:q0 + 512], in_=go), r=[gk])
                pend2[0] = partBm

    for s_ in range(NB2):
        S.op("pool", lambda e, s_=s_: e.memset(qpT[s_][64:128, :], 0.0), w=["qpz%d" % s_])
    S.op("pool", lambda e: e.memset(kpT[64:128, :], 0.0), w=["kpz"])
    pend = [None]
    pend2 = [None]
    pend3 = [None]
    if stop_after >= 2:
        items = [(h, 0) for h in range(8)] + [(h, 1) for h in range(8)]
        S.dma(lambda e: e.dma_start(out=kpT[0:64, :], in_=Kp[:, :]), w=["kpT"])
        load_head(*items[0])
        prefetch_p3()
        for i, it in enumerate(items):
            attention(it[0], it[1], items[i + 1] if i + 1 < len(items) else None)
        if pend[0] is not None:
            pend[0]()
            pend[0] = None
        if pend2[0] is not None:
            pend2[0](1)
            pend2[0] = None
        if pend3[0] is not None:
            pend3[0]()
            pend3[0] = None

    S.barrier()
    cur[0] = base0
    if stop_after < 2:
        prefetch_p3()
    WF = lambda i: [("wfin", i, 0), ("wfin", i, 1)]
    gaS = alloc([8, 512], BF16); gbS = alloc([8, 512], BF16); sgS = alloc([16, 512], BF16)
    mT = alloc([8, 512], BF16)
    ft = alloc([512], F32); fu2 = alloc([512], F32)
    xr2 = [alloc([1024], F32) for _ in range(2)]; xnr2 = xr2
    yy2 = [alloc([1024], F32) for _ in range(2)]; ybf2 = [alloc([1024], BF16) for _ in range(2)]
    yT2 = [alloc([8, 128], BF16) for _ in range(2)]
    sgi2 = [alloc([1024], F32) for _ in range(2)]; y22 = [alloc([1024], F32) for _ in range(2)]
    st32 = [alloc([12], F32) for _ in range(2)]; mv32 = [alloc([4], F32) for _ in range(2)]
    oo = [alloc([1024], F32) for _ in range(2)]
    MT = [("mT", dc) for dc in range(8)]
    assert cur[0] <= base2, ("phase-3 work buffers overlap prefetched weights", cur[0], base2)

    def blk_load(blk):
        c0 = blk * 512
        S.dma(lambda e, c0=c0: e.dma_start(out=gaS, in_=Ga.ap().rearrange("h p t -> p h t")[:, :, c0:c0 + 512]), w=["gaS"])
        S.dma(lambda e, c0=c0: e.dma_start(out=gbS, in_=Gb.ap().rearrange("h p t -> p h t")[:, :, c0:c0 + 512]), w=["gbS"])
        for hh2 in range(2):
            S.dma(lambda e, c0=c0, hh2=hh2: e.dma_start(out=sgS[:, 8 * hh2:8 * hh2 + 8, :], in_=Sg.ap().rearrange("h p t -> p h t")[:, 8 * hh2:8 * hh2 + 8, c0:c0 + 512]), w=["sgS%d" % hh2])

    def blk_prep(blk):
        for dc in range(8):
            bA = 7; bB = 6
            for hh in range(8):
                S.op("pe", lambda e, hh=hh, dc=dc, bA=bA: e.matmul(B(bA), lhsT=wfin[0][:, hh, dc * 128:(dc + 1) * 128], rhs=gaS[:, hh, :], start=(hh == 0), stop=(hh == 7)),
                     r=WF(0) + ["gaS"], x=[BK(bA)])
            for hh in range(8):
                S.op("pe", lambda e, hh=hh, dc=dc, bB=bB: e.matmul(B(bB), lhsT=wfin[1][:, hh, dc * 128:(dc + 1) * 128], rhs=gbS[:, hh, :], start=(hh == 0), stop=(hh == 7)),
                     r=WF(1) + ["gbS"], x=[BK(bB)])
            S.op("dve", lambda e, dc=dc, bA=bA: e.tensor_tensor(out=ft, in0=B(bA), in1=sgS[:, dc, :], op=ALU.mult), r=["sgS0"], w=["ft"], x=[BK(bA)])
            S.op("dve", lambda e, dc=dc, bB=bB: e.tensor_tensor(out=fu2, in0=B(bB), in1=sgS[:, 8 + dc, :], op=ALU.mult), r=["sgS1"], w=["fu2"], x=[BK(bB)])
            S.op("pool", lambda e, dc=dc: e.tensor_tensor(out=mT[:, dc, :], in0=ft, in1=fu2, op=ALU.add), r=["ft", "fu2"], w=[("mT", dc)])

    def bufs(t):
        u = t % 2
        return dict(u=u, xr=xr2[u], xnr=xnr2[u], yy=yy2[u], ybf=ybf2[u], yT=yT2[u], sgi=sgi2[u], y2=y22[u], st3=st32[u], mv3=mv32[u], o_=oo[u])

    def st_X(t):
        d_ = bufs(t); u = d_["u"]
        xr = d_["xr"]
        K = lambda n: "%s_%d" % (n, u)
        S.dma(lambda e, t=t, xr=xr: e.dma_start(out=xr, in_=x_in[t * 128:(t + 1) * 128, :]), w=[K("xr"), K("xnr")])
        S.op("dve", lambda e, t=t, xr=xr: e.scalar_tensor_tensor(out=xr, in0=xr, scalar=stats[:, t, 0:1], in1=rows[:, 0, :], op0=ALU.subtract, op1=ALU.mult),
             r=[K("xr"), ("rows", 0)], w=[K("xnr0")])
        S.op("dve", lambda e, t=t, xr=xr: e.scalar_tensor_tensor(out=xr, in0=xr, scalar=stats[:, t, 3:4], in1=rows[:, 1, :], op0=ALU.mult, op1=ALU.add),
             r=[K("xnr0"), ("rows", 1)], w=[K("xnr")])

    def st_A(t):
        d_ = bufs(t); u = d_["u"]
        xr = d_["xr"]; xnr = d_["xnr"]; yy = d_["yy"]; ybf = d_["ybf"]; yT = d_["yT"]
        K = lambda n: "%s_%d" % (n, u)
        tc0 = (t % 4) * 128
        bt = 6
        for half in range(2):
            for dc in range(8):
                S.op("pe", lambda e, dc=dc, half=half, tc0=tc0: e.matmul(B(4 + half), lhsT=mT[:, dc, tc0:tc0 + 128], rhs=wfin[2][:, dc, half * 512:(half + 1) * 512],
                                                                       start=(dc == 0), stop=(dc == 7)),
                     r=WF(2) + MT, x=[BK(4 + half)])
        for half in range(2):
            S.op("dve", lambda e, half=half, yy=yy, xnr=xnr: e.scalar_tensor_tensor(out=yy[:, half * 512:(half + 1) * 512], in0=xnr[:, half * 512:(half + 1) * 512], scalar=ALPHA,
                                                                  in1=B(4 + half), op0=ALU.mult, op1=ALU.add),
                 r=[K("xnr")], w=[K("yy%d" % half)], x=[BK(4 + half)])
        S.op("act", lambda e, ybf=ybf, yy=yy: e.activation(out=ybf, in_=yy, func=AF.Copy), r=[K("yy0"), K("yy1")], w=[K("ybf")])

    def st_A2(t):
        d_ = bufs(t); u = d_["u"]
        ybf = d_["ybf"]; yT = d_["yT"]
        K = lambda n: "%s_%d" % (n, u)
        bt = 6
        for dc in range(8):
            S.op("pe", lambda e, dc=dc, ybf=ybf, bt=bt: e.transpose(out=pst16[:, bt, dc * 128:(dc + 1) * 128], in_=ybf[:, dc * 128:(dc + 1) * 128], identity=identb),
                 r=[K("ybf"), "identb"], x=[BK(bt)])
        S.op("dve", lambda e, yT=yT, bt=bt: e.tensor_copy(out=yT, in_=pst16[:, bt, :].rearrange("p (a b) -> p a b", a=8)), w=[K("yT")], x=[BK(bt)])

    def st_B(t):
        d_ = bufs(t); u = d_["u"]
        yy = d_["yy"]; yT = d_["yT"]; sgi = d_["sgi"]; y2 = d_["y2"]
        K = lambda n: "%s_%d" % (n, u)
        for half in range(2):
            for dc in range(8):
                S.op("pe", lambda e, dc=dc, half=half, yT=yT: e.matmul(B(half), lhsT=yT[:, dc, :], rhs=wfin[3][:, dc, half * 512:(half + 1) * 512], start=(dc == 0), stop=(dc == 7)),
                     r=WF(3) + [K("yT")], x=[BK(half)])
            for j in range(2):
                S.op("pe", lambda e, j=j, half=half, t=t: e.matmul(B(2 + half), lhsT=pTs[:, j, t * 128:(t + 1) * 128], rhs=wpp[:, j, half * 512:(half + 1) * 512], start=(j == 0), stop=(j == 1)),
                     r=["pTs", "wpp"], x=[BK(2 + half)])
        for half in range(2):
            hs = slice(half * 512, (half + 1) * 512)
            S.op("dve", lambda e, half=half, hs=hs, sgi=sgi: e.tensor_tensor(out=sgi[:, hs], in0=B(half), in1=rows[:, 2, hs], op=ALU.add), r=[("rows", 2)], w=[K("sgi%d" % half)], x=[BK(half)])
            S.op("act", lambda e, hs=hs, sgi=sgi: e.activation(out=sgi[:, hs], in_=sgi[:, hs], func=AF.Sigmoid), r=[K("sgi%d" % half)], w=[K("sgo%d" % half)])
            S.op("dve", lambda e, half=half, hs=hs, sgi=sgi, y2=y2: e.tensor_tensor(out=y2[:, hs], in0=B(2 + half), in1=sgi[:, hs], op=ALU.mult), r=[K("sgo%d" % half)], w=[K("y2a%d" % half)], x=[BK(2 + half)])
            S.op("pool", lambda e, hs=hs, y2=y2, yy=yy: e.tensor_tensor(out=y2[:, hs], in0=y2[:, hs], in1=yy[:, hs], op=ALU.add), r=[K("y2a%d" % half), K("yy%d" % half)], w=[K("y2%d" % half)])

    def st_C(t):
        d_ = bufs(t); u = d_["u"]
        y2 = d_["y2"]; st3 = d_["st3"]; mv3 = d_["mv3"]; o_ = d_["o_"]
        K = lambda n: "%s_%d" % (n, u)
        ok_ = "oo%d" % u
        S.op("dve", lambda e, st3=st3, y2=y2: e.bn_stats(out=st3[:, 0:6], in_=y2[:, 0:512]), r=[K("y20")], w=[K("st3a")])
        S.op("dve", lambda e, st3=st3, y2=y2: e.bn_stats(out=st3[:, 6:12], in_=y2[:, 512:1024]), r=[K("y21")], w=[K("st3b")])
        S.op("dve", lambda e, st3=st3, mv3=mv3: e.bn_aggr(out=mv3[:, 0:2], in_=st3), r=[K("st3a"), K("st3b")], w=[K("mv3")])
        S.op("act", lambda e, mv3=mv3: e.activation(out=mv3[:, 2:3], in_=mv3[:, 1:2], func=AF.Ln, bias=sm[:, 40:41], scale=1.0), r=[K("mv3"), "sm"], w=[K("lv3")])
        S.op("act", lambda e, mv3=mv3: e.activation(out=mv3[:, 3:4], in_=mv3[:, 2:3], func=AF.Exp, scale=-0.5), r=[K("lv3")], w=[K("rs3")])
        S.op("dve", lambda e, o_=o_, y2=y2, mv3=mv3: e.scalar_tensor_tensor(out=o_, in0=y2, scalar=mv3[:, 0:1], in1=rows[:, 3, :], op0=ALU.subtract, op1=ALU.mult),
             r=[K("y20"), K("y21"), K("mv3"), ("rows", 3)], w=[ok_ + "a"])
        S.op("dve", lambda e, o_=o_, mv3=mv3: e.scalar_tensor_tensor(out=o_, in0=o_, scalar=mv3[:, 3:4], in1=rows[:, 4, :], op0=ALU.mult, op1=ALU.add),
             r=[ok_ + "a", K("rs3"), ("rows", 4)], w=[ok_])
        S.dma(lambda e, o_=o_, t=t: e.dma_start(out=out_t[t * 128:(t + 1) * 128, :], in_=o_), r=[ok_])

    blk_load(0)
    st_X(0)
    for it in range(16 + 2):
        if 0 <= it - 2 < 16:
            st_C(it - 2)
        if it < 16:
            if it % 4 == 0:
                blk_prep(it // 4)
                if it // 4 + 1 < 4:
                    blk_load(it // 4 + 1)
            st_A(it)
            if it + 1 < 16:
                st_X(it + 1)
        if 0 <= it - 1 < 16:
            st_B(it - 1)
        if it < 16:
            st_A2(it)
    S.emit()
    return nc


def _rot_perm():
    n = np.arange(128)
    src = 64 * (n // 64) + np.where(n % 64 < 32, n % 64 + 32, n % 64 - 32)
    pm = np.zeros((128, 128), np.float32)
    pm[src, n] = 1.0
    return pm


def _prep_shared(inp):
    f = lambda k: np.asarray(inp[k], dtype=np.float32)
    W = f("w_in")[0]
    def chunk(cols):
        return W[:, cols].reshape(8, 128, len(cols)).transpose(1, 0, 2)
    def rot128(base):
        n = np.arange(128); m = n // 64; i = n % 64
        return base + 64 * m + np.where(i < 32, i + 32, i - 32)
    wf = []
    for h in range(8): wf.append(chunk(np.arange(128 * h, 128 * h + 128)))
    for h in range(8): wf.append(chunk(rot128(128 * h)))
    for h in range(8): wf.append(chunk(np.arange(1024 + 128 * h, 1024 + 128 * h + 128)))
    for h in range(8): wf.append(chunk(rot128(1024 + 128 * h)))
    for h in range(8): wf.append(chunk(np.arange(3072 + 128 * h, 3072 + 128 * h + 128)))
    for h in range(8): wf.append(chunk(np.arange(4800 + 128 * h, 4800 + 128 * h + 128)))
    for j in range(16): wf.append(chunk(np.arange(5824 + 128 * j, 5824 + 128 * j + 128)))
    for j in range(3): wf.append(chunk(np.arange(4096 + 128 * j, 4096 + 128 * j + 128)))
    for j in range(2): wf.append(chunk(np.arange(4480 + 128 * j, 4480 + 128 * j + 128)))
    i64 = np.arange(64)
    r64 = np.where(i64 < 32, i64 + 32, i64 - 32)
    wf.append(chunk(np.concatenate([4736 + i64, 4736 + r64])))
    wf = np.ascontiguousarray(np.stack(wf), dtype=np.float32)
    wv = np.ascontiguousarray(np.stack([W[:, 2048 + g * 512: 2048 + (g + 1) * 512].reshape(8, 128, 512).transpose(1, 0, 2) for g in range(2)]), dtype=np.float32)
    uq = f("mla_w_uq")[0]; ukv = f("mla_w_ukv")[0]
    wuqn = np.ascontiguousarray(np.stack([uq[:, h * 192:h * 192 + 128].reshape(3, 128, 128).transpose(1, 0, 2) for h in range(8)]), dtype=np.float32)
    wuqp = np.ascontiguousarray(np.stack([uq[:, np.concatenate([h * 192 + 128 + i64, h * 192 + 128 + r64])].reshape(3, 128, 128).transpose(1, 0, 2) for h in range(8)]), dtype=np.float32)
    wukk = np.ascontiguousarray(np.stack([ukv[:, h * 256:h * 256 + 128].reshape(2, 128, 128).transpose(1, 0, 2) for h in range(8)]), dtype=np.float32)
    wukv = np.ascontiguousarray(np.stack([ukv[:, np.concatenate([h * 256 + 128 + np.arange(128) for h in range(4 * g, 4 * g + 4)])].reshape(2, 128, 512).transpose(1, 0, 2) for g in range(2)]), dtype=np.float32)
    wfin = np.ascontiguousarray(np.stack([f(k)[0].reshape(8, 128, 1024).transpose(1, 0, 2) for k in ("w_o_a", "w_o_b", "w_out", "ple_w_gate")]), dtype=np.float32)
    wpp = np.ascontiguousarray(f("ple_w_proj")[0].reshape(2, 128, 1024).transpose(1, 0, 2), dtype=np.float32)
    sm = np.zeros((128, 64), np.float32)
    sm[:, 0:8] = f("ln_emb_g").reshape(8, 128).T
    sm[:, 8:16] = f("ln_emb_b").reshape(8, 128).T
    sm[:, 16:32] = f("b_gate")[0].reshape(16, 128).T
    sm[:, 32:35] = f("mla_q_norm_g")[0].reshape(3, 128).T
    sm[:, 35:37] = f("mla_kv_norm_g")[0].reshape(2, 128).T
    sm[:, 37] = f("diff_subln_g")[0]
    inv = (np.float32(10000.0) ** (-(np.arange(0, 64, 2, dtype=np.float32)) / np.float32(64))).astype(np.float32)
    pp = np.arange(128)
    sm[:, 38] = inv[pp % 32]
    sm[:, 39] = np.where((pp % 64) < 32, -1.0, 1.0)
    sm[:, 40] = 1e-5
    sm[:, 41] = 1e-6
    rows = np.ascontiguousarray(np.stack([f("ln_emb_g"), f("ln_emb_b"), f("ple_b_gate")[0], f("ln_post_g")[0], f("ln_post_b")[0]]), dtype=np.float32)
    dl = np.ascontiguousarray(f("diff_lambda")[0].reshape(1, 256))
    return dict(wf=wf, wv=wv, wuqn=wuqn, wuqp=wuqp, wukk=wukk, wukv=wukv, wfin=wfin, wpp=wpp, sm=sm, rows=rows, dl=dl,
                ident=np.eye(128, dtype=np.float32), perm=_rot_perm())


def _core_maps(inp, shared):
    x = np.asarray(inp["x"], dtype=np.float32)
    p = np.asarray(inp["p"], dtype=np.float32)[0]
    pos = np.asarray(inp["positions"]).astype(np.int32)
    maps = []
    orders = []
    kk = np.arange(128)
    tri = np.where(kk[:, None] <= kk[None, :], 0.0, -30000.0).astype(np.float32)
    for c in range(8):
        b, hf = c // 2, c % 2
        order = [2 * s + hf for s in range(16)] + [2 * u + 1 - hf for u in range(16)]
        idx = np.concatenate([np.arange(t * 128, (t + 1) * 128) for t in order])
        m = dict(shared)
        m["x"] = np.ascontiguousarray(x[b][idx])
        m["pos"] = np.ascontiguousarray(pos[b][idx][None, :])
        m["pT"] = np.ascontiguousarray(p[b][idx[:NO]].T.reshape(2, 128, NO).transpose(1, 0, 2))
        m["masks"] = np.ascontiguousarray(np.concatenate([tri, np.full((128, 128), 0.0 if hf == 1 else -30000.0, np.float32)], axis=1))
        maps.append(m)
        orders.append(order)
    return maps, orders


_NC_CACHE = {}


def kernel(**inp):
    if "nc" not in _NC_CACHE:
        _NC_CACHE["nc"] = build_program()
    nc = _NC_CACHE["nc"]
    shared = _prep_shared(inp)
    maps, orders = _core_maps(inp, shared)
    res = run_bass_kernel_spmd(nc, maps, core_ids=list(range(8)))
    out = np.zeros((4, 4096, 1024), np.float32)
    for c in range(8):
        b = c // 2
        o = np.asarray(res.results[c]["out"], dtype=np.float32)
        for s in range(16):
            t = orders[c][s]
            out[b, t * 128:(t + 1) * 128] = o[s * 128:(s + 1) * 128]
    return out
```

```python
import numpy as np
import contextlib
import concourse.bass as bass
import concourse.mybir as mybir
from concourse.bass_utils import run_bass_kernel_spmd

F32 = mybir.dt.float32
BF16 = mybir.dt.bfloat16
F16 = mybir.dt.float16
I32 = mybir.dt.int32
AF = mybir.ActivationFunctionType
ALU = mybir.AluOpType
AX = mybir.AxisListType

ENGS = ("pe", "act", "dve", "pool", "sp")


class _Op:
    __slots__ = ("idx", "eng", "fn", "deps", "is_dma", "dsem", "dval", "flag", "rank", "predma")

    def __init__(self, idx, eng, fn, is_dma):
        self.idx = idx
        self.eng = eng
        self.fn = fn
        self.deps = set()
        self.is_dma = is_dma
        self.dsem = None
        self.dval = 0
        self.flag = False
        self.rank = 0
        self.predma = None


class Sched:
    def __init__(self, nc, n_dma_sems=40):
        self.nc = nc
        self.ops = []
        self.res = {}
        self.n_dma_sems = n_dma_sems
        self.n_dma = 0
        self.dma_hist = []
        self.bar = None
        self.bar_seen = set()
        self.last_on = {}
        self.dma_since_bar = []

    def _add(self, eng, fn, r, w, is_dma, x=()):
        op = _Op(len(self.ops), eng, fn, is_dma)
        self.ops.append(op)
        ek = ("dma", op.idx) if is_dma else eng
        deps = {}
        for key in x:
            st = self.res.get(key)
            if st is not None:
                for e, i in st[0].items():
                    deps.setdefault(i, False)
                for e, i in st[1].items():
                    deps.setdefault(i, False)
        for key in r:
            st = self.res.get(key)
            if st is not None:
                for e, i in st[0].items():
                    deps[i] = True
        for key in w:
            st = self.res.get(key)
            if st is not None:
                for e, i in st[0].items():
                    deps.setdefault(i, False)
                for e, i in st[1].items():
                    deps.setdefault(i, False)
        for i, raw in deps.items():
            d = self.ops[i]
            if (not d.is_dma) and (not is_dma) and d.eng == eng and not raw:
                continue
            if (not d.is_dma) and (not is_dma) and d.eng == eng and eng == "pe":
                continue
            op.deps.add(i)
        if self.bar is not None and eng not in self.bar_seen:
            self.bar_seen.add(eng)
            for i in self.bar:
                if i != op.idx:
                    d = self.ops[i]
                    if d.is_dma or d.eng != eng:
                        op.deps.add(i)
        for key in r:
            st = self.res.setdefault(key, ({}, {}))
            st[1][ek] = op.idx
        for key in w:
            self.res[key] = ({ek: op.idx}, {})
        for key in x:
            self.res[key] = ({ek: op.idx}, {})
        if is_dma:
            k = self.n_dma % self.n_dma_sems
            op.dsem = k
            op.dval = 16 * (self.n_dma // self.n_dma_sems + 1)
            if self.n_dma >= self.n_dma_sems:
                op.predma = self.dma_hist[self.n_dma - self.n_dma_sems]
            self.dma_hist.append(op.idx)
            self.n_dma += 1
            self.dma_since_bar.append(op.idx)
        else:
            self.last_on[eng] = op.idx
        return op

    def op(self, eng, fn, r=(), w=(), x=()):
        return self._add(eng, fn, r, w, False, x)

    def dma(self, fn, r=(), w=(), q="sp"):
        return self._add(q, fn, r, w, True)

    def barrier(self):
        self.bar = list(self.last_on.values()) + list(self.dma_since_bar)
        self.bar_seen = set()
        self.dma_since_bar = []

    def emit(self, final_waits=True):
        nc = self.nc
        ops = self.ops
        for op in ops:
            for i in op.deps:
                ops[i].flag = True
        cnt = {e: 0 for e in ENGS}
        for op in ops:
            if not op.is_dma and op.flag:
                cnt[op.eng] += 1
                op.rank = cnt[op.eng]
        per = {e: [] for e in ENGS}
        for op in ops:
            per[op.eng].append(op)
        import contextlib

        with contextlib.ExitStack() as st:
            esem = {e: st.enter_context(nc.semaphore("s_" + e)) for e in ENGS if e != "sp"}
            dsem = [st.enter_context(nc.semaphore("d_%d" % i)) for i in range(self.n_dma_sems)]
            block = st.enter_context(nc.Block())
            last_dma_vals = {}
            for op in ops:
                if op.is_dma:
                    last_dma_vals[op.dsem] = op.dval

            def run(eng_name, eng):
                waited = {}

                def wait(sem, key, val):
                    if waited.get(key, 0) >= val:
                        return
                    waited[key] = val
                    eng.wait_ge(sem, val)

                for op in per[eng_name]:
                    if op.predma is not None:
                        p = ops[op.predma]
                        wait(dsem[p.dsem], ("d", p.dsem), p.dval)
                    for i in sorted(op.deps):
                        d = ops[i]
                        if d.is_dma:
                            wait(dsem[d.dsem], ("d", d.dsem), d.dval)
                        else:
                            wait(esem[d.eng], ("e", d.eng), d.rank)
                    ins = op.fn(eng)
                    if op.is_dma:
                        ins.then_inc(dsem[op.dsem], 16)
                    elif op.flag:
                        ins.then_inc(esem[op.eng], 1)
                if eng_name == "sp" and final_waits:
                    for k, v in last_dma_vals.items():
                        wait(dsem[k], ("d", k), v)

            @block.tensor
            def _(e):
                run("pe", e)

            @block.scalar
            def _(e):
                run("act", e)

            @block.vector
            def _(e):
                run("dve", e)

            @block.gpsimd
            def _(e):
                run("pool", e)

            @block.sync
            def _(e):
                run("sp", e)


PI = float(np.pi)
C1 = 6.28125
C2 = float(2.0 * np.pi - 6.28125)
NT = 4096
NO = 2048
LAM_INIT = 0.2
ALPHA = float(2.0 ** 0.25)
LN_EPS_ = 1e-5


def build_program(stop_after=3, debug=False):
    nc = bass.Bass("TRN2", target_bir_lowering=False)
    S = Sched(nc, n_dma_sems=48)

    def din(name, shape, dt=F32):
        return nc.dram_tensor(name, list(shape), dt, kind="ExternalInput")

    x_in = din("x", [NT, 1024])
    pos_in = din("pos", [1, NT], I32)
    pT_in = din("pT", [128, 2, NO])
    wf_in = din("wf", [70, 128, 8, 128])
    wv_in = din("wv", [2, 128, 8, 512])
    wuqn_in = din("wuqn", [8, 128, 3, 128])
    wuqp_in = din("wuqp", [8, 128, 3, 128])
    wukk_in = din("wukk", [8, 128, 2, 128])
    wukv_in = din("wukv", [2, 128, 2, 512])
    wfin_in = din("wfin", [4, 128, 8, 1024])
    wpp_in = din("wpp", [128, 2, 1024])
    sm_in = din("sm", [128, 64])
    rows_in = din("rows", [5, 1024])
    dl_in = din("dl", [1, 256])
    ident_in = din("ident", [128, 128])
    mask_in = din("masks", [128, 256])
    perm_in = din("perm", [128, 128])
    out_t = nc.dram_tensor("out", [NO, 1024], F32, kind="ExternalOutput")

    def scr(name, shape):
        if debug:
            return nc.dram_tensor(name, list(shape), BF16, kind="ExternalOutput")
        return nc.dram_tensor(name, list(shape), BF16)

    Qa = scr("s_qa", [8, 128, NO]); Ka = scr("s_ka", [8, 128, NT]); Va = scr("s_va", [NT, 1024])
    Za = scr("s_za", [8, 128, NO]); Zb = scr("s_zb", [8, 128, NO]); Sg = scr("s_sg", [16, 128, NO])
    Qn = scr("s_qn", [8, 128, NO]); Qp = scr("s_qp", [8, 64, NO]); Kn = scr("s_kn", [8, 128, NT])
    Kp = scr("s_kp", [64, NT]); Vb = scr("s_vb", [NT, 1024])
    Ga = scr("s_ga", [8, 128, NO]); Gb = scr("s_gb", [8, 128, NO])

    ARENA = 206000
    arena = nc.alloc_sbuf_tensor("arena", [128, ARENA // 4], F32)
    views = {F32: arena, BF16: arena.bitcast(BF16), F16: arena.bitcast(F16), I32: arena.bitcast(I32)}
    esz = {F32: 4, BF16: 2, F16: 2, I32: 4}
    cur = [0]

    def alloc(shape, dt):
        n = int(np.prod(shape))
        nb = (n * esz[dt] + 63) // 64 * 64
        off = cur[0]
        cur[0] += nb
        assert cur[0] <= ARENA, ("SBUF arena overflow", cur[0])
        v = views[dt][:, off // esz[dt]: off // esz[dt] + n]
        if len(shape) == 2:
            v = v.rearrange("p (a b) -> p a b", a=shape[0])
        elif len(shape) == 3:
            v = v.rearrange("p (a b c) -> p a b c", a=shape[0], b=shape[1])
        return v

    pst = nc.alloc_psum_tensor("pst", [128, 8, 512], F32)
    pst16 = pst.bitcast(BF16)

    def B(i):
        return pst[:, i, :]

    def BK(i):
        return ("B", i)

    uid = [0]

    def U(p="u"):
        uid[0] += 1
        return "%s%d" % (p, uid[0])

    sm = alloc([64], F32)
    identf = alloc([128], F32)
    identb = alloc([128], BF16)
    onesb = alloc([128], BF16)
    masks = alloc([256], BF16)
    maskf = alloc([256], F32)
    stats = alloc([32, 4], F32)
    neglam = alloc([4], F32)
    gsub = alloc([1], F32)
    S.dma(lambda e: e.dma_start(out=sm, in_=sm_in[:, :]), w=["sm"])
    S.dma(lambda e: e.dma_start(out=identf, in_=ident_in[:, :]), w=["identf"])
    S.dma(lambda e: e.dma_start(out=maskf, in_=mask_in[:, :]), w=["maskf"])
    S.op("dve", lambda e: e.tensor_copy(out=identb, in_=identf), r=["identf"], w=["identb"])
    S.op("dve", lambda e: e.tensor_copy(out=masks, in_=maskf), r=["maskf"], w=["masks"])
    S.op("dve", lambda e: e.memset(onesb, 1.0), w=["onesb"])
    permb = alloc([128], BF16)
    S.dma(lambda e: e.dma_start(out=permb, in_=perm_in[:, :]), w=["permb"], q="pool")
    base0 = cur[0]

    xnT = alloc([8, NT], BF16)
    ctab = alloc([NT], F16)
    stab = alloc([NT], F16)
    base1 = cur[0]

    dlb = alloc([256], F32)
    dlp = alloc([128], F32)
    dls = alloc([4], F32)
    S.dma(lambda e: e.dma_start(out=dlb, in_=dl_in.ap().broadcast_to([128, 256])), w=["dlb"])
    S.op("dve", lambda e: e.tensor_tensor(out=dlp[:, 0:64], in0=dlb[:, 0:64], in1=dlb[:, 64:128], op=ALU.mult), r=["dlb"], w=["dlp0"])
    S.op("dve", lambda e: e.tensor_tensor(out=dlp[:, 64:128], in0=dlb[:, 128:192], in1=dlb[:, 192:256], op=ALU.mult), r=["dlb"], w=["dlp1"])
    S.op("dve", lambda e: e.reduce_sum(out=dls[:, 0:1], in_=dlp[:, 0:64], axis=AX.X), r=["dlp0"], w=["dls0"])
    S.op("dve", lambda e: e.reduce_sum(out=dls[:, 1:2], in_=dlp[:, 64:128], axis=AX.X), r=["dlp1"], w=["dls1"])
    S.op("act", lambda e: e.activation(out=dls[:, 2:4], in_=dls[:, 0:2], func=AF.Exp), r=["dls0", "dls1"], w=["dle"])
    S.op("dve", lambda e: e.tensor_tensor(out=neglam[:, 0:1], in0=dls[:, 3:4], in1=dls[:, 2:3], op=ALU.subtract), r=["dle"], w=["nl0"])
    S.op("dve", lambda e: e.tensor_scalar(out=neglam[:, 1:2], in0=neglam[:, 0:1], scalar1=-LAM_INIT, scalar2=None, op0=ALU.add), r=["nl0"], w=["neglam"])
    S.op("dve", lambda e: e.tensor_scalar(out=gsub, in0=sm[:, 37:38], scalar1=1.0 - LAM_INIT, scalar2=None, op0=ALU.mult), r=["sm"], w=["gsub"])

    posi = alloc([NT], I32)
    ang = alloc([NT], F32)
    kf = alloc([NT], F32)
    ki = alloc([NT], I32)
    rr = alloc([NT], F32)
    S.dma(lambda e: e.dma_start(out=posi, in_=pos_in.ap().broadcast_to([128, NT])), w=["posi"])
    S.op("dve", lambda e: e.tensor_copy(out=ang, in_=posi), r=["posi"], w=["angf"])
    S.op("dve", lambda e: e.tensor_scalar(out=ang, in0=ang, scalar1=sm[:, 38:39], scalar2=None, op0=ALU.mult), r=["angf", "sm"], w=["ang"])
    S.op("dve", lambda e: e.tensor_scalar(out=kf, in0=ang, scalar1=float(1.0 / (2 * np.pi)), scalar2=None, op0=ALU.mult), r=["ang"], w=["kf0"])
    S.op("dve", lambda e: e.tensor_copy(out=ki, in_=kf), r=["kf0"], w=["ki"])
    S.op("dve", lambda e: e.tensor_copy(out=kf, in_=ki), r=["ki"], w=["kf"])
    S.op("dve", lambda e: e.scalar_tensor_tensor(out=rr, in0=kf, scalar=-C1, in1=ang, op0=ALU.mult, op1=ALU.add), r=["kf", "ang"], w=["rr1"])
    S.op("dve", lambda e: e.scalar_tensor_tensor(out=rr, in0=kf, scalar=-C2, in1=rr, op0=ALU.mult, op1=ALU.add), r=["kf", "rr1"], w=["rr2"])
    S.op("dve", lambda e: e.tensor_scalar(out=kf, in0=rr, scalar1=PI, scalar2=None, op0=ALU.is_gt), r=["rr2"], w=["m1"])
    S.op("dve", lambda e: e.scalar_tensor_tensor(out=rr, in0=kf, scalar=-2 * PI, in1=rr, op0=ALU.mult, op1=ALU.add), r=["m1", "rr2"], w=["rr3"])
    S.op("dve", lambda e: e.tensor_scalar(out=kf, in0=rr, scalar1=-PI, scalar2=None, op0=ALU.is_lt), r=["rr3"], w=["m2"])
    S.op("dve", lambda e: e.scalar_tensor_tensor(out=rr, in0=kf, scalar=2 * PI, in1=rr, op0=ALU.mult, op1=ALU.add), r=["m2", "rr3"], w=["rs"])
    S.op("dve", lambda e: e.tensor_scalar(out=rr, in0=rr, scalar1=PI, scalar2=-PI, op0=ALU.min, op1=ALU.max), r=["rs"], w=["rs2"])
    S.op("act", lambda e: e.activation(out=stab, in_=rr, func=AF.Sin, scale=sm[:, 39:40]), r=["rs2", "sm"], w=["stab"])
    S.op("dve", lambda e: e.tensor_scalar(out=ang, in0=rr, scalar1=PI / 2, scalar2=None, op0=ALU.add), r=["rs2", "ang"], w=["rc"])
    S.op("dve", lambda e: e.tensor_scalar(out=kf, in0=ang, scalar1=PI, scalar2=None, op0=ALU.is_gt), r=["rc"], w=["m3"])
    S.op("dve", lambda e: e.scalar_tensor_tensor(out=ang, in0=kf, scalar=-2 * PI, in1=ang, op0=ALU.mult, op1=ALU.add), r=["m3", "rc"], w=["rc2"])
    S.op("dve", lambda e: e.tensor_scalar(out=ang, in0=ang, scalar1=PI, scalar2=-PI, op0=ALU.min, op1=ALU.max), r=["rc2"], w=["rc3"])
    S.op("act", lambda e: e.activation(out=ctab, in_=ang, func=AF.Sin), r=["rc3"], w=["ctab"])
    S.barrier()
    cur[0] = base1

    xb = [alloc([1024], F32) for _ in range(3)]
    xh = [alloc([1024], F32) for _ in range(2)]
    bst = [alloc([12], F32) for _ in range(2)]
    def p0_L(t):
        xt = xb[t % 3]
        xk = "xb%d" % (t % 3)
        S.dma(lambda e, xt=xt, t=t: e.dma_start(out=xt, in_=x_in[t * 128:(t + 1) * 128, :]), w=[xk])

    def p0_A(t):
        xt = xb[t % 3]
        xk = "xb%d" % (t % 3)
        st_ = bst[t % 2]
        sk = "bst%d" % (t % 2)
        S.op("dve", lambda e, st_=st_, xt=xt: e.bn_stats(out=st_[:, 0:6], in_=xt[:, 0:512]), r=[xk], w=[sk + "a"])
        S.op("dve", lambda e, st_=st_, xt=xt: e.bn_stats(out=st_[:, 6:12], in_=xt[:, 512:1024]), r=[xk], w=[sk + "b"])
        S.op("dve", lambda e, st_=st_, t=t: e.bn_aggr(out=stats[:, t, 0:2], in_=st_[:, 0:12]), r=[sk + "a", sk + "b"], w=["mv%d" % t])
        S.op("act", lambda e, t=t: e.activation(out=stats[:, t, 2:3], in_=stats[:, t, 1:2], func=AF.Ln, bias=sm[:, 40:41], scale=1.0), r=["mv%d" % t, "sm"], w=["lv%d" % t])
        S.op("act", lambda e, t=t: e.activation(out=stats[:, t, 3:4], in_=stats[:, t, 2:3], func=AF.Exp, scale=-0.5), r=["lv%d" % t], w=["rs%d" % t])
        xhh = xh[t % 2]
        hk = "xh%d" % (t % 2)
        S.op("dve", lambda e, xhh=xhh, xt=xt, t=t: e.tensor_scalar(out=xhh, in0=xt, scalar1=stats[:, t, 0:1], scalar2=stats[:, t, 3:4],
                                                               op0=ALU.subtract, op1=ALU.mult), r=[xk, "mv%d" % t, "rs%d" % t], w=[hk])
        return None

    def p0_B(t):
        xhh = xh[t % 2]
        hk = "xh%d" % (t % 2)
        b0 = 2 * (t % 4)
        for c in range(8):
            bk = b0 + c // 4
            S.op("pe", lambda e, bk=bk, c=c, xhh=xhh: e.transpose(out=B(bk)[:, (c % 4) * 128:(c % 4 + 1) * 128], in_=xhh[:, c * 128:(c + 1) * 128], identity=identf),
                 r=[hk, "identf"], x=[BK(bk)])
        for c in range(8):
            bk = b0 + c // 4
            if c % 2 == 0:
                S.op("act", lambda e, bk=bk, c=c, t=t: e.activation(out=xnT[:, c, t * 128:(t + 1) * 128], in_=B(bk)[:, (c % 4) * 128:(c % 4 + 1) * 128],
                                                                  func=AF.Identity, scale=sm[:, c:c + 1], bias=sm[:, 8 + c:9 + c]),
                     r=["sm"], w=[("xnTw", t, c)], x=[BK(bk)])
            else:
                S.op("dve", lambda e, bk=bk, c=c, t=t: e.tensor_scalar(out=xnT[:, c, t * 128:(t + 1) * 128], in0=B(bk)[:, (c % 4) * 128:(c % 4 + 1) * 128],
                                                                     scalar1=sm[:, c:c + 1], scalar2=sm[:, 8 + c:9 + c], op0=ALU.mult, op1=ALU.add),
                     r=["sm"], w=[("xnTw", t, c)], x=[BK(bk)])

    p0_L(0)
    p0_L(1)
    p0_A(0)
    for t in range(32):
        if t + 2 < 32:
            p0_L(t + 2)
        if t + 1 < 32:
            p0_A(t + 1)
        p0_B(t)
    cur[0] = base1
    if stop_after == 0:
        S.barrier()
        dbg = alloc([1024], F32)
        S.op("dve", lambda e: e.tensor_copy(out=dbg, in_=xnT[:, 0, 0:1024]), r=[("xnT", i) for i in range(8)], w=["dbg"])
        S.dma(lambda e: e.dma_start(out=out_t[0:128, :], in_=dbg), r=["dbg"])
        S.emit()
        return nc

    S.barrier()
    NW = 6
    wsl = [alloc([8, 128], BF16) for _ in range(NW)]
    wcnt = [0]

    def load_w(j):
        k = wcnt[0] % NW
        wcnt[0] += 1
        key = "wsl%d" % k
        S.dma(lambda e, k=k, j=j: e.dma_start(out=wsl[k], in_=wf_in[j], max_dma_last_dim=4096), w=[key], q="pool")
        return wsl[k], key

    t1b = [alloc([512], F32) for _ in range(2)]
    t2b = [alloc([512], F32) for _ in range(2)]
    ob = [alloc([512], BF16) for _ in range(4)]
    ocnt = [0]
    bcnt = [0]

    def nb():
        b = bcnt[0] % 8
        bcnt[0] += 1
        return b

    def proj_fm(wt, wkey, tb, bank, M=128, woff=0):
        for c in range(8):
            S.op("pe", lambda e, c=c: e.matmul(B(bank)[0:M, :], lhsT=wt[:, c, woff:woff + M], rhs=xnT[:, c, tb * 512:(tb + 1) * 512],
                                               start=(c == 0), stop=(c == 7)),
                 r=[wkey, ("xnT", tb)], x=[BK(bank)])

    def rope_out(bankA, bankB, tb, M, dst):
        i = ocnt[0]
        ocnt[0] += 1
        t1 = t1b[i % 2]; t2 = t2b[i % 2]; o = ob[i % 4]
        k1 = "t1_%d" % (i % 2); k2 = "t2_%d" % (i % 2); ko = "ob%d" % (i % 4)
        S.op("dve", lambda e: e.tensor_tensor(out=t1[0:M, :], in0=B(bankA)[0:M, :], in1=ctab[0:M, tb * 512:(tb + 1) * 512], op=ALU.mult),
             r=["ctab"], w=[k1], x=[BK(bankA)])
        S.op("dve", lambda e: e.tensor_tensor(out=t2[0:M, :], in0=B(bankB)[0:M, :], in1=stab[0:M, tb * 512:(tb + 1) * 512], op=ALU.mult),
             r=["stab"], w=[k2], x=[BK(bankB)])
        S.op("pool", lambda e: e.tensor_tensor(out=o[0:M, :], in0=t1[0:M, :], in1=t2[0:M, :], op=ALU.add), r=[k1, k2], w=[ko])
        S.dma(lambda e: e.dma_start(out=dst, in_=o[0:M, :]), r=[ko])

    def simple_out(bank, dst, kind, M=128, bias=None):
        i = ocnt[0]
        ocnt[0] += 1
        o = ob[i % 4]; ko = "ob%d" % (i % 4)
        if kind == "silu":
            S.op("act", lambda e: e.activation(out=o[0:M, :], in_=B(bank)[0:M, :], func=AF.Silu), w=[ko], x=[BK(bank)])
        elif kind == "sig":
            S.op("act", lambda e: e.activation(out=o[0:M, :], in_=B(bank)[0:M, :], func=AF.Sigmoid, bias=bias), r=["sm"], w=[ko], x=[BK(bank)])
        elif kind == "copy_act":
            S.op("act", lambda e: e.activation(out=o[0:M, :], in_=B(bank)[0:M, :], func=AF.Copy), w=[ko], x=[BK(bank)])
        else:
            S.op("dve", lambda e: e.tensor_copy(out=o[0:M, :], in_=B(bank)[0:M, :]), w=[ko], x=[BK(bank)])
        S.dma(lambda e: e.dma_start(out=dst, in_=o[0:M, :]), r=[ko])

    tasks = []

    hbuf = [alloc([512], BF16) for _ in range(3)]
    hcnt = [0]
    rpend = [None]

    def rflush():
        if rpend[0] is None:
            return
        bA, tb, hb, hk, dst = rpend[0]
        rpend[0] = None
        bB = nb()
        S.op("pe", lambda e: e.matmul(B(bB), lhsT=permb, rhs=hb, start=True, stop=True), r=[hk, "permb"], x=[BK(bB)])
        rope_out(bA, bB, tb, 128, dst)

    def t_rope(ja, jr, ntb, dstf):
        def ld():
            return [load_w(ja)]
        def cp_(ws):
            (wa, ka_) = ws[0]
            for tb in range(ntb):
                bA = nb()
                proj_fm(wa, ka_, tb, bA)
                hi = hcnt[0] % 3; hcnt[0] += 1
                hb = hbuf[hi]; hk = "hb%d" % hi
                S.op("act", lambda e, hb=hb, bA=bA: e.activation(out=hb, in_=B(bA), func=AF.Copy), w=[hk], x=[BK(bA)])
                rflush()
                rpend[0] = (bA, tb, hb, hk, dstf(tb))
        tasks.append((ld, cp_))

    def t_simple(j, dstf, kind, bias=None):
        def ld():
            return [load_w(j)]
        def cp_(ws):
            wa, ka_ = ws[0]
            for tb in range(4):
                bA = nb(); proj_fm(wa, ka_, tb, bA)
                rflush()
                simple_out(bA, dstf(tb), kind, bias=bias)
        tasks.append((ld, cp_))

    for h in range(8):
        t_rope(h, 8 + h, 4, lambda tb, h=h: Qa[h, :, tb * 512:(tb + 1) * 512])
        t_rope(16 + h, 24 + h, 8, lambda tb, h=h: Ka[h, :, tb * 512:(tb + 1) * 512])
    for h in range(8):
        t_simple(32 + h, lambda tb, h=h: Za[h, :, tb * 512:(tb + 1) * 512], "silu")
        t_simple(40 + h, lambda tb, h=h: Zb[h, :, tb * 512:(tb + 1) * 512], "silu")
    for j in range(16):
        t_simple(48 + j, lambda tb, j=j: Sg[j, :, tb * 512:(tb + 1) * 512], "sig", bias=sm[:, 16 + j:17 + j])
    hnd = {}
    for i in range(min(2, len(tasks))):
        hnd[i] = tasks[i][0]()
    for i in range(len(tasks)):
        if i + 2 < len(tasks):
            hnd[i + 2] = tasks[i + 2][0]()
        tasks[i][1](hnd.pop(i))
    rflush()
    wvs2 = [alloc([8, 512], BF16) for _ in range(2)]
    cp = [0]
    for g in range(2):
        S.dma(lambda e, g=g: e.dma_start(out=wvs2[g], in_=wv_in[g], max_dma_last_dim=4096), w=["wvs%d" % g], q="pool")
    for g in range(2):
        wvs = wvs2[g]
        for tt in range(32):
            bA = nb()
            for c in range(8):
                S.op("pe", lambda e, c=c, tt=tt, bA=bA, wvs=wvs: e.matmul(B(bA), lhsT=xnT[:, c, tt * 128:(tt + 1) * 128], rhs=wvs[:, c, :], start=(c == 0), stop=(c == 7)),
                     r=["wvs%d" % g, ("xnT", tt // 4)], x=[BK(bA)])
            cp[0] += 1
            simple_out(bA, Va[tt * 128:(tt + 1) * 128, g * 512:(g + 1) * 512], "copy_act" if cp[0] % 2 else "copy")
    cqn = alloc([3, NO], BF16)
    ckvn = alloc([2, NT], BF16)
    latf = alloc([3, 512], F32)
    sqb = alloc([3, 512], BF16)
    lnv = alloc([512], F32)
    rstd = alloc([512], F32)

    def latent(jlist, ntb, dstn, gcol, nfeat):
        ws = [load_w(j) for j in jlist]
        n = len(jlist)
        for tb in range(ntb):
            banks = []
            for q in range(n):
                bA = nb(); banks.append(bA)
                proj_fm(ws[q][0], ws[q][1], tb, bA)
            for q in range(n):
                S.op("dve", lambda e, q=q, bA=banks[q]: e.tensor_copy(out=latf[:, q, :], in_=B(bA)), w=[("latf", q)], x=[BK(banks[q])])
                S.op("pool", lambda e, q=q: e.tensor_tensor(out=sqb[:, q, :], in0=latf[:, q, :], in1=latf[:, q, :], op=ALU.mult), r=[("latf", q)], w=[("sqb", q)])
            bM = nb()
            for q in range(n):
                S.op("pe", lambda e, q=q, bM=bM: e.matmul(B(bM), lhsT=onesb, rhs=sqb[:, q, :], start=(q == 0), stop=(q == n - 1)),
                     r=["onesb", ("sqb", q)], x=[BK(bM)])
            S.op("act", lambda e, bM=bM: e.activation(out=lnv, in_=B(bM), func=AF.Ln, scale=1.0 / nfeat, bias=sm[:, 41:42]), r=["sm"], w=["lnv"], x=[BK(bM)])
            S.op("act", lambda e: e.activation(out=rstd, in_=lnv, func=AF.Exp, scale=-0.5), r=["lnv"], w=["rstd"])
            for q in range(n):
                S.op("dve", lambda e, q=q, tb=tb: e.scalar_tensor_tensor(out=dstn[:, q, tb * 512:(tb + 1) * 512], in0=latf[:, q, :], scalar=sm[:, gcol + q:gcol + q + 1],
                                                                       in1=rstd, op0=ALU.mult, op1=ALU.mult),
                     r=[("latf", q), "rstd", "sm"], w=[("lat", id(dstn), q, tb)])

    latent([64, 65, 66], 4, cqn, 32, 384.0)
    latent([67, 68], 8, ckvn, 35, 256.0)
    LATK = [("lat", id(cqn), q, tb) for q in range(3) for tb in range(4)]
    LATKV = [("lat", id(ckvn), q, tb) for q in range(2) for tb in range(8)]
    wa, ka_ = load_w(69)
    for tb in range(8):
        bA = nb(); bB = nb()
        proj_fm(wa, ka_, tb, bA, M=64, woff=0); proj_fm(wa, ka_, tb, bB, M=64, woff=64)
        rope_out(bA, bB, tb, 64, Kp[:, tb * 512:(tb + 1) * 512])
    wqn = [alloc([3, 128], BF16) for _ in range(2)]
    wqp = [alloc([3, 128], BF16) for _ in range(2)]
    wkk = [alloc([2, 128], BF16) for _ in range(2)]
    def ld_up(h):
        s2 = h % 2
        S.dma(lambda e, h=h, s2=s2: e.dma_start(out=wqn[s2], in_=wuqn_in[h], max_dma_last_dim=4096), w=["wqn%d" % s2], q="pool")
        S.dma(lambda e, h=h, s2=s2: e.dma_start(out=wqp[s2], in_=wuqp_in[h], max_dma_last_dim=4096), w=["wqp%d" % s2], q="pool")
        S.dma(lambda e, h=h, s2=s2: e.dma_start(out=wkk[s2], in_=wukk_in[h], max_dma_last_dim=4096), w=["wkk%d" % s2], q="pool")
    wvv2 = [alloc([2, 512], BF16) for _ in range(2)]
    for g in range(2):
        S.dma(lambda e, g=g: e.dma_start(out=wvv2[g], in_=wukv_in[g], max_dma_last_dim=4096), w=["wvv%d" % g], q="pool")
    ld_up(0)
    for h in range(8):
        s2 = h % 2
        if h + 1 < 8:
            ld_up(h + 1)
        for tb in range(4):
            bA = nb()
            for j in range(3):
                S.op("pe", lambda e, j=j, bA=bA, tb=tb, s2=s2: e.matmul(B(bA), lhsT=wqn[s2][:, j, :], rhs=cqn[:, j, tb * 512:(tb + 1) * 512], start=(j == 0), stop=(j == 2)),
                     r=["wqn%d" % s2] + LATK, x=[BK(bA)])
            simple_out(bA, Qn[h, :, tb * 512:(tb + 1) * 512], "copy")
            bA = nb(); bB = nb()
            for j in range(3):
                S.op("pe", lambda e, j=j, bA=bA, tb=tb, s2=s2: e.matmul(B(bA)[0:64, :], lhsT=wqp[s2][:, j, 0:64], rhs=cqn[:, j, tb * 512:(tb + 1) * 512], start=(j == 0), stop=(j == 2)),
                     r=["wqp%d" % s2] + LATK, x=[BK(bA)])
            for j in range(3):
                S.op("pe", lambda e, j=j, bB=bB, tb=tb, s2=s2: e.matmul(B(bB)[0:64, :], lhsT=wqp[s2][:, j, 64:128], rhs=cqn[:, j, tb * 512:(tb + 1) * 512], start=(j == 0), stop=(j == 2)),
                     r=["wqp%d" % s2] + LATK, x=[BK(bB)])
            rope_out(bA, bB, tb, 64, Qp[h, :, tb * 512:(tb + 1) * 512])
        for tb in range(8):
            bA = nb()
            for j in range(2):
                S.op("pe", lambda e, j=j, bA=bA, tb=tb, s2=s2: e.matmul(B(bA), lhsT=wkk[s2][:, j, :], rhs=ckvn[:, j, tb * 512:(tb + 1) * 512], start=(j == 0), stop=(j == 1)),
                     r=["wkk%d" % s2] + LATKV, x=[BK(bA)])
            simple_out(bA, Kn[h, :, tb * 512:(tb + 1) * 512], "copy_act")
    for g in range(2):
        wvv = wvv2[g]
        for tt in range(32):
            bA = nb()
            for j in range(2):
                S.op("pe", lambda e, j=j, tt=tt, bA=bA, wvv=wvv: e.matmul(B(bA), lhsT=ckvn[:, j, tt * 128:(tt + 1) * 128], rhs=wvv[:, j, :], start=(j == 0), stop=(j == 1)),
                     r=["wvv%d" % g] + LATKV, x=[BK(bA)])
            cp[0] += 1
            simple_out(bA, Vb[tt * 128:(tt + 1) * 128, g * 512:(g + 1) * 512], "copy_act" if cp[0] % 2 else "copy")

    S.barrier()
    cur[0] = base0
    NB2 = 2
    qT = [alloc([NO], BF16) for _ in range(NB2)]
    kT = [alloc([NT], BF16) for _ in range(NB2)]
    vS = [alloc([32, 128], BF16) for _ in range(NB2)]
    zT = [alloc([NO], BF16) for _ in range(NB2)]
    qpT = [alloc([NO], BF16) for _ in range(NB2)]
    kpT = alloc([NT], BF16)
    NP = 4
    P1 = [alloc([512], BF16) for _ in range(NP)]
    P2 = [alloc([512], BF16) for _ in range(NP)]
    fr1 = alloc([512], F32); fo1 = alloc([512], F32); fr2 = alloc([512], F32); ft2 = alloc([512], F32)
    fo = alloc([512], F32); fsq = alloc([512], BF16); fln = alloc([512], F32); frs = alloc([512], F32); fu = alloc([512], F32)
    gob = [alloc([512], BF16) for _ in range(2)]
    gcnt = [0]
    cur[0] = max(cur[0], 98560)
    base2 = cur[0]
    wfin = [alloc([8, 1024], BF16) for _ in range(4)]
    wpp = alloc([2, 1024], BF16)
    pTs = alloc([2, NO], BF16)
    rows = alloc([5, 1024], F32)

    def prefetch_p3():
        for i in range(4):
            for hh in range(2):
                S.dma(lambda e, i=i, hh=hh: e.dma_start(out=wfin[i][:, 4 * hh:4 * hh + 4, :], in_=wfin_in[i, :, 4 * hh:4 * hh + 4, :], max_dma_last_dim=4096), w=[("wfin", i, hh)], q="pool")
        S.dma(lambda e: e.dma_start(out=wpp, in_=wpp_in[:, :, :], max_dma_last_dim=4096), w=["wpp"], q="pool")
        S.dma(lambda e: e.dma_start(out=pTs, in_=pT_in[:, :, :], max_dma_last_dim=4096), w=["pTs"], q="pool")
        for i in range(5):
            S.dma(lambda e, i=i: e.dma_start(out=rows[:, i, :], in_=rows_in[i:i + 1, :].broadcast_to([128, 1024])), w=[("rows", i)], q="pool")

    def ktiles(g):
        lst = []
        for t in range(4 * g):
            lst.append((t, 0, None))
            lst.append((16 + t, 0, None))
        for m in range(4):
            lst.append((4 * g + m, 128 * m, 0))
            lst.append((16 + 4 * g + m, 128 * m, 1))
        return lst

    SC_A = float(64 ** -0.5)
    SC_B = float(192 ** -0.5)
    pcnt = [0]

    def load_head(h, mixer):
        s = h % NB2
        kq = "qT%d" % s; kk = "kT%d" % s; kv = "vS%d" % s; kz = "zT%d" % s; kqp = "qpT%d" % s
        if mixer == 0:
            S.dma(lambda e: e.dma_start(out=qT[s], in_=Qa[h]), w=[kq])
            S.dma(lambda e: e.dma_start(out=kT[s], in_=Ka[h]), w=[kk])
            for qq in range(4):
                S.dma(lambda e, qq=qq: e.dma_start(out=vS[s][:, 8 * qq:8 * qq + 8, :], in_=Va.ap().rearrange("(t p) n -> p t n", p=128)[:, 8 * qq:8 * qq + 8, h * 128:(h + 1) * 128]), w=[kv + "_%d" % qq])
            S.dma(lambda e: e.dma_start(out=zT[s], in_=Za[h]), w=[kz])
        else:
            S.dma(lambda e: e.dma_start(out=qT[s], in_=Qn[h]), w=[kq])
            S.dma(lambda e: e.dma_start(out=kT[s], in_=Kn[h]), w=[kk])
            for qq in range(4):
                S.dma(lambda e, qq=qq: e.dma_start(out=vS[s][:, 8 * qq:8 * qq + 8, :], in_=Vb.ap().rearrange("(t p) n -> p t n", p=128)[:, 8 * qq:8 * qq + 8, h * 128:(h + 1) * 128]), w=[kv + "_%d" % qq])
            S.dma(lambda e: e.dma_start(out=zT[s], in_=Zb[h]), w=[kz])
            S.dma(lambda e: e.dma_start(out=qpT[s][0:64, :], in_=Qp[h]), w=[kqp])

    def attention(h, mixer, nxt=None):
        s = h % NB2
        kq = "qT%d" % s; kk = "kT%d" % s; kv = "vS%d" % s; kz = "zT%d" % s; kqp = "qpT%d" % s
        nmap = 2 if mixer == 0 else 1
        for g in range(4):
            kts = ktiles(g)
            q0 = 512 * g

            def qk(i):
                slot, c0, mk = kts[i]
                for mp in range(nmap):
                    bank = (2 * mp + (i % 2)) if mixer == 0 else (i % 4)
                    nomask = (mk is None)
                    if mixer == 0:
                        rows = slice(64 * mp, 64 * mp + 64)
                        S.op("pe", lambda e, rows=rows, bank=bank, slot=slot, c0=c0, q0=q0, nomask=nomask: e.matmul(B(bank)[:, c0:512], lhsT=kT[s][rows, slot * 128:(slot + 1) * 128],
                                                                                            rhs=qT[s][rows, q0 + c0:q0 + 512], start=True, stop=True),
                             r=[kq, kk], x=[BK(bank)])
                    else:
                        S.op("pe", lambda e, bank=bank, slot=slot, c0=c0, q0=q0: e.matmul(B(bank)[:, c0:512], lhsT=kT[s][:, slot * 128:(slot + 1) * 128],
                                                                                 rhs=qT[s][:, q0 + c0:q0 + 512], start=True, stop=False),
                             r=[kq, kk], x=[BK(bank)])
                        S.op("pe", lambda e, bank=bank, slot=slot, c0=c0, q0=q0, nomask=nomask: e.matmul(B(bank)[:, c0:512], lhsT=kpT[:, slot * 128:(slot + 1) * 128],
                                                                                 rhs=qpT[s][:, q0 + c0:q0 + 512], start=False, stop=True),
                             r=[kqp, "kpT", "kpz", "qpz%d" % s], x=[BK(bank)])

            def expo(i):
                slot, c0, mk = kts[i]
                res = []
                for mp in range(nmap):
                    bank = (2 * mp + (i % 2)) if mixer == 0 else (i % 4)
                    pc = pcnt[0] % NP
                    Pb = (P1 if mp == 0 else P2)[pc]
                    pk = "P%d_%d" % (mp, pc)
                    S.op("act", lambda e, bank=bank, Pb=Pb, c0=c0: e.activation(out=Pb[:, c0:512], in_=B(bank)[:, c0:512], func=AF.Exp,
                                                                              scale=(SC_A if mixer == 0 else SC_B)),
                         w=[pk], x=[BK(bank)])
                    if mk is not None:
                        S.op("dve", lambda e, Pb=Pb, c0=c0, mk=mk: e.tensor_tensor(out=Pb[:, c0:c0 + 128], in0=Pb[:, c0:c0 + 128], in1=masks[:, mk * 128:(mk + 1) * 128], op=ALU.mult),
                             r=["masks"], x=[pk])
                    res.append((Pb, pk))
                pcnt[0] += 1
                return res

            def pv(i, pbs):
                slot, c0, mk = kts[i]
                first = (i == 0); last = (i == len(kts) - 1)
                for mp in range(nmap):
                    Pb, pk = pbs[mp]
                    bo = 4 + mp; bl = 6 + mp
                    S.op("pe", lambda e, Pb=Pb, bo=bo, slot=slot, c0=c0: e.matmul(B(bo)[:, c0:512], lhsT=vS[s][:, slot, :], rhs=Pb[:, c0:512], start=first, stop=last),
                         r=[pk] + [kv + "_%d" % qq for qq in range(4)], x=[BK(bo)])
                    S.op("pe", lambda e, Pb=Pb, bl=bl, c0=c0: e.matmul(B(bl)[:, c0:512], lhsT=onesb, rhs=Pb[:, c0:512], start=first, stop=last),
                         r=[pk, "onesb"], x=[BK(bl)])

            LA = 2
            for i0 in range(LA):
                qk(i0)
            for i in range(len(kts)):
                pbs = expo(i)
                if i == 5 and pend2[0] is not None:
                    pend2[0](i)
                    pend2[0] = None
                if i + LA < len(kts):
                    qk(i + LA)
                pv(i, pbs)
                if i == 1 and pend[0] is not None:
                    pend[0]()
                    pend[0] = None
                if i == 6 and pend3[0] is not None:
                    pend3[0]()
                    pend3[0] = None
                if i == 7 and g == 0 and nxt is not None:
                    load_head(*nxt)
            if pend[0] is not None:
                pend[0]()
                pend[0] = None
            if pend2[0] is not None:
                pend2[0](len(kts) - 1)
                pend2[0] = None
            if pend3[0] is not None:
                pend3[0]()
                pend3[0] = None
            gi = gcnt[0] % 2
            gcnt[0] += 1
            go = gob[gi]; gk = "gob%d" % gi
            if mixer == 0:
                S.op("dve", lambda e: e.tensor_copy(out=fr1, in_=B(6)), w=["fr1c"], x=[BK(6)])
                S.op("dve", lambda e: e.tensor_copy(out=fo1, in_=B(4)), w=["fo1c"], x=[BK(4)])
                S.op("dve", lambda e: e.tensor_copy(out=fr2, in_=B(7)), w=["fr2c"], x=[BK(7)])
                S.op("dve", lambda e: e.tensor_copy(out=ft2, in_=B(5)), w=["ft2c"], x=[BK(5)])

                def partB1():
                    S.op("dve", lambda e: e.tensor_tensor(out=fo1, in0=fo1, in1=fr2, op=ALU.mult), r=["fo1c", "fr2c"], w=["fo1"])
                    S.op("dve", lambda e: e.tensor_tensor(out=ft2, in0=ft2, in1=fr1, op=ALU.mult), r=["ft2c", "fr1c"], w=["ft2"])
                    S.op("dve", lambda e: e.scalar_tensor_tensor(out=fo, in0=ft2, scalar=neglam[:, 1:2], in1=fo1, op0=ALU.mult, op1=ALU.add),
                         r=["ft2", "fo1", "neglam"], w=["fo"])
                    S.op("dve", lambda e: e.tensor_tensor(out=fr1, in0=fr1, in1=fr2, op=ALU.mult), r=["fr1c", "fr2c", "ft2"], w=["fcc"])
                    S.op("dve", lambda e: e.scalar_tensor_tensor(out=fr2, in0=fr1, scalar=1e-6, in1=fr1, op0=ALU.mult, op1=ALU.mult),
                         r=["fcc", "fo1"], w=["fce"])
                    S.op("pool", lambda e: e.tensor_tensor(out=fsq, in0=fo, in1=fo, op=ALU.mult), r=["fo"], w=["fsq"])

                def partB2(i_):
                    bk_ = i_ % 2
                    S.op("pe", lambda e: e.matmul(B(bk_), lhsT=onesb, rhs=fsq, start=True, stop=True), r=["fsq", "onesb"], x=[BK(bk_)])
                    S.op("dve", lambda e: e.scalar_tensor_tensor(out=fln, in0=B(bk_), scalar=1.0 / 128.0, in1=fr2, op0=ALU.mult, op1=ALU.add),
                         r=["fce"], w=["flt"], x=[BK(bk_)])

                def partB3(go=go, gk=gk, q0=q0, h=h, s=s, kz=kz):
                    S.op("act", lambda e: e.activation(out=fln, in_=fln, func=AF.Ln), r=["flt"], w=["fln"])
                    S.op("act", lambda e: e.activation(out=frs, in_=fln, func=AF.Exp, scale=-0.5), r=["fln"], w=["frs"])
                    S.op("dve", lambda e: e.scalar_tensor_tensor(out=fu, in0=fo, scalar=gsub, in1=frs, op0=ALU.mult, op1=ALU.mult),
                         r=["fo", "frs", "gsub"], w=["fu"])
                    S.op("pool", lambda e: e.tensor_tensor(out=go, in0=fu, in1=zT[s][:, q0:q0 + 512], op=ALU.mult), r=["fu", kz], w=[gk])
                    S.dma(lambda e: e.dma_start(out=Ga[h, :, q0:q0 + 512], in_=go), r=[gk])
                pend3[0] = partB3
                pend[0] = partB1
                pend2[0] = partB2
            else:
                S.op("dve", lambda e: e.tensor_copy(out=fr1, in_=B(6)), w=["fr1c"], x=[BK(6)])
                S.op("dve", lambda e: e.tensor_copy(out=fo1, in_=B(4)), w=["fo1c"], x=[BK(4)])

                def partBm(i_, go=go, gk=gk, q0=q0, h=h, s=s, kz=kz):
                    S.op("dve", lambda e: e.reciprocal(out=fr1, in_=fr1), r=["fr1c"], w=["fr1"])
                    S.op("dve", lambda e: e.tensor_tensor(out=fo1, in0=fo1, in1=fr1, op=ALU.mult), r=["fr1", "fo1c"], w=["fo1"])
                    S.op("pool", lambda e: e.tensor_tensor(out=go, in0=fo1, in1=zT[s][:, q0:q0 + 512], op=ALU.mult), r=["fo1", kz], w=[gk])
                    S.dma(lambda e: e.dma_start(out=Gb[h, :, q0:q0 + 512], in_=go), r=[gk])
                pend2[0] = partBm

    for s_ in range(NB2):
        S.op("pool", lambda e, s_=s_: e.memset(qpT[s_][64:128, :], 0.0), w=["qpz%d" % s_])
    S.op("pool", lambda e: e.memset(kpT[64:128, :], 0.0), w=["kpz"])
    pend = [None]
    pend2 = [None]
    pend3 = [None]
    if stop_after >= 2:
        items = [(h, 0) for h in range(8)] + [(h, 1) for h in range(8)]
        S.dma(lambda e: e.dma_start(out=kpT[0:64, :], in_=Kp[:, :]), w=["kpT"])
        load_head(*items[0])
        prefetch_p3()
        for i, it in enumerate(items):
            attention(it[0], it[1], items[i + 1] if i + 1 < len(items) else None)
        if pend[0] is not None:
            pend[0]()
            pend[0] = None
        if pend2[0] is not None:
            pend2[0](1)
            pend2[0] = None
        if pend3[0] is not None:
            pend3[0]()
            pend3[0] = None

    S.barrier()
    cur[0] = base0
    if stop_after < 2:
        prefetch_p3()
    WF = lambda i: [("wfin", i, 0), ("wfin", i, 1)]
    gaS = alloc([8, 512], BF16); gbS = alloc([8, 512], BF16); sgS = alloc([16, 512], BF16)
    mT = alloc([8, 512], BF16)
    ft = alloc([512], F32); fu2 = alloc([512], F32)
    xr2 = [alloc([1024], F32) for _ in range(2)]; xnr2 = xr2
    yy2 = [alloc([1024], F32) for _ in range(2)]; ybf2 = [alloc([1024], BF16) for _ in range(2)]
    yT2 = [alloc([8, 128], BF16) for _ in range(2)]
    sgi2 = [alloc([1024], F32) for _ in range(2)]; y22 = [alloc([1024], F32) for _ in range(2)]
    st32 = [alloc([12], F32) for _ in range(2)]; mv32 = [alloc([4], F32) for _ in range(2)]
    oo = [alloc([1024], F32) for _ in range(2)]
    MT = [("mT", dc) for dc in range(8)]
    assert cur[0] <= base2, ("phase-3 work buffers overlap prefetched weights", cur[0], base2)

    def blk_load(blk):
        c0 = blk * 512
        S.dma(lambda e, c0=c0: e.dma_start(out=gaS, in_=Ga.ap().rearrange("h p t -> p h t")[:, :, c0:c0 + 512]), w=["gaS"])
        S.dma(lambda e, c0=c0: e.dma_start(out=gbS, in_=Gb.ap().rearrange("h p t -> p h t")[:, :, c0:c0 + 512]), w=["gbS"])
        for hh2 in range(2):
            S.dma(lambda e, c0=c0, hh2=hh2: e.dma_start(out=sgS[:, 8 * hh2:8 * hh2 + 8, :], in_=Sg.ap().rearrange("h p t -> p h t")[:, 8 * hh2:8 * hh2 + 8, c0:c0 + 512]), w=["sgS%d" % hh2])

    def blk_prep(blk):
        for dc in range(8):
            bA = 7; bB = 6
            for hh in range(8):
                S.op("pe", lambda e, hh=hh, dc=dc, bA=bA: e.matmul(B(bA), lhsT=wfin[0][:, hh, dc * 128:(dc + 1) * 128], rhs=gaS[:, hh, :], start=(hh == 0), stop=(hh == 7)),
                     r=WF(0) + ["gaS"], x=[BK(bA)])
            for hh in range(8):
                S.op("pe", lambda e, hh=hh, dc=dc, bB=bB: e.matmul(B(bB), lhsT=wfin[1][:, hh, dc * 128:(dc + 1) * 128], rhs=gbS[:, hh, :], start=(hh == 0), stop=(hh == 7)),
                     r=WF(1) + ["gbS"], x=[BK(bB)])
            S.op("dve", lambda e, dc=dc, bA=bA: e.tensor_tensor(out=ft, in0=B(bA), in1=sgS[:, dc, :], op=ALU.mult), r=["sgS0"], w=["ft"], x=[BK(bA)])
            S.op("dve", lambda e, dc=dc, bB=bB: e.tensor_tensor(out=fu2, in0=B(bB), in1=sgS[:, 8 + dc, :], op=ALU.mult), r=["sgS1"], w=["fu2"], x=[BK(bB)])
            S.op("pool", lambda e, dc=dc: e.tensor_tensor(out=mT[:, dc, :], in0=ft, in1=fu2, op=ALU.add), r=["ft", "fu2"], w=[("mT", dc)])

    def bufs(t):
        u = t % 2
        return dict(u=u, xr=xr2[u], xnr=xnr2[u], yy=yy2[u], ybf=ybf2[u], yT=yT2[u], sgi=sgi2[u], y2=y22[u], st3=st32[u], mv3=mv32[u], o_=oo[u])

    def st_X(t):
        d_ = bufs(t); u = d_["u"]
        xr = d_["xr"]
        K = lambda n: "%s_%d" % (n, u)
        S.dma(lambda e, t=t, xr=xr: e.dma_start(out=xr, in_=x_in[t * 128:(t + 1) * 128, :]), w=[K("xr"), K("xnr")])
        S.op("dve", lambda e, t=t, xr=xr: e.scalar_tensor_tensor(out=xr, in0=xr, scalar=stats[:, t, 0:1], in1=rows[:, 0, :], op0=ALU.subtract, op1=ALU.mult),
             r=[K("xr"), ("rows", 0)], w=[K("xnr0")])
        S.op("dve", lambda e, t=t, xr=xr: e.scalar_tensor_tensor(out=xr, in0=xr, scalar=stats[:, t, 3:4], in1=rows[:, 1, :], op0=ALU.mult, op1=ALU.add),
             r=[K("xnr0"), ("rows", 1)], w=[K("xnr")])

    def st_A(t):
        d_ = bufs(t); u = d_["u"]
        xr = d_["xr"]; xnr = d_["xnr"]; yy = d_["yy"]; ybf = d_["ybf"]; yT = d_["yT"]
        K = lambda n: "%s_%d" % (n, u)
        tc0 = (t % 4) * 128
        bt = 6
        for half in range(2):
            for dc in range(8):
                S.op("pe", lambda e, dc=dc, half=half, tc0=tc0: e.matmul(B(4 + half), lhsT=mT[:, dc, tc0:tc0 + 128], rhs=wfin[2][:, dc, half * 512:(half + 1) * 512],
                                                                       start=(dc == 0), stop=(dc == 7)),
                     r=WF(2) + MT, x=[BK(4 + half)])
        for half in range(2):
            S.op("dve", lambda e, half=half, yy=yy, xnr=xnr: e.scalar_tensor_tensor(out=yy[:, half * 512:(half + 1) * 512], in0=xnr[:, half * 512:(half + 1) * 512], scalar=ALPHA,
                                                                  in1=B(4 + half), op0=ALU.mult, op1=ALU.add),
                 r=[K("xnr")], w=[K("yy%d" % half)], x=[BK(4 + half)])
        S.op("act", lambda e, ybf=ybf, yy=yy: e.activation(out=ybf, in_=yy, func=AF.Copy), r=[K("yy0"), K("yy1")], w=[K("ybf")])

    def st_A2(t):
        d_ = bufs(t); u = d_["u"]
        ybf = d_["ybf"]; yT = d_["yT"]
        K = lambda n: "%s_%d" % (n, u)
        bt = 6
        for dc in range(8):
            S.op("pe", lambda e, dc=dc, ybf=ybf, bt=bt: e.transpose(out=pst16[:, bt, dc * 128:(dc + 1) * 128], in_=ybf[:, dc * 128:(dc + 1) * 128], identity=identb),
                 r=[K("ybf"), "identb"], x=[BK(bt)])
        S.op("dve", lambda e, yT=yT, bt=bt: e.tensor_copy(out=yT, in_=pst16[:, bt, :].rearrange("p (a b) -> p a b", a=8)), w=[K("yT")], x=[BK(bt)])

    def st_B(t):
        d_ = bufs(t); u = d_["u"]
        yy = d_["yy"]; yT = d_["yT"]; sgi = d_["sgi"]; y2 = d_["y2"]
        K = lambda n: "%s_%d" % (n, u)
        for half in range(2):
            for dc in range(8):
                S.op("pe", lambda e, dc=dc, half=half, yT=yT: e.matmul(B(half), lhsT=yT[:, dc, :], rhs=wfin[3][:, dc, half * 512:(half + 1) * 512], start=(dc == 0), stop=(dc == 7)),
                     r=WF(3) + [K("yT")], x=[BK(half)])
            for j in range(2):
                S.op("pe", lambda e, j=j, half=half, t=t: e.matmul(B(2 + half), lhsT=pTs[:, j, t * 128:(t + 1) * 128], rhs=wpp[:, j, half * 512:(half + 1) * 512], start=(j == 0), stop=(j == 1)),
                     r=["pTs", "wpp"], x=[BK(2 + half)])
        for half in range(2):
            hs = slice(half * 512, (half + 1) * 512)
            S.op("dve", lambda e, half=half, hs=hs, sgi=sgi: e.tensor_tensor(out=sgi[:, hs], in0=B(half), in1=rows[:, 2, hs], op=ALU.add), r=[("rows", 2)], w=[K("sgi%d" % half)], x=[BK(half)])
            S.op("act", lambda e, hs=hs, sgi=sgi: e.activation(out=sgi[:, hs], in_=sgi[:, hs], func=AF.Sigmoid), r=[K("sgi%d" % half)], w=[K("sgo%d" % half)])
            S.op("dve", lambda e, half=half, hs=hs, sgi=sgi, y2=y2: e.tensor_tensor(out=y2[:, hs], in0=B(2 + half), in1=sgi[:, hs], op=ALU.mult), r=[K("sgo%d" % half)], w=[K("y2a%d" % half)], x=[BK(2 + half)])
            S.op("pool", lambda e, hs=hs, y2=y2, yy=yy: e.tensor_tensor(out=y2[:, hs], in0=y2[:, hs], in1=yy[:, hs], op=ALU.add), r=[K("y2a%d" % half), K("yy%d" % half)], w=[K("y2%d" % half)])

    def st_C(t):
        d_ = bufs(t); u = d_["u"]
        y2 = d_["y2"]; st3 = d_["st3"]; mv3 = d_["mv3"]; o_ = d_["o_"]
        K = lambda n: "%s_%d" % (n, u)
        ok_ = "oo%d" % u
        S.op("dve", lambda e, st3=st3, y2=y2: e.bn_stats(out=st3[:, 0:6], in_=y2[:, 0:512]), r=[K("y20")], w=[K("st3a")])
        S.op("dve", lambda e, st3=st3, y2=y2: e.bn_stats(out=st3[:, 6:12], in_=y2[:, 512:1024]), r=[K("y21")], w=[K("st3b")])
        S.op("dve", lambda e, st3=st3, mv3=mv3: e.bn_aggr(out=mv3[:, 0:2], in_=st3), r=[K("st3a"), K("st3b")], w=[K("mv3")])
        S.op("act", lambda e, mv3=mv3: e.activation(out=mv3[:, 2:3], in_=mv3[:, 1:2], func=AF.Ln, bias=sm[:, 40:41], scale=1.0), r=[K("mv3"), "sm"], w=[K("lv3")])
        S.op("act", lambda e, mv3=mv3: e.activation(out=mv3[:, 3:4], in_=mv3[:, 2:3], func=AF.Exp, scale=-0.5), r=[K("lv3")], w=[K("rs3")])
        S.op("dve", lambda e, o_=o_, y2=y2, mv3=mv3: e.scalar_tensor_tensor(out=o_, in0=y2, scalar=mv3[:, 0:1], in1=rows[:, 3, :], op0=ALU.subtract, op1=ALU.mult),
             r=[K("y20"), K("y21"), K("mv3"), ("rows", 3)], w=[ok_ + "a"])
        S.op("dve", lambda e, o_=o_, mv3=mv3: e.scalar_tensor_tensor(out=o_, in0=o_, scalar=mv3[:, 3:4], in1=rows[:, 4, :], op0=ALU.mult, op1=ALU.add),
             r=[ok_ + "a", K("rs3"), ("rows", 4)], w=[ok_])
        S.dma(lambda e, o_=o_, t=t: e.dma_start(out=out_t[t * 128:(t + 1) * 128, :], in_=o_), r=[ok_])

    blk_load(0)
    st_X(0)
    for it in range(16 + 2):
        if 0 <= it - 2 < 16:
            st_C(it - 2)
        if it < 16:
            if it % 4 == 0:
                blk_prep(it // 4)
                if it // 4 + 1 < 4:
                    blk_load(it // 4 + 1)
            st_A(it)
            if it + 1 < 16:
                st_X(it + 1)
        if 0 <= it - 1 < 16:
            st_B(it - 1)
        if it < 16:
            st_A2(it)
    S.emit()
    return nc


def _rot_perm():
    n = np.arange(128)
    src = 64 * (n // 64) + np.where(n % 64 < 32, n % 64 + 32, n % 64 - 32)
    pm = np.zeros((128, 128), np.float32)
    pm[src, n] = 1.0
    return pm


def _prep_shared(inp):
    f = lambda k: np.asarray(inp[k], dtype=np.float32)
    W = f("w_in")[0]
    def chunk(cols):
        return W[:, cols].reshape(8, 128, len(cols)).transpose(1, 0, 2)
    def rot128(base):
        n = np.arange(128); m = n // 64; i = n % 64
        return base + 64 * m + np.where(i < 32, i + 32, i - 32)
    wf = []
    for h in range(8): wf.append(chunk(np.arange(128 * h, 128 * h + 128)))
    for h in range(8): wf.append(chunk(rot128(128 * h)))
    for h in range(8): wf.append(chunk(np.arange(1024 + 128 * h, 1024 + 128 * h + 128)))
    for h in range(8): wf.append(chunk(rot128(1024 + 128 * h)))
    for h in range(8): wf.append(chunk(np.arange(3072 + 128 * h, 3072 + 128 * h + 128)))
    for h in range(8): wf.append(chunk(np.arange(4800 + 128 * h, 4800 + 128 * h + 128)))
    for j in range(16): wf.append(chunk(np.arange(5824 + 128 * j, 5824 + 128 * j + 128)))
    for j in range(3): wf.append(chunk(np.arange(4096 + 128 * j, 4096 + 128 * j + 128)))
    for j in range(2): wf.append(chunk(np.arange(4480 + 128 * j, 4480 + 128 * j + 128)))
    i64 = np.arange(64)
    r64 = np.where(i64 < 32, i64 + 32, i64 - 32)
    wf.append(chunk(np.concatenate([4736 + i64, 4736 + r64])))
    wf = np.ascontiguousarray(np.stack(wf), dtype=np.float32)
    wv = np.ascontiguousarray(np.stack([W[:, 2048 + g * 512: 2048 + (g + 1) * 512].reshape(8, 128, 512).transpose(1, 0, 2) for g in range(2)]), dtype=np.float32)
    uq = f("mla_w_uq")[0]; ukv = f("mla_w_ukv")[0]
    wuqn = np.ascontiguousarray(np.stack([uq[:, h * 192:h * 192 + 128].reshape(3, 128, 128).transpose(1, 0, 2) for h in range(8)]), dtype=np.float32)
    wuqp = np.ascontiguousarray(np.stack([uq[:, np.concatenate([h * 192 + 128 + i64, h * 192 + 128 + r64])].reshape(3, 128, 128).transpose(1, 0, 2) for h in range(8)]), dtype=np.float32)
    wukk = np.ascontiguousarray(np.stack([ukv[:, h * 256:h * 256 + 128].reshape(2, 128, 128).transpose(1, 0, 2) for h in range(8)]), dtype=np.float32)
    wukv = np.ascontiguousarray(np.stack([ukv[:, np.concatenate([h * 256 + 128 + np.arange(128) for h in range(4 * g, 4 * g + 4)])].reshape(2, 128, 512).transpose(1, 0, 2) for g in range(2)]), dtype=np.float32)
    wfin = np.ascontiguousarray(np.stack([f(k)[0].reshape(8, 128, 1024).transpose(1, 0, 2) for k in ("w_o_a", "w_o_b", "w_out", "ple_w_gate")]), dtype=np.float32)
    wpp = np.ascontiguousarray(f("ple_w_proj")[0].reshape(2, 128, 1024).transpose(1, 0, 2), dtype=np.float32)
    sm = np.zeros((128, 64), np.float32)
    sm[:, 0:8] = f("ln_emb_g").reshape(8, 128).T
    sm[:, 8:16] = f("ln_emb_b").reshape(8, 128).T
    sm[:, 16:32] = f("b_gate")[0].reshape(16, 128).T
    sm[:, 32:35] = f("mla_q_norm_g")[0].reshape(3, 128).T
    sm[:, 35:37] = f("mla_kv_norm_g")[0].reshape(2, 128).T
    sm[:, 37] = f("diff_subln_g")[0]
    inv = (np.float32(10000.0) ** (-(np.arange(0, 64, 2, dtype=np.float32)) / np.float32(64))).astype(np.float32)
    pp = np.arange(128)
    sm[:, 38] = inv[pp % 32]
    sm[:, 39] = np.where((pp % 64) < 32, -1.0, 1.0)
    sm[:, 40] = 1e-5
    sm[:, 41] = 1e-6
    rows = np.ascontiguousarray(np.stack([f("ln_emb_g"), f("ln_emb_b"), f("ple_b_gate")[0], f("ln_post_g")[0], f("ln_post_b")[0]]), dtype=np.float32)
    dl = np.ascontiguousarray(f("diff_lambda")[0].reshape(1, 256))
    return dict(wf=wf, wv=wv, wuqn=wuqn, wuqp=wuqp, wukk=wukk, wukv=wukv, wfin=wfin, wpp=wpp, sm=sm, rows=rows, dl=dl,
                ident=np.eye(128, dtype=np.float32), perm=_rot_perm())


def _core_maps(inp, shared):
    x = np.asarray(inp["x"], dtype=np.float32)
    p = np.asarray(inp["p"], dtype=np.float32)[0]
    pos = np.asarray(inp["positions"]).astype(np.int32)
    maps = []
    orders = []
    kk = np.arange(128)
    tri = np.where(kk[:, None] <= kk[None, :], 1.0, 0.0).astype(np.float32)
    for c in range(8):
        b, hf = c // 2, c % 2
        order = [2 * s + hf for s in range(16)] + [2 * u + 1 - hf for u in range(16)]
        idx = np.concatenate([np.arange(t * 128, (t + 1) * 128) for t in order])
        m = dict(shared)
        m["x"] = np.ascontiguousarray(x[b][idx])
        m["pos"] = np.ascontiguousarray(pos[b][idx][None, :])
        m["pT"] = np.ascontiguousarray(p[b][idx[:NO]].T.reshape(2, 128, NO).transpose(1, 0, 2))
        m["masks"] = np.ascontiguousarray(np.concatenate([tri, np.full((128, 128), 1.0 if hf == 1 else 0.0, np.float32)], axis=1))
        maps.append(m)
        orders.append(order)
    return maps, orders


_NC_CACHE = {}


def kernel(**inp):
    if "nc" not in _NC_CACHE:
        _NC_CACHE["nc"] = build_program()
    nc = _NC_CACHE["nc"]
    shared = _prep_shared(inp)
    maps, orders = _core_maps(inp, shared)
    res = run_bass_kernel_spmd(nc, maps, core_ids=list(range(8)))
    out = np.zeros((4, 4096, 1024), np.float32)
    for c in range(8):
        b = c // 2
        o = np.asarray(res.results[c]["out"], dtype=np.float32)
        for s in range(16):
            t = orders[c][s]
            out[b, t * 128:(t + 1) * 128] = o[s * 128:(s + 1) * 128]
    return out
```
